# Optimizing a Trainium2 kernel written in Bass

```python
import math
import jax, jax.numpy as jnp
from jax import lax
import numpy as np

D_MODEL = 1024
BATCH = 4
SEQ = 4096
DEPTH = 4
DEC_BATCH = 8
DEC_SEQ = 16
PAST_LEN = 4096

CHUNK = 64
Q_BLOCK = 128
N_EVEN = (DEPTH + 1) // 2
N_ODD = DEPTH // 2
D_A = D_MODEL
D_B = D_MODEL
CONV_W = 31
HB_DK = 128
HB_HEADS = D_B // HB_DK
HB_DV = D_B // HB_HEADS
D_C = 2 * D_MODEL
HC_HEADS = 16
HC_DIM = D_C // HC_HEADS // 2
IN_EVEN = 3 * D_A + 4 * D_B
IN_ODD = 4 * D_C
ALPHA = (2.0 * DEPTH) ** 0.25
BETA = (8.0 * DEPTH) ** -0.25
EPS = 1e-5

kernel_name = "hybrid_stream_conv_hgrn2_diffattn_step"


def layer_norm(x, g, b):
    xf = x.astype(jnp.float32)
    mu = jnp.mean(xf, axis=-1, keepdims=True)
    var = jnp.mean(jnp.square(xf - mu), axis=-1, keepdims=True)
    return ((xf - mu) * lax.rsqrt(var + EPS) * g.astype(jnp.float32) + b.astype(jnp.float32)).astype(x.dtype)


def rms_norm(x, g):
    xf = x.astype(jnp.float32)
    y = xf * lax.rsqrt(jnp.mean(jnp.square(xf), axis=-1, keepdims=True) + EPS)
    return (y * g.astype(jnp.float32)).astype(x.dtype)


def causal_depthwise_conv(u, buf, w, b):
    full = jnp.concatenate([buf.astype(u.dtype), u], axis=1)
    y = lax.conv_general_dilated(full, w[:, None, :].astype(u.dtype), window_strides=(1,), padding='VALID',
                                 dimension_numbers=('NWC', 'WIO', 'NWC'), feature_group_count=u.shape[-1])
    return y + b.astype(u.dtype), full[:, -(CONV_W - 1):]


def hgrn2_recurrence(q, k, v, logf, S0):
    Bsz, T, H, K = q.shape
    V = v.shape[-1]
    chunk = CHUNK if T % CHUNK == 0 else T
    n = T // chunk

    def blocks(a):
        return jnp.moveaxis(a.reshape(Bsz, n, chunk, *a.shape[2:]), 1, 0)

    tri = jnp.tril(jnp.ones((chunk, chunk), dtype=bool))[None, :, :, None, None]

    def step(S, inp):
        qc, kc, vc, gc = inp
        b = jnp.cumsum(gc, axis=1)
        o_inter = jnp.einsum('blhk,bhkv->blhv', qc * jnp.exp(b), S)
        diff = b[:, :, None] - b[:, None, :]
        decay = jnp.exp(jnp.where(tri, diff, -jnp.inf))
        att = jnp.einsum('bthk,btshk,bshk->btsh', qc, decay, kc)
        o = o_inter + jnp.einsum('btsh,bshv->bthv', att, vc)
        b_last = b[:, -1]
        S_new = jnp.exp(b_last)[..., None] * S + jnp.einsum(
            'bshk,bshv->bhkv', kc * jnp.exp(b_last[:, None] - b), vc)
        return S_new, o

    S, o = lax.scan(step, S0.astype(jnp.float32), (blocks(q), blocks(k), blocks(v), blocks(logf)))
    o = jnp.moveaxis(o, 0, 1).reshape(Bsz, T, H, V)
    return o, S


def even_mixer(x, conv_buf, S0, w_in, w_out, conv_w, conv_b, cln_g, cln_b, lb, hn_g):
    Bsz, T, _ = x.shape
    h = x @ w_in
    cuts = [D_A, 2 * D_A, 3 * D_A, 3 * D_A + D_B, 3 * D_A + 2 * D_B, 3 * D_A + 3 * D_B]
    a, ga, za, q, fr, i, zb = jnp.split(h, cuts, axis=-1)
    u = a * jax.nn.sigmoid(ga)
    c, new_buf = causal_depthwise_conv(u, conv_buf, conv_w, conv_b)
    c = jax.nn.silu(layer_norm(c, cln_g, cln_b))
    out_a = c * jax.nn.silu(za)
    hs = (Bsz, T, HB_HEADS, HB_DK)
    frf = fr.astype(jnp.float32).reshape(hs)
    lbh = lb.astype(jnp.float32).reshape(HB_HEADS, HB_DK)
    logf = jnp.logaddexp(jnp.log(lbh), jnp.log1p(-lbh) + jax.nn.log_sigmoid(frf))
    kk = (1.0 - lbh) * jax.nn.sigmoid(-frf)
    qq = jax.nn.silu(q.astype(jnp.float32)).reshape(hs)
    vv = i.astype(jnp.float32).reshape(Bsz, T, HB_HEADS, HB_DV)
    o, S = hgrn2_recurrence(qq, kk, vv, logf, S0)
    o = rms_norm(o, hn_g).astype(x.dtype).reshape(Bsz, T, D_B)
    out_b = o * jax.nn.silu(zb)
    y = jnp.concatenate([out_a, out_b], axis=-1) @ w_out
    return y, new_buf, S


def diff_attend(q, k, v, q_pos, k_pos, lam):
    s = jnp.einsum('bqhcd,bkhcd->bhcqk', q, k).astype(jnp.float32) * (HC_DIM ** -0.5)
    mask = (k_pos[None, :] // CHUNK) <= (q_pos[:, None] // CHUNK)
    p = jax.nn.softmax(jnp.where(mask, s, -jnp.inf), axis=-1)
    p = p[:, :, 0] - lam * p[:, :, 1]
    return jnp.einsum('bhqk,bkhe->bqhe', p.astype(v.dtype), v)


def odd_mixer(x, k_past, v_past, w_in, w_out, lq1, lk1, lq2, lk2, sub_g, lam_init):
    Bsz, T, _ = x.shape
    q, k, v, z = jnp.split(x @ w_in, 4, axis=-1)
    q = q.reshape(Bsz, T, HC_HEADS, 2, HC_DIM)
    k = k.reshape(Bsz, T, HC_HEADS, 2, HC_DIM)
    v = v.reshape(Bsz, T, HC_HEADS, 2 * HC_DIM)
    f32 = jnp.float32
    lam = (jnp.exp(jnp.sum(lq1.astype(f32) * lk1.astype(f32))) - jnp.exp(jnp.sum(lq2.astype(f32) * lk2.astype(f32)))
           + lam_init)
    if k_past is None:
        nb = T // Q_BLOCK
        qb = jnp.moveaxis(q.reshape(Bsz, nb, Q_BLOCK, HC_HEADS, 2, HC_DIM), 1, 0)
        k_pos = jnp.arange(T)

        def one(args):
            qi, start = args
            return diff_attend(qi, k, v, start + jnp.arange(Q_BLOCK), k_pos, lam)

        o = lax.map(one, (qb, jnp.arange(nb) * Q_BLOCK))
        o = jnp.moveaxis(o, 0, 1).reshape(Bsz, T, HC_HEADS, 2 * HC_DIM)
    else:
        P = k_past.shape[1]
        k_all = jnp.concatenate([k_past.astype(k.dtype).reshape(Bsz, P, HC_HEADS, 2, HC_DIM), k], axis=1)
        v_all = jnp.concatenate([v_past.astype(v.dtype), v], axis=1)
        o = diff_attend(q, k_all, v_all, P + jnp.arange(T), jnp.arange(P + T), lam)
    o = rms_norm(o, sub_g) * (1.0 - lam_init)
    o = o.reshape(Bsz, T, D_C) * jax.nn.silu(z)
    return o @ w_out, k.reshape(Bsz, T, HC_HEADS, 2 * HC_DIM), v


def setup_inputs(seed: int = 0) -> dict:
    key = jax.random.key(seed)
    ks = jax.random.split(key, 24)
    nrm = jax.random.normal
    f32 = jnp.float32
    return {
        "x_prompt": nrm(ks[0], (BATCH, SEQ, D_MODEL), f32),
        "x_sample": nrm(ks[1], (DEC_BATCH, DEC_SEQ, D_MODEL), f32),
        "state_conv": 0.5 * nrm(ks[2], (N_EVEN, DEC_BATCH, CONV_W - 1, D_A), f32),
        "state_hgrn": 0.5 * nrm(ks[3], (N_EVEN, DEC_BATCH, HB_HEADS, HB_DK, HB_DV), f32),
        "cache_k": nrm(ks[4], (N_ODD, DEC_BATCH, PAST_LEN, HC_HEADS, 2 * HC_DIM), f32),
        "cache_v": nrm(ks[5], (N_ODD, DEC_BATCH, PAST_LEN, HC_HEADS, 2 * HC_DIM), f32),
        "w_in_even": nrm(ks[6], (N_EVEN, D_MODEL, IN_EVEN), f32) * D_MODEL ** -0.5,
        "w_out_even": nrm(ks[7], (N_EVEN, D_A + D_B, D_MODEL), f32) * (D_A + D_B) ** -0.5 * BETA,
        "conv_w": nrm(ks[8], (N_EVEN, CONV_W, D_A), f32) * CONV_W ** -0.5,
        "conv_b": 0.01 * nrm(ks[9], (N_EVEN, D_A), f32),
        "conv_ln_g": 1.0 + 0.02 * nrm(ks[10], (N_EVEN, D_A), f32),
        "conv_ln_b": 0.02 * nrm(ks[11], (N_EVEN, D_A), f32),
        "hgrn_lb_logits": nrm(ks[12], (N_EVEN, D_B), f32),
        "hgrn_norm_g": 1.0 + 0.02 * nrm(ks[13], (N_EVEN, HB_DV), f32),
        "w_in_odd": nrm(ks[14], (N_ODD, D_MODEL, IN_ODD), f32) * D_MODEL ** -0.5,
        "w_out_odd": nrm(ks[15], (N_ODD, D_C, D_MODEL), f32) * D_C ** -0.5 * BETA,
        "lam_q1": 0.1 * nrm(ks[16], (N_ODD, HC_DIM), f32),
        "lam_k1": 0.1 * nrm(ks[17], (N_ODD, HC_DIM), f32),
        "lam_q2": 0.1 * nrm(ks[18], (N_ODD, HC_DIM), f32),
        "lam_k2": 0.1 * nrm(ks[19], (N_ODD, HC_DIM), f32),
        "subln_g": 1.0 + 0.02 * nrm(ks[20], (N_ODD, 2 * HC_DIM), f32),
        "post_ln_g": 1.0 + 0.02 * nrm(ks[21], (DEPTH, D_MODEL), f32),
        "post_ln_b": 0.02 * nrm(ks[22], (DEPTH, D_MODEL), f32),
    }


def reference(x_prompt, x_sample, state_conv, state_hgrn, cache_k, cache_v, w_in_even, w_out_even, conv_w, conv_b,
              conv_ln_g, conv_ln_b, hgrn_lb_logits, hgrn_norm_g, w_in_odd, w_out_odd, lam_q1, lam_k1, lam_q2, lam_k2,
              subln_g, post_ln_g, post_ln_b):
    lb_cum = jnp.cumsum(jax.nn.softmax(hgrn_lb_logits.astype(jnp.float32), axis=0), axis=0)
    lb_all = lb_cum - lb_cum[0:1]
    xp, xs = x_prompt, x_sample
    bp = xp.shape[0]
    conv_p, hgrn_p, k_p, v_p = [], [], [], []
    conv_s, hgrn_s, k_s, v_s = [], [], [], []
    for l in range(DEPTH):
        if l % 2 == 0:
            e = l // 2
            prm = (w_in_even[e], w_out_even[e], conv_w[e], conv_b[e], conv_ln_g[e], conv_ln_b[e], lb_all[e],
                   hgrn_norm_g[e])
            buf0 = jnp.zeros((bp, CONV_W - 1, D_A), xp.dtype)
            S0 = jnp.zeros((bp, HB_HEADS, HB_DK, HB_DV), jnp.float32)
            yp, cb_p, S_p = even_mixer(xp, buf0, S0, *prm)
            ys, cb_s, S_s = even_mixer(xs, state_conv[e], state_hgrn[e], *prm)
            conv_p.append(cb_p); hgrn_p.append(S_p)
            conv_s.append(cb_s); hgrn_s.append(S_s)
        else:
            o = l // 2
            lam_init = 0.8 - 0.6 * math.exp(-0.3 * l)
            prm = (w_in_odd[o], w_out_odd[o], lam_q1[o], lam_k1[o], lam_q2[o], lam_k2[o], subln_g[o], lam_init)
            yp, kn_p, vn_p = odd_mixer(xp, None, None, *prm)
            ys, kn_s, vn_s = odd_mixer(xs, cache_k[o], cache_v[o], *prm)
            k_p.append(kn_p); v_p.append(vn_p)
            k_s.append(kn_s); v_s.append(vn_s)
        xp = layer_norm(ALPHA * xp + yp, post_ln_g[l], post_ln_b[l])
        xs = layer_norm(ALPHA * xs + ys, post_ln_g[l], post_ln_b[l])
    return (xp, xs, jnp.stack(conv_p), jnp.stack(hgrn_p), jnp.stack(k_p), jnp.stack(v_p),
            jnp.stack(conv_s), jnp.stack(hgrn_s), jnp.stack(k_s), jnp.stack(v_s))
```

```python
import contextlib
import math
import numpy as np
import concourse.bass as bass
import concourse.mybir as mybir
from concourse.bass_utils import run_bass_kernel_spmd

F32 = mybir.dt.float32
BF16 = mybir.dt.bfloat16
AF = mybir.ActivationFunctionType
ALU = mybir.AluOpType
AX = mybir.AxisListType

D = 1024
DEPTH = 4
TS = 16
CONV_W = 31
HALO = CONV_W - 1
ALPHA = (2.0 * DEPTH) ** 0.25
EPS = 1e-5
NH = 16


class Buf:
    __slots__ = ("w", "r", "multi")

    def __init__(self, multi=False):
        self.multi = multi
        self.w = {}
        self.r = {}


class Eng:
    def __init__(self, eng, sem, raw_same=True):
        self.eng = eng
        self.sem = sem
        self.key = id(sem)
        self.count = 0
        self.waited = {}
        self.raw_same = raw_same


class FW:
    def __init__(self, nc, n_dma_sems=24):
        self.nc = nc
        self.es = contextlib.ExitStack()
        es = self.es
        self.pe = Eng(nc.tensor, es.enter_context(nc.semaphore("s_pe")), raw_same=False)
        self.act = Eng(nc.scalar, es.enter_context(nc.semaphore("s_act")))
        self.dve = Eng(nc.vector, es.enter_context(nc.semaphore("s_dve")))
        self.pool = Eng(nc.gpsimd, es.enter_context(nc.semaphore("s_pool")))
        self.sp = Eng(nc.sync, es.enter_context(nc.semaphore("s_sp")))
        self.engs = [self.pe, self.act, self.dve, self.pool, self.sp]
        self.dma_sems = {}
        for q in ("sp", "pool"):
            lst = [[es.enter_context(nc.semaphore(f"d_{q}{i}")), 0] for i in range(n_dma_sems)]
            self.dma_sems[q] = [lst, 0]
        self.n_inst = 0

    def _wait(self, E, sem, val):
        k = id(sem)
        if E.waited.get(k, 0) >= val:
            return
        E.eng.wait_ge(sem, val)
        E.waited[k] = val

    def _deps(self, E, reads, writes):
        for b in reads:
            for k, (sem, val) in b.w.items():
                if k == E.key and not E.raw_same:
                    continue
                self._wait(E, sem, val)
        for b in writes:
            if not b.multi:
                for k, (sem, val) in b.w.items():
                    if k == E.key:
                        continue
                    self._wait(E, sem, val)
            for k, (sem, val) in b.r.items():
                if k == E.key:
                    continue
                self._wait(E, sem, val)

    def _record(self, sem, val, reads, writes):
        k = id(sem)
        for b in reads:
            b.r[k] = (sem, val)
        for b in writes:
            if b.multi:
                b.w[k] = (sem, val)
            else:
                b.w = {k: (sem, val)}
                b.r = {}

    def _chk(self):
        import os
        lim = int(os.environ.get("KOPS", "0"))
        if lim and self.n_inst >= lim and not getattr(self, "stopped", False):
            self._barrier()
            self.stopped = True
            print("build stopped at inst", self.n_inst)

    def op(self, E, fn, reads=(), writes=()):
        self._chk()
        if getattr(self, "stopped", False):
            return None
        self._deps(E, reads, writes)
        inst = fn()
        E.count += 1
        inst.then_inc(E.sem, 1)
        self._record(E.sem, E.count, reads, writes)
        self.n_inst += 1
        return inst

    def dma(self, q, out, in_, reads=(), writes=(), **kw):
        self._chk()
        if getattr(self, "stopped", False):
            return None
        E = self.sp if q == "sp" else self.pool
        pool = self.dma_sems[q]
        lst, idx = pool
        ent = lst[idx % len(lst)]
        pool[1] = idx + 1
        sem, cnt = ent
        if cnt > 0:
            self._wait(E, sem, 16 * cnt)
        self._deps(E, reads, writes)
        inst = E.eng.dma_start(out=out, in_=in_, **kw)
        ent[1] = cnt + 1
        inst.then_inc(sem, 16)
        self._record(sem, 16 * (cnt + 1), reads, writes)
        self.n_inst += 1
        return inst

    def barrier(self):
        import os
        self.n_bar = getattr(self, "n_bar", 0) + 1
        if getattr(self, "stopped", False):
            return
        if int(os.environ.get("KSTOP", "0")) == self.n_bar:
            self._barrier()
            self.stopped = True
            print("build stopped at barrier", self.n_bar)
            return
        self._barrier()

    def _barrier(self):
        for E in self.engs:
            for q, (lst, idx) in self.dma_sems.items():
                for sem, cnt in lst:
                    if cnt > 0:
                        self._wait(E, sem, 16 * cnt)
            for X in self.engs:
                if X is E or X.count == 0 or X is self.sp:
                    continue
                self._wait(E, X.sem, X.count)

    def finish(self):
        if not getattr(self, "stopped", False):
            self._barrier()

    def close(self):
        self.es.close()


class StopBuild(Exception):
    pass


class Ring:
    def __init__(self, tiles):
        self.tiles = [(t, Buf()) for t in tiles]
        self.i = 0

    def next(self):
        t = self.tiles[self.i % len(self.tiles)]
        self.i += 1
        return t


def build_program(T, P):
    nc = bass.Bass("TRN2", target_bir_lowering=False)
    fw = FW(nc)
    pe, act, dve, pool = fw.pe, fw.act, fw.dve, fw.pool
    TT = T + TS
    NTK = P // 128

    def din(name, shape):
        return nc.dram_tensor(name, list(shape), F32, kind="ExternalInput").ap()

    def dout(name, shape):
        return nc.dram_tensor(name, list(shape), F32, kind="ExternalOutput").ap()

    def dscr(name, shape, dt):
        return nc.dram_tensor(name, list(shape), dt).ap()

    x_prompt = din("x_prompt", [T, D])
    x_sample = din("x_sample", [TS, D])
    state_conv = din("state_conv", [2, HALO, D])
    state_hgrn = din("state_hgrn", [2, 8, 128, 128])
    cache_k = din("cache_k", [2, P, 2048])
    cache_v = din("cache_v", [2, P, 2048])
    w_in_even = din("w_in_even", [2, D, 7168])
    w_out_even = din("w_out_even", [2, 2048, D])
    conv_w = din("conv_w", [2, CONV_W, D])
    conv_b = din("conv_b", [2, D])
    conv_ln_g = din("conv_ln_g", [2, D])
    conv_ln_b = din("conv_ln_b", [2, D])
    hgrn_lb_logits = din("hgrn_lb_logits", [2, D])
    hgrn_norm_g = din("hgrn_norm_g", [2, 128])
    w_in_odd = din("w_in_odd", [2, D, 8192])
    w_out_odd = din("w_out_odd", [2, 2048, D])
    lam_q1 = din("lam_q1", [2, 64])
    lam_k1 = din("lam_k1", [2, 64])
    lam_q2 = din("lam_q2", [2, 64])
    lam_k2 = din("lam_k2", [2, 64])
    subln_g = din("subln_g", [2, 128])
    post_ln_g = din("post_ln_g", [4, D])
    post_ln_b = din("post_ln_b", [4, D])
    y_prompt = dout("y_prompt", [T, D])
    y_sample = dout("y_sample", [TS, D])
    conv_prompt = dout("conv_prompt", [2, HALO, D])
    hgrn_prompt = dout("hgrn_prompt", [2, 8, 128, 128])
    k_prompt = dout("k_prompt", [2, T, 2048])
    v_prompt = dout("v_prompt", [2, T, 2048])
    conv_sample = dout("conv_sample", [2, HALO, D])
    hgrn_sample = dout("hgrn_sample", [2, 8, 128, 128])
    k_sample = dout("k_sample", [2, TS, 2048])
    v_sample = dout("v_sample", [2, TS, 2048])
    xres = [dscr("xres0", [TT, D], F32), dscr("xres1", [TT, D], F32)]
    UTp = dscr("UTp", [8, 128, HALO + T], BF16)
    UTs = dscr("UTs", [8, 128, HALO + TS], BF16)
    QTe = dscr("QTe", [8, 128, TT], F32)
    KKT = dscr("KKT", [8, 128, TT], F32)
    ZA = dscr("ZA", [TT, D], BF16)
    ZB = dscr("ZB", [TT, D], BF16)
    Gs = dscr("Gs", [TT, D], F32)
    KKs = dscr("KKs", [TT, D], F32)
    VH = dscr("VH", [TT, D], BF16)
    AT = dscr("AT", [16, 128, TT], BF16)
    QTo = dscr("QTo", [16, 128, TT], BF16)
    KTo = dscr("KTo", [16, 128, TT], BF16)
    Vo = dscr("Vo", [TT, 2048], BF16)
    Zo = dscr("Zo", [TT, 2048], BF16)
    import os
    if os.environ.get("KDUMP", "") == "1":
        dbg_AT = nc.dram_tensor("dbg_AT", [16, 128, TT], BF16, kind="ExternalOutput").ap()
    B_out = Buf(multi=True)
    Bx = [Buf(multi=True), Buf(multi=True)]
    B_UT, B_QTe, B_KKT, B_ZA, B_ZB, B_G, B_KK, B_VH, B_AT = [Buf(multi=True) for _ in range(9)]
    B_QTo, B_KTo, B_Vo, B_Zo = [Buf(multi=True) for _ in range(4)]
    B_in = Buf()
    _dma = fw.dma

    def dma_route(q, out, in_, reads=(), writes=(), **kw):
        if any(b is B_out for b in writes):
            q = "pool"
        return _dma(q, out, in_, reads=reads, writes=writes, **kw)

    fw.dma = dma_route

    token_blocks = [(t0, 512) for t0 in range(0, T, 512)] + [(T, TS)]
    token_tiles = [(t0, 128) for t0 in range(0, T, 128)] + [(T, TS)]

    top = contextlib.ExitStack()

    uid = [0]

    def sb(es, name, shape, dt):
        uid[0] += 1
        return es.enter_context(nc.sbuf_tensor(f"{name}_{uid[0]}", list(shape), dt))

    def psum(es, name, shape, dt):
        uid[0] += 1
        return es.enter_context(nc.psum_tensor(f"{name}_{uid[0]}", list(shape), dt))

    XT = sb(top, "XT", [128, 8, TT], BF16)
    B_XT = [Buf() for _ in token_tiles]
    ident_bf = sb(top, "ident_bf", [128, 128], BF16)
    ident_f = sb(top, "ident_f", [128, 128], F32)
    Mcum = sb(top, "Mcum", [128, 128], F32)
    Mext = sb(top, "Mext", [128, 2], F32)
    Mrev = sb(top, "Mrev", [128, 128], F32)
    Mmask = sb(top, "Mmask", [128, 128], F32)
    epsb = sb(top, "epsb", [128, 1], F32)
    plg = sb(top, "plg", [128, D], F32)
    plb = sb(top, "plb", [128, D], F32)
    B_const = Buf()
    B_pl = Buf()

    def xt_bufs(t0, n):
        return [B_XT[i] for i, (a, m) in enumerate(token_tiles) if a < t0 + n and a + m > t0]

    def build_consts():
        g = nc.gpsimd
        ops = [
            lambda: g.memset(ident_f[:], 1.0),
            lambda: g.affine_select(out=ident_f[:], in_=ident_f[:], pattern=[[-1, 128]], compare_op=ALU.is_equal,
                                    fill=0.0, base=0, channel_multiplier=1),
            lambda: g.tensor_copy(ident_bf[:], ident_f[:]),
            lambda: g.memset(Mmask[:], 1.0),
            lambda: g.affine_select(out=Mmask[:], in_=Mmask[:], pattern=[[1, 128]], compare_op=ALU.is_ge,
                                    fill=0.0, base=0, channel_multiplier=-1),
            lambda: g.memset(Mext[:], 1.0),
            lambda: g.affine_select(out=Mext[:, 0:1], in_=Mext[:, 0:1], pattern=[[0, 1]], compare_op=ALU.is_ge,
                                    fill=0.0, base=31, channel_multiplier=-1),
            lambda: g.tensor_scalar(out=Mcum[:], in0=Mmask[:], scalar1=Mext[:, 0:1], scalar2=None, op0=ALU.subtract),
            lambda: g.memset(Mrev[:], 1.0),
            lambda: g.affine_select(out=Mrev[:], in_=Mrev[:], pattern=[[-1, 128]], compare_op=ALU.is_gt,
                                    fill=0.0, base=0, channel_multiplier=1),
            lambda: g.memset(epsb[:], EPS),
        ]
        for f in ops:
            fw.op(pool, f, reads=[B_const], writes=[B_const])

    build_consts()

    def phase_load_x():
        with contextlib.ExitStack() as es:
            xin = Ring([sb(es, f"xin{i}", [128, D], F32) for i in range(2)])
            xbf = Ring([sb(es, f"xbf{i}", [128, D], BF16) for i in range(2)])
            pt = Ring([psum(es, f"ptx{i}", [128, 8, 128], BF16) for i in range(2)])
            for ti, (t0, n) in enumerate(token_tiles):
                src = x_prompt[t0:t0 + n, :] if t0 < T else x_sample[:, :]
                xt_, bx = xin.next()
                fw.dma("sp", xt_[0:n, :], src, writes=[bx])
                fw.dma("sp", xres[0][t0:t0 + n, :], xt_[0:n, :], reads=[bx], writes=[Bx[0]])
                xb, bxb = xbf.next()
                fw.op(act, lambda: nc.scalar.copy(out=xb[0:n, :], in_=xt_[0:n, :]), reads=[bx], writes=[bxb])
                p, bp = pt.next()
                for c in range(8):
                    fw.op(pe, lambda: nc.tensor.transpose(p[:, c, 0:n], xb[0:n, c * 128:(c + 1) * 128],
                                                          ident_bf[0:n, 0:n]),
                          reads=[bxb, B_const], writes=[bp])
                fw.op(dve, lambda: nc.vector.tensor_copy(XT[:, :, t0:t0 + n], p[:, :, 0:n]), reads=[bp],
                      writes=[B_XT[ti]])
        fw.barrier()

    def load_w(wt, bw, W, col0, ncols):
        fw.dma("pool", wt[:, :, 0:ncols], W[:, col0:col0 + ncols].rearrange("(c p) n -> p c n", p=128),
               reads=[B_in], writes=[bw])

    def mm_fm(p_ap, bp, wt, bw, wc0, t0, n):
        for c in range(8):
            fw.op(pe, lambda: nc.tensor.matmul(p_ap, wt[:, c, wc0:wc0 + 128], XT[:, c, t0:t0 + n],
                                               start=(c == 0), stop=(c == 7)),
                  reads=[bw] + xt_bufs(t0, n), writes=[bp])

    def mm_tm(p_ap, bp, wt, bw, wc0, ncols, t0, n):
        for c in range(8):
            fw.op(pe, lambda: nc.tensor.matmul(p_ap, XT[:, c, t0:t0 + n], wt[:, c, wc0:wc0 + ncols],
                                               start=(c == 0), stop=(c == 7)),
                  reads=[bw] + xt_bufs(t0, n), writes=[bp])

    def sigmoid_from(es_tile, src_ap, bsrc, dst, bdst, sign=-1.0):
        fw.op(act, lambda: nc.scalar.activation(out=dst, in_=src_ap, func=AF.Exp, scale=sign), reads=[bsrc],
              writes=[bdst])
        fw.op(dve, lambda: nc.vector.tensor_scalar_add(dst, dst, 1.0), reads=[bdst], writes=[bdst])
        fw.op(dve, lambda: nc.vector.reciprocal(dst, dst), reads=[bdst], writes=[bdst])

    def load_post_ln(l):
        fw.dma("sp", plg[:], post_ln_g[l:l + 1, :].partition_broadcast(128), writes=[B_pl])
        fw.dma("sp", plb[:], post_ln_b[l:l + 1, :].partition_broadcast(128), writes=[B_pl])

    def layer_norm_rows(es, src, bsrc, n, dst, bdst, gb, bb_, Bg, tag):
        st = es_tiles[tag + "st"]
        stats, bst = st.next()
        for hlf in range(2):
            fw.op(dve, lambda: nc.vector.bn_stats(stats[0:n, hlf, :], src[0:n, hlf * 512:(hlf + 1) * 512]),
                  reads=[bsrc], writes=[bst])
        mv, bmv = es_tiles[tag + "mv"].next()
        fw.op(dve, lambda: nc.vector.bn_aggr(mv[0:n, 0:2], stats[0:n, :, :]), reads=[bst], writes=[bmv])
        fw.op(act, lambda: nc.scalar.activation(out=mv[0:n, 2:3], in_=mv[0:n, 1:2], func=AF.Ln, bias=epsb[0:n, :]),
              reads=[bmv, B_const], writes=[bmv])
        fw.op(act, lambda: nc.scalar.activation(out=mv[0:n, 2:3], in_=mv[0:n, 2:3], func=AF.Exp, scale=-0.5),
              reads=[bmv], writes=[bmv])
        fw.op(dve, lambda: nc.vector.scalar_tensor_tensor(out=mv[0:n, 3:4], in0=mv[0:n, 0:1], scalar=-1.0,
                                                          in1=mv[0:n, 2:3], op0=ALU.mult, op1=ALU.mult),
              reads=[bmv], writes=[bmv])
        fw.op(dve, lambda: nc.vector.tensor_scalar(out=dst[0:n, :], in0=src[0:n, :], scalar1=mv[0:n, 2:3],
                                                   scalar2=mv[0:n, 3:4], op0=ALU.mult, op1=ALU.add),
              reads=[bsrc, bmv], writes=[bdst])
        fw.op(dve, lambda: nc.vector.tensor_tensor(out=dst[0:n, :], in0=dst[0:n, :], in1=gb[0:n, :], op=ALU.mult),
              reads=[bdst, Bg], writes=[bdst])
        fw.op(dve, lambda: nc.vector.tensor_tensor(out=dst[0:n, :], in0=dst[0:n, :], in1=bb_[0:n, :], op=ALU.add),
              reads=[bdst, Bg], writes=[bdst])

    es_tiles = {}

    def phase_out(l, W_out, xsrc, bxsrc, last):
        load_post_ln(l)
        xdst = xres[(l + 1) % 2]
        bxdst = Bx[(l + 1) % 2]
        with contextlib.ExitStack() as es:
            wo = sb(es, "wo", [128, 16, D], BF16)
            bwo = Buf()
            for c in range(0, 16, 4):
                fw.dma("pool", wo[:, c:c + 4, :],
                       W_out[c * 128:(c + 4) * 128, :].rearrange("(c p) n -> p c n", p=128), reads=[B_in],
                       writes=[bwo] if c == 0 else [bwo])
            at_r = Ring([sb(es, f"at{i}", [128, 16, 128], BF16) for i in range(2)])
            x_r = Ring([sb(es, f"xo{i}", [128, D], F32) for i in range(2)])
            r_r = Ring([sb(es, f"ro{i}", [128, D], F32) for i in range(2)])
            xn_r = Ring([sb(es, f"xn{i}", [128, D], F32) for i in range(2)])
            xb_r = Ring([sb(es, f"xnb{i}", [128, D], BF16) for i in range(2)])
            es_tiles["ost"] = Ring([sb(es, f"ost{i}", [128, 2, 6], F32) for i in range(2)])
            es_tiles["omv"] = Ring([sb(es, f"omv{i}", [128, 4], F32) for i in range(2)])
            py_r = Ring([psum(es, f"py{i}", [128, D], F32) for i in range(2)])
            pt_r = Ring([psum(es, f"pto{i}", [128, 8, 128], BF16) for i in range(2)])
            for ti, (t0, n) in enumerate(token_tiles):
                at_, bat = at_r.next()
                fw.dma("sp", at_[:, :, 0:n], AT[:, :, t0:t0 + n].rearrange("c p t -> p c t"), reads=[B_AT],
                       writes=[bat])
                xt_, bx = x_r.next()
                fw.dma("sp", xt_[0:n, :], xsrc[t0:t0 + n, :], reads=[bxsrc], writes=[bx])
                py, bpy = py_r.next()
                for hlf in range(2):
                    for c in range(16):
                        fw.op(pe, lambda: nc.tensor.matmul(py[0:n, hlf * 512:(hlf + 1) * 512], at_[:, c, 0:n],
                                                           wo[:, c, hlf * 512:(hlf + 1) * 512],
                                                           start=(c == 0), stop=(c == 15)),
                              reads=[bat, bwo], writes=[bpy])
                r_, br = r_r.next()
                fw.op(dve, lambda: nc.vector.scalar_tensor_tensor(out=r_[0:n, :], in0=xt_[0:n, :], scalar=ALPHA,
                                                                  in1=py[0:n, :], op0=ALU.mult, op1=ALU.add),
                      reads=[bx, bpy], writes=[br])
                xn, bxn = xn_r.next()
                layer_norm_rows(es, r_, br, n, xn, bxn, plg, plb, B_pl, "o")
                if last:
                    dst = y_prompt[t0:t0 + n, :] if t0 < T else y_sample[:, :]
                    fw.dma("sp", dst, xn[0:n, :], reads=[bxn], writes=[B_out])
                else:
                    fw.dma("sp", xdst[t0:t0 + n, :], xn[0:n, :], reads=[bxn], writes=[bxdst])
                    xb, bxb = xb_r.next()
                    fw.op(act, lambda: nc.scalar.copy(out=xb[0:n, :], in_=xn[0:n, :]), reads=[bxn], writes=[bxb])
                    p, bp = pt_r.next()
                    for c in range(8):
                        fw.op(pe, lambda: nc.tensor.transpose(p[:, c, 0:n], xb[0:n, c * 128:(c + 1) * 128],
                                                              ident_bf[0:n, 0:n]),
                              reads=[bxb, B_const], writes=[bp])
                    fw.op(act, lambda: nc.scalar.copy(out=XT[:, :, t0:t0 + n], in_=p[:, :, 0:n]), reads=[bp],
                          writes=[B_XT[ti]])
        fw.barrier()

    def even_layer(l):
        e = l // 2
        W = w_in_even[e]
        xsrc, bxsrc = xres[l % 2], Bx[l % 2]
        with contextlib.ExitStack() as esL:
            lbT = sb(esL, "lbT", [128, 8], F32)
            omlT = sb(esL, "omlT", [128, 8], F32)
            lbB = sb(esL, "lbB", [128, D], F32)
            omlB = sb(esL, "omlB", [128, D], F32)
            B_lb = Buf()
            if e == 0:
                fw.op(pool, lambda: nc.gpsimd.memset(lbT[:], 0.0), writes=[B_lb])
                fw.op(pool, lambda: nc.gpsimd.memset(omlT[:], 1.0), writes=[B_lb])
                fw.op(pool, lambda: nc.gpsimd.memset(lbB[:], 0.0), writes=[B_lb])
                fw.op(pool, lambda: nc.gpsimd.memset(omlB[:], 1.0), writes=[B_lb])
            else:
                with contextlib.ExitStack() as es:
                    l0T = sb(es, "l0T", [128, 8], F32)
                    l0B = sb(es, "l0B", [128, D], F32)
                    fw.dma("sp", lbT[:], hgrn_lb_logits[1, :].rearrange("(h k) -> k h", k=128), writes=[B_lb],
                           allow_slow_non_contiguous=True)
                    fw.dma("sp", l0T[:], hgrn_lb_logits[0, :].rearrange("(h k) -> k h", k=128), writes=[B_lb],
                           allow_slow_non_contiguous=True)
                    fw.dma("sp", lbB[:], hgrn_lb_logits[1:2, :].partition_broadcast(128), writes=[B_lb])
                    fw.dma("sp", l0B[:], hgrn_lb_logits[0:1, :].partition_broadcast(128), writes=[B_lb])
                    for (a_, b_, o_) in ((lbT, l0T, omlT), (lbB, l0B, omlB)):
                        fw.op(dve, lambda: nc.vector.tensor_tensor(out=a_[:], in0=a_[:], in1=b_[:], op=ALU.subtract),
                              reads=[B_lb], writes=[B_lb])
                        fw.op(act, lambda: nc.scalar.activation(out=a_[:], in_=a_[:], func=AF.Exp, scale=-1.0),
                              reads=[B_lb], writes=[B_lb])
                        fw.op(dve, lambda: nc.vector.tensor_scalar_add(a_[:], a_[:], 1.0), reads=[B_lb], writes=[B_lb])
                        fw.op(dve, lambda: nc.vector.reciprocal(a_[:], a_[:]), reads=[B_lb], writes=[B_lb])
                        fw.op(dve, lambda: nc.vector.tensor_scalar(out=o_[:], in0=a_[:], scalar1=-1.0, scalar2=1.0,
                                                                   op0=ALU.mult, op1=ALU.add),
                              reads=[B_lb], writes=[B_lb])
                    fw.barrier()

            with contextlib.ExitStack() as es:
                w_r = Ring([sb(es, f"w{i}", [128, 8, 512], BF16) for i in range(3)])
                ps_r = Ring([psum(es, f"pp{i}", [128, 512], F32) for i in range(6)])
                tmp_r = Ring([sb(es, f"tmp{i}", [128, 512], F32) for i in range(3)])
                tmp2_r = Ring([sb(es, f"tmpb{i}", [128, 512], F32) for i in range(3)])
                of_r = Ring([sb(es, f"of{i}", [128, 512], F32) for i in range(3)])
                of2_r = Ring([sb(es, f"ofb{i}", [128, 512], F32) for i in range(3)])
                ob_r = Ring([sb(es, f"ob{i}", [128, 512], BF16) for i in range(3)])
                halo = sb(es, "halo", [128, 8, HALO], BF16)
                bhalo = Buf()

                fw.op(pool, lambda: nc.gpsimd.memset(halo[:], 0.0), writes=[bhalo])
                fw.dma("sp", UTp[:, :, 0:HALO].rearrange("c p t -> p c t"), halo[:], reads=[bhalo], writes=[B_UT])
                with contextlib.ExitStack() as es2:
                    stt = sb(es2, "stt", [HALO, D], F32)
                    halo2 = sb(es2, "halo2", [128, 8, HALO], BF16)
                    bst, bh2 = Buf(), Buf()
                    pth = psum(es2, "pth", [128, 8, 32], F32)
                    bpth = Buf()
                    fw.dma("sp", stt[:], state_conv[e], writes=[bst])
                    for c in range(8):
                        fw.op(pe, lambda: nc.tensor.transpose(pth[:, c, 0:HALO], stt[0:HALO, c * 128:(c + 1) * 128],
                                                              ident_f[0:HALO, 0:HALO]),
                              reads=[bst, B_const], writes=[bpth])
                    fw.op(dve, lambda: nc.vector.tensor_copy(halo2[:], pth[:, :, 0:HALO]), reads=[bpth], writes=[bh2])
                    fw.dma("sp", UTs[:, :, 0:HALO].rearrange("c p t -> p c t"), halo2[:], reads=[bh2], writes=[B_UT])
                    fw.dma("sp", conv_sample[e, 0:HALO - TS, :], stt[TS:HALO, :], reads=[bst], writes=[B_out])
                    fw.barrier()

                def ut_dst(c, t0, n):
                    if t0 < T:
                        return UTp[c, :, HALO + t0:HALO + t0 + n]
                    return UTs[c, :, HALO:HALO + n]

                for j in range(2):
                    wa, bwa = w_r.next()
                    load_w(wa, bwa, W, j * 512, 512)
                    wg, bwg = w_r.next()
                    load_w(wg, bwg, W, 1024 + j * 512, 512)
                    for cc in range(4):
                        c = j * 4 + cc
                        for (t0, n) in token_blocks:
                            pa, bpa = ps_r.next()
                            pg, bpg = ps_r.next()
                            mm_fm(pa[:, 0:n], bpa, wa, bwa, cc * 128, t0, n)
                            mm_fm(pg[:, 0:n], bpg, wg, bwg, cc * 128, t0, n)
                            tm, btm = tmp_r.next()
                            sigmoid_from(None, pg[:, 0:n], bpg, tm[:, 0:n], btm)
                            ob, bob = ob_r.next()
                            fw.op(dve, lambda: nc.vector.tensor_tensor(out=ob[:, 0:n], in0=pa[:, 0:n], in1=tm[:, 0:n],
                                                                       op=ALU.mult),
                                  reads=[bpa, btm], writes=[bob])
                            fw.dma("sp", ut_dst(c, t0, n), ob[:, 0:n], reads=[bob], writes=[B_UT])
                for j in range(2):
                    wq, bwq = w_r.next()
                    load_w(wq, bwq, W, 3072 + j * 512, 512)
                    for cc in range(4):
                        c = j * 4 + cc
                        for (t0, n) in token_blocks:
                            pq, bpq = ps_r.next()
                            mm_fm(pq[:, 0:n], bpq, wq, bwq, cc * 128, t0, n)
                            tm, btm = tmp_r.next()
                            sigmoid_from(None, pq[:, 0:n], bpq, tm[:, 0:n], btm)
                            of, bof = of_r.next()
                            fw.op(dve, lambda: nc.vector.tensor_tensor(out=of[:, 0:n], in0=pq[:, 0:n], in1=tm[:, 0:n],
                                                                       op=ALU.mult),
                                  reads=[bpq, btm], writes=[bof])
                            fw.dma("sp", QTe[c, :, t0:t0 + n], of[:, 0:n], reads=[bof], writes=[B_QTe])
                for j in range(2):
                    wf, bwf = w_r.next()
                    load_w(wf, bwf, W, 4096 + j * 512, 512)
                    for cc in range(4):
                        c = j * 4 + cc
                        for (t0, n) in token_blocks:
                            pf, bpf = ps_r.next()
                            mm_fm(pf[:, 0:n], bpf, wf, bwf, cc * 128, t0, n)
                            tm, btm = tmp_r.next()
                            sigmoid_from(None, pf[:, 0:n], bpf, tm[:, 0:n], btm, sign=1.0)
                            of, bof = of_r.next()
                            fw.op(dve, lambda: nc.vector.tensor_scalar(out=of[:, 0:n], in0=tm[:, 0:n],
                                                                       scalar1=omlT[:, c:c + 1], scalar2=None,
                                                                       op0=ALU.mult),
                                  reads=[btm, B_lb], writes=[bof])
                            fw.dma("sp", KKT[c, :, t0:t0 + n], of[:, 0:n], reads=[bof], writes=[B_KKT])
                for (col0, dstT, bdst, kind) in ((2048, ZA, B_ZA, "silu"), (6144, ZB, B_ZB, "silu"),
                                                 (5120, VH, B_VH, "copy"), (4096, None, None, "f")):
                    for j in range(2):
                        w_, bw = w_r.next()
                        load_w(w_, bw, W, col0 + j * 512, 512)
                        for (t0, n) in token_tiles:
                            p_, bp = ps_r.next()
                            mm_tm(p_[0:n, :], bp, w_, bw, 0, 512, t0, n)
                            cs = slice(j * 512, (j + 1) * 512)
                            if kind == "copy":
                                ob, bob = ob_r.next()
                                fw.op(act, lambda: nc.scalar.copy(out=ob[0:n, :], in_=p_[0:n, :]), reads=[bp],
                                      writes=[bob])
                                fw.dma("sp", dstT[t0:t0 + n, cs], ob[0:n, :], reads=[bob], writes=[bdst])
                            elif kind == "silu":
                                tm, btm = tmp_r.next()
                                sigmoid_from(None, p_[0:n, :], bp, tm[0:n, :], btm)
                                ob, bob = ob_r.next()
                                fw.op(dve, lambda: nc.vector.tensor_tensor(out=ob[0:n, :], in0=p_[0:n, :],
                                                                           in1=tm[0:n, :], op=ALU.mult),
                                      reads=[bp, btm], writes=[bob])
                                fw.dma("sp", dstT[t0:t0 + n, cs], ob[0:n, :], reads=[bob], writes=[bdst])
                            else:
                                tm, btm = tmp_r.next()
                                sigmoid_from(None, p_[0:n, :], bp, tm[0:n, :], btm)
                                fw.op(dve, lambda: nc.vector.tensor_tensor(out=tm[0:n, :], in0=tm[0:n, :],
                                                                           in1=omlB[0:n, cs], op=ALU.mult),
                                      reads=[btm, B_lb], writes=[btm])
                                t2, bt2 = tmp2_r.next()
                                fw.op(dve, lambda: nc.vector.tensor_tensor(out=t2[0:n, :], in0=tm[0:n, :],
                                                                           in1=lbB[0:n, cs], op=ALU.add),
                                      reads=[btm, B_lb], writes=[bt2])
                                of, bof = of_r.next()
                                fw.op(act, lambda: nc.scalar.activation(out=of[0:n, :], in_=t2[0:n, :], func=AF.Ln),
                                      reads=[bt2], writes=[bof])
                                fw.dma("sp", Gs[t0:t0 + n, cs], of[0:n, :], reads=[bof], writes=[B_G])
                                o2, bo2 = of2_r.next()
                                fw.op(dve, lambda: nc.vector.tensor_tensor(out=o2[0:n, :], in0=omlB[0:n, cs],
                                                                           in1=tm[0:n, :], op=ALU.subtract),
                                      reads=[btm, B_lb], writes=[bo2])
                                fw.dma("sp", KKs[t0:t0 + n, cs], o2[0:n, :], reads=[bo2], writes=[B_KK])
                for j in range(2):
                    wa, bwa = w_r.next()
                    load_w(wa, bwa, W, j * 512, 512)
                    wg, bwg = w_r.next()
                    load_w(wg, bwg, W, 1024 + j * 512, 512)
                    cs = slice(j * 512, (j + 1) * 512)
                    for (t0, n) in ((T - 128, 128), (T, TS)):
                        pa, bpa = ps_r.next()
                        pg, bpg = ps_r.next()
                        mm_tm(pa[0:n, :], bpa, wa, bwa, 0, 512, t0, n)
                        mm_tm(pg[0:n, :], bpg, wg, bwg, 0, 512, t0, n)
                        tm, btm = tmp_r.next()
                        sigmoid_from(None, pg[0:n, :], bpg, tm[0:n, :], btm)
                        of, bof = of_r.next()
                        fw.op(dve, lambda: nc.vector.tensor_tensor(out=of[0:n, :], in0=pa[0:n, :], in1=tm[0:n, :],
                                                                   op=ALU.mult),
                              reads=[bpa, btm], writes=[bof])
                        if t0 < T:
                            fw.dma("sp", conv_prompt[e, :, cs], of[128 - HALO:128, :], reads=[bof], writes=[B_out])
                        else:
                            fw.dma("sp", conv_sample[e, HALO - TS:HALO, cs], of[0:TS, :], reads=[bof], writes=[B_out])
            fw.barrier()

            with contextlib.ExitStack() as es:
                diag = sb(es, "diag", [128, 8, CONV_W, 128], BF16)
                cwT = sb(es, "cwT", [128, 8, 32], F32)
                cbB = sb(es, "cbB", [128, D], F32)
                lgB = sb(es, "lgB", [128, D], F32)
                lbB2 = sb(es, "lbB2", [128, D], F32)
                B_cp = Buf()
                with contextlib.ExitStack() as es2:
                    cw = sb(es2, "cw", [CONV_W, D], F32)
                    bcw = Buf()
                    pcw = psum(es2, "pcw", [128, 8, 32], F32)
                    bpcw = Buf()
                    fw.dma("sp", cw[:], conv_w[e], writes=[bcw])
                    for c in range(8):
                        fw.op(pe, lambda: nc.tensor.transpose(pcw[:, c, 0:CONV_W], cw[0:CONV_W, c * 128:(c + 1) * 128],
                                                              ident_f[0:CONV_W, 0:CONV_W]),
                              reads=[bcw, B_const], writes=[bpcw])
                    fw.op(dve, lambda: nc.vector.tensor_copy(cwT[:, :, 0:CONV_W], pcw[:, :, 0:CONV_W]), reads=[bpcw],
                          writes=[B_cp])
                    for c in range(8):
                        for j in range(CONV_W):
                            eng = dve if (j % 2 == 0) else pool
                            ee = nc.vector if (j % 2 == 0) else nc.gpsimd
                            fw.op(eng, lambda: ee.tensor_scalar(out=diag[:, c, j, :], in0=ident_f[:],
                                                                scalar1=cwT[:, c, j:j + 1], scalar2=None,
                                                                op0=ALU.mult),
                                  reads=[B_cp, B_const], writes=[B_cp])
                    fw.dma("sp", cbB[:], conv_b[e:e + 1, :].partition_broadcast(128), writes=[B_cp])
                    fw.dma("sp", lgB[:], conv_ln_g[e:e + 1, :].partition_broadcast(128), writes=[B_cp])
                    fw.dma("sp", lbB2[:], conv_ln_b[e:e + 1, :].partition_broadcast(128), writes=[B_cp])
                    fw.barrier()
                wn_r = Ring([sb(es, f"wn{i}", [128, 8, 128 + HALO], BF16) for i in range(2)])
                za_r = Ring([sb(es, f"za{i}", [128, D], BF16) for i in range(2)])
                cb_r = Ring([sb(es, f"cb{i}", [128, D], F32) for i in range(2)])
                xn_r = Ring([sb(es, f"cxn{i}", [128, D], F32) for i in range(2)])
                ex_r = Ring([sb(es, f"cex{i}", [128, D], F32) for i in range(2)])
                oa_r = Ring([sb(es, f"coa{i}", [128, D], BF16) for i in range(2)])
                oT_r = Ring([sb(es, f"coT{i}", [128, 8, 128], BF16) for i in range(2)])
                es_tiles["cst"] = Ring([sb(es, f"cst{i}", [128, 2, 6], F32) for i in range(2)])
                es_tiles["cmv"] = Ring([sb(es, f"cmv{i}", [128, 4], F32) for i in range(2)])
                pc_r = Ring([psum(es, f"pc{i}", [128, D], F32) for i in range(2)])
                pt_r = Ring([psum(es, f"ptc{i}", [128, 8, 128], BF16) for i in range(2)])
                for (t0, n) in token_tiles:
                    wn, bwn = wn_r.next()
                    src = UTp[:, :, t0:t0 + n + HALO] if t0 < T else UTs[:, :, 0:n + HALO]
                    fw.dma("sp", wn[:, :, 0:n + HALO], src.rearrange("c p t -> p c t"), reads=[B_UT], writes=[bwn])
                    za, bza = za_r.next()
                    fw.dma("sp", za[0:n, :], ZA[t0:t0 + n, :], reads=[B_ZA], writes=[bza])
                    pc, bpc = pc_r.next()
                    for c in range(8):
                        for j in range(CONV_W):
                            fw.op(pe, lambda: nc.tensor.matmul(pc[0:n, c * 128:(c + 1) * 128], wn[:, c, j:j + n],
                                                               diag[:, c, j, :], start=(j == 0),
                                                               stop=(j == CONV_W - 1)),
                                  reads=[bwn, B_cp], writes=[bpc])
                    cb, bcb = cb_r.next()
                    fw.op(dve, lambda: nc.vector.tensor_tensor(out=cb[0:n, :], in0=pc[0:n, :], in1=cbB[0:n, :],
                                                               op=ALU.add),
                          reads=[bpc, B_cp], writes=[bcb])
                    xn, bxn = xn_r.next()
                    layer_norm_rows(es, cb, bcb, n, xn, bxn, lgB, lbB2, B_cp, "c")
                    ex, bex = ex_r.next()
                    sigmoid_from(None, xn[0:n, :], bxn, ex[0:n, :], bex)
                    fw.op(dve, lambda: nc.vector.tensor_tensor(out=ex[0:n, :], in0=ex[0:n, :], in1=xn[0:n, :],
                                                               op=ALU.mult),
                          reads=[bex, bxn], writes=[bex])
                    oa, boa = oa_r.next()
                    fw.op(dve, lambda: nc.vector.tensor_tensor(out=oa[0:n, :], in0=ex[0:n, :], in1=za[0:n, :],
                                                               op=ALU.mult),
                          reads=[bex, bza], writes=[boa])
                    p, bp = pt_r.next()
                    for c in range(8):
                        fw.op(pe, lambda: nc.tensor.transpose(p[:, c, 0:n], oa[0:n, c * 128:(c + 1) * 128],
                                                              ident_bf[0:n, 0:n]),
                              reads=[boa, B_const], writes=[bp])
                    oT, boT = oT_r.next()
                    fw.op(act, lambda: nc.scalar.copy(out=oT[:, :, 0:n], in_=p[:, :, 0:n]), reads=[bp], writes=[boT])
                    fw.dma("sp", AT[0:8, :, t0:t0 + n].rearrange("c p t -> p c t"), oT[:, :, 0:n], reads=[boT],
                           writes=[B_AT])
            fw.barrier()

            with contextlib.ExitStack() as es:
                hgB = sb(es, "hgB", [128, 128], F32)
                bhg = Buf()
                fw.dma("sp", hgB[:], hgrn_norm_g[e:e + 1, :].partition_broadcast(128), writes=[bhg])
                S = sb(es, "S", [128, 8, 128], F32)
                bS = Buf()
                Sa = sb(es, "Sa", [128, 8, 128], BF16)
                bSa = Buf()
                g_r = Ring([sb(es, f"hg{i}", [128, D], F32) for i in range(2)])
                kk_r = Ring([sb(es, f"hkk{i}", [128, D], F32) for i in range(2)])
                v_r = Ring([sb(es, f"hv{i}", [128, D], BF16) for i in range(2)])
                zb_r = Ring([sb(es, f"hzb{i}", [128, D], BF16) for i in range(2)])
                qT_r = Ring([sb(es, f"hqT{i}", [128, 8, 128], F32) for i in range(2)])
                kT_r = Ring([sb(es, f"hkT{i}", [128, 8, 128], F32) for i in range(2)])
                Eq = sb(es, "Eq", [128, 8, 128], F32)
                Ek = sb(es, "Ek", [128, 8, 128], F32)
                ER = sb(es, "ER", [128, D], F32)
                Ee = sb(es, "Ee", [128, 8, 2], F32)
                qtl = sb(es, "qtl", [128, 8, 128], BF16)
                ktl = sb(es, "ktl", [128, 8, 128], BF16)
                khat = sb(es, "khat", [128, D], BF16)
                attm = sb(es, "attm", [128, 8, 128], BF16)
                osq = sb(es, "osq", [128, D], F32)
                oss = sb(es, "oss", [128, 8], F32)
                on = sb(es, "on", [128, D], F32)
                ob = sb(es, "hob", [128, D], BF16)
                oT = sb(es, "hoT", [128, 8, 128], BF16)
                bEq, bEk, bER, bEe, bqtl, bktl, bkhat, battm, bosq, boss, bon, bob, boT = [Buf() for _ in range(13)]
                pBA = psum(es, "pBA", [128, 8, 128], F32)
                pRS = psum(es, "pRS", [128, D], F32)
                pO = psum(es, "pO", [128, D], F32)
                pE = psum(es, "pE", [128, 8, 2], F32)
                pT = psum(es, "pTh", [128, 8, 128], BF16)
                bpBA, bpRS, bpO, bpE, bpT = [Buf() for _ in range(5)]

                for seq in (0, 1):
                    tiles = [(t0, 64) for t0 in range(0, T, 64)] if seq == 0 else [(T, TS)]
                    if seq == 0:
                        fw.op(pool, lambda: nc.gpsimd.memset(S[:], 0.0), reads=[bS], writes=[bS])
                    else:
                        fw.dma("sp", S[:], state_hgrn[e].rearrange("h k v -> k h v"), reads=[bS], writes=[bS])
                    for (t0, n) in tiles:
                        g_, bg = g_r.next()
                        fw.dma("sp", g_[0:n, :], Gs[t0:t0 + n, :], reads=[B_G], writes=[bg])
                        kk_, bkk = kk_r.next()
                        fw.dma("sp", kk_[0:n, :], KKs[t0:t0 + n, :], reads=[B_KK], writes=[bkk])
                        v_, bv = v_r.next()
                        fw.dma("sp", v_[0:n, :], VH[t0:t0 + n, :], reads=[B_VH], writes=[bv])
                        zb_, bzb = zb_r.next()
                        fw.dma("sp", zb_[0:n, :], ZB[t0:t0 + n, :], reads=[B_ZB], writes=[bzb])
                        qT, bqT = qT_r.next()
                        fw.dma("sp", qT[:, :, 0:n], QTe[:, :, t0:t0 + n].rearrange("h k t -> k h t"), reads=[B_QTe],
                               writes=[bqT])
                        kT, bkT = kT_r.next()
                        fw.dma("sp", kT[:, :, 0:n], KKT[:, :, t0:t0 + n].rearrange("h k t -> k h t"), reads=[B_KKT],
                               writes=[bkT])
                        for h in range(8):
                            hs = slice(h * 128, (h + 1) * 128)
                            fw.op(pe, lambda: nc.tensor.matmul(pBA[:, h, 0:n], g_[0:n, hs], Mcum[0:n, 0:n],
                                                               start=True, stop=True),
                                  reads=[bg, B_const], writes=[bpBA])
                            fw.op(pe, lambda: nc.tensor.matmul(pE[:, h, :], g_[0:n, hs], Mext[0:n, :], start=True,
                                                               stop=True),
                                  reads=[bg, B_const], writes=[bpE])
                        for hlf in range(2):
                            cs = slice(hlf * 512, (hlf + 1) * 512)
                            fw.op(pe, lambda: nc.tensor.matmul(pRS[0:n, cs], Mrev[0:n, 0:n], g_[0:n, cs], start=True,
                                                               stop=True),
                                  reads=[bg, B_const], writes=[bpRS])
                        fw.op(act, lambda: nc.scalar.activation(out=Eq[:, :, 0:n], in_=pBA[:, :, 0:n], func=AF.Exp),
                              reads=[bpBA], writes=[bEq])
                        fw.op(act, lambda: nc.scalar.activation(out=Ek[:, :, 0:n], in_=pBA[:, :, 0:n], func=AF.Exp,
                                                                scale=-1.0),
                              reads=[bpBA], writes=[bEk])
                        fw.op(act, lambda: nc.scalar.activation(out=ER[0:n, :], in_=pRS[0:n, :], func=AF.Exp),
                              reads=[bpRS], writes=[bER])
                        fw.op(act, lambda: nc.scalar.activation(out=Ee[:], in_=pE[:], func=AF.Exp), reads=[bpE],
                              writes=[bEe])
                        fw.op(dve, lambda: nc.vector.tensor_tensor(out=qtl[:, :, 0:n], in0=qT[:, :, 0:n],
                                                                   in1=Eq[:, :, 0:n], op=ALU.mult),
                              reads=[bqT, bEq], writes=[bqtl])
                        fw.op(dve, lambda: nc.vector.tensor_tensor(out=ktl[:, :, 0:n], in0=kT[:, :, 0:n],
                                                                   in1=Ek[:, :, 0:n], op=ALU.mult),
                              reads=[bkT, bEk], writes=[bktl])
                        fw.op(dve, lambda: nc.vector.tensor_tensor(out=khat[0:n, :], in0=kk_[0:n, :], in1=ER[0:n, :],
                                                                   op=ALU.mult),
                              reads=[bkk, bER], writes=[bkhat])
                        fw.op(dve, lambda: nc.vector.tensor_tensor(out=Sa[:], in0=S[:],
                                                                   in1=Ee[:, :, 0:1].broadcast_to([128, 8, 128]),
                                                                   op=ALU.mult),
                              reads=[bS, bEe], writes=[bSa])
                        for h in range(8):
                            fw.op(pe, lambda: nc.tensor.matmul(pBA[0:n, h, 0:n], ktl[:, h, 0:n], qtl[:, h, 0:n],
                                                               start=True, stop=True),
                                  reads=[bktl, bqtl, bEq, bEk], writes=[bpBA])
                        fw.op(dve, lambda: nc.vector.tensor_tensor(
                            out=attm[0:n, :, 0:n], in0=pBA[0:n, :, 0:n],
                            in1=Mmask[0:n, 0:n].unsqueeze(1).broadcast_to([n, 8, n]), op=ALU.mult),
                            reads=[bpBA, B_const], writes=[battm])
                        for h in range(8):
                            hs = slice(h * 128, (h + 1) * 128)
                            fw.op(pe, lambda: nc.tensor.matmul(pO[0:n, hs], qtl[:, h, 0:n], Sa[:, h, :], start=True,
                                                               stop=False),
                                  reads=[bqtl, bSa], writes=[bpO])
                            fw.op(pe, lambda: nc.tensor.matmul(pO[0:n, hs], attm[0:n, h, 0:n], v_[0:n, hs],
                                                               start=False, stop=True),
                                  reads=[battm, bv], writes=[bpO])
                        for h in range(8):
                            hs = slice(h * 128, (h + 1) * 128)
                            fw.op(pe, lambda: nc.tensor.matmul(pRS[:, hs], khat[0:n, hs], v_[0:n, hs], start=True,
                                                               stop=True),
                                  reads=[bkhat, bv, bER], writes=[bpRS])
                        fw.op(dve, lambda: nc.vector.tensor_tensor(out=S[:], in0=S[:],
                                                                   in1=Ee[:, :, 1:2].broadcast_to([128, 8, 128]),
                                                                   op=ALU.mult),
                              reads=[bS, bEe, bSa], writes=[bS])
                        fw.op(dve, lambda: nc.vector.tensor_tensor(out=S[:], in0=S[:],
                                                                   in1=pRS[:].rearrange("k (h v) -> k h v", h=8),
                                                                   op=ALU.add),
                              reads=[bS, bpRS], writes=[bS])
                        fw.op(act, lambda: nc.scalar.activation(out=osq[0:n, :], in_=pO[0:n, :], func=AF.Square),
                              reads=[bpO], writes=[bosq])
                        fw.op(dve, lambda: nc.vector.tensor_reduce(out=oss[0:n, :],
                                                                   in_=osq[0:n, :].rearrange("t (h v) -> t h v", h=8),
                                                                   axis=AX.X, op=ALU.add),
                              reads=[bosq], writes=[boss])
                        fw.op(act, lambda: nc.scalar.activation(out=oss[0:n, :], in_=oss[0:n, :], func=AF.Ln,
                                                                scale=1.0 / 128, bias=epsb[0:n, :]),
                              reads=[boss, B_const], writes=[boss])
                        fw.op(act, lambda: nc.scalar.activation(out=oss[0:n, :], in_=oss[0:n, :], func=AF.Exp,
                                                                scale=-0.5),
                              reads=[boss], writes=[boss])
                        fw.op(dve, lambda: nc.vector.tensor_tensor(
                            out=on[0:n, :].rearrange("t (h v) -> t h v", h=8),
                            in0=pO[0:n, :].rearrange("t (h v) -> t h v", h=8),
                            in1=oss[0:n, :].unsqueeze(2).broadcast_to([n, 8, 128]), op=ALU.mult),
                            reads=[bpO, boss], writes=[bon])
                        fw.op(dve, lambda: nc.vector.tensor_tensor(
                            out=on[0:n, :].rearrange("t (h v) -> t h v", h=8),
                            in0=on[0:n, :].rearrange("t (h v) -> t h v", h=8),
                            in1=hgB[0:n, :].unsqueeze(1).broadcast_to([n, 8, 128]), op=ALU.mult),
                            reads=[bon, bhg], writes=[bon])
                        fw.op(dve, lambda: nc.vector.tensor_tensor(out=ob[0:n, :], in0=on[0:n, :], in1=zb_[0:n, :],
                                                                   op=ALU.mult),
                              reads=[bon, bzb], writes=[bob])
                        for h in range(8):
                            fw.op(pe, lambda: nc.tensor.transpose(pT[:, h, 0:n], ob[0:n, h * 128:(h + 1) * 128],
                                                                  ident_bf[0:n, 0:n]),
                                  reads=[bob, B_const], writes=[bpT])
                        fw.op(act, lambda: nc.scalar.copy(out=oT[:, :, 0:n], in_=pT[:, :, 0:n]), reads=[bpT],
                              writes=[boT])
                        fw.dma("sp", AT[8:16, :, t0:t0 + n].rearrange("c p t -> p c t"), oT[:, :, 0:n], reads=[boT],
                               writes=[B_AT])
                    dstS = hgrn_prompt[e] if seq == 0 else hgrn_sample[e]
                    fw.dma("sp", dstS.rearrange("h k v -> k h v"), S[:], reads=[bS], writes=[B_out])
            fw.barrier()
        phase_out(l, w_out_even[e], xsrc, bxsrc, last=False)

    def odd_layer(l):
        o = l // 2
        W = w_in_odd[o]
        lam_init = 0.8 - 0.6 * math.exp(-0.3 * l)
        xsrc, bxsrc = xres[l % 2], Bx[l % 2]
        with contextlib.ExitStack() as esL:
            nlam = sb(esL, "nlam", [128, 1], F32)
            sgB = sb(esL, "sgB", [128, 128], F32)
            B_lm = Buf()
            with contextlib.ExitStack() as es:
                lt = sb(es, "lt", [128, 4, 64], F32)
                ls = sb(es, "ls", [128, 2], F32)
                for i, src in enumerate((lam_q1, lam_k1, lam_q2, lam_k2)):
                    fw.dma("sp", lt[:, i, :], src[o:o + 1, :].partition_broadcast(128), writes=[B_lm])
                fw.op(dve, lambda: nc.vector.tensor_tensor(out=lt[:, 0, :], in0=lt[:, 0, :], in1=lt[:, 1, :],
                                                           op=ALU.mult), reads=[B_lm], writes=[B_lm])
                fw.op(dve, lambda: nc.vector.tensor_tensor(out=lt[:, 2, :], in0=lt[:, 2, :], in1=lt[:, 3, :],
                                                           op=ALU.mult), reads=[B_lm], writes=[B_lm])
                fw.op(dve, lambda: nc.vector.tensor_reduce(out=ls[:, 0:1], in_=lt[:, 0, :], axis=AX.X, op=ALU.add),
                      reads=[B_lm], writes=[B_lm])
                fw.op(dve, lambda: nc.vector.tensor_reduce(out=ls[:, 1:2], in_=lt[:, 2, :], axis=AX.X, op=ALU.add),
                      reads=[B_lm], writes=[B_lm])
                fw.op(act, lambda: nc.scalar.activation(out=ls[:], in_=ls[:], func=AF.Exp), reads=[B_lm],
                      writes=[B_lm])
                fw.op(dve, lambda: nc.vector.tensor_tensor(out=nlam[:], in0=ls[:, 1:2], in1=ls[:, 0:1],
                                                           op=ALU.subtract), reads=[B_lm], writes=[B_lm])
                fw.op(dve, lambda: nc.vector.tensor_scalar_add(nlam[:], nlam[:], -lam_init), reads=[B_lm],
                      writes=[B_lm])
                fw.dma("sp", sgB[:], subln_g[o:o + 1, :].partition_broadcast(128), writes=[B_lm])
                fw.op(dve, lambda: nc.vector.tensor_scalar(out=sgB[:], in0=sgB[:], scalar1=1.0 - lam_init,
                                                           scalar2=None, op0=ALU.mult), reads=[B_lm], writes=[B_lm])
                fw.barrier()

            with contextlib.ExitStack() as es:
                w_r = Ring([sb(es, f"w{i}", [128, 8, 512], BF16) for i in range(3)])
                ps_r = Ring([psum(es, f"pp{i}", [128, 512], F32) for i in range(6)])
                tmp_r = Ring([sb(es, f"tmp{i}", [128, 512], F32) for i in range(3)])
                of_r = Ring([sb(es, f"of{i}", [128, 512], F32) for i in range(3)])
                ob_r = Ring([sb(es, f"ob{i}", [128, 512], BF16) for i in range(3)])
                import os
                KSKIP = os.environ.get("KSKIP", "")
                for (col0, dstT, bdst, sc) in ((0, QTo, B_QTo, 0.125), (2048, KTo, B_KTo, 1.0)):
                    if "A" in KSKIP:
                        continue
                    for j in range(4):
                        w_, bw = w_r.next()
                        load_w(w_, bw, W, col0 + j * 512, 512)
                        for cc in range(4):
                            hh = j * 4 + cc
                            for (t0, n) in token_blocks:
                                p_, bp = ps_r.next()
                                mm_fm(p_[:, 0:n], bp, w_, bw, cc * 128, t0, n)
                                ob, bob = ob_r.next()
                                fw.op(act, lambda: nc.scalar.mul(out=ob[:, 0:n], in_=p_[:, 0:n], mul=sc), reads=[bp],
                                      writes=[bob])
                                fw.dma("sp", dstT[hh, :, t0:t0 + n], ob[:, 0:n], reads=[bob], writes=[bdst])
                for (col0, kind) in ((2048, "k"), (4096, "v"), (6144, "z")):
                    if kind in KSKIP:
                        continue
                    for j in range(4):
                        w_, bw = w_r.next()
                        load_w(w_, bw, W, col0 + j * 512, 512)
                        cs = slice(j * 512, (j + 1) * 512)
                        for (t0, n) in token_tiles:
                            p_, bp = ps_r.next()
                            mm_tm(p_[0:n, :], bp, w_, bw, 0, 512, t0, n)
                            if kind in ("k", "v"):
                                of, bof = of_r.next()
                                fw.op(act, lambda: nc.scalar.copy(out=of[0:n, :], in_=p_[0:n, :]), reads=[bp],
                                      writes=[bof])
                                if kind == "k":
                                    dst = k_prompt[o, t0:t0 + n, cs] if t0 < T else k_sample[o, :, cs]
                                else:
                                    dst = v_prompt[o, t0:t0 + n, cs] if t0 < T else v_sample[o, :, cs]
                                fw.dma("pool", dst, of[0:n, :], reads=[bof], writes=[B_out])
                                if kind == "v":
                                    ob, bob = ob_r.next()
                                    fw.op(dve, lambda: nc.vector.tensor_copy(ob[0:n, :], of[0:n, :]), reads=[bof],
                                          writes=[bob])
                                    fw.dma("sp", Vo[t0:t0 + n, cs], ob[0:n, :], reads=[bob], writes=[B_Vo])
                            else:
                                tm, btm = tmp_r.next()
                                sigmoid_from(None, p_[0:n, :], bp, tm[0:n, :], btm)
                                ob, bob = ob_r.next()
                                fw.op(dve, lambda: nc.vector.tensor_tensor(out=ob[0:n, :], in0=p_[0:n, :],
                                                                           in1=tm[0:n, :], op=ALU.mult),
                                      reads=[bp, btm], writes=[bob])
                                fw.dma("sp", Zo[t0:t0 + n, cs], ob[0:n, :], reads=[bob], writes=[B_Zo])
            fw.barrier()

            with contextlib.ExitStack() as es:
                NKT_P = T // 128
                NKT_S = NTK + 1
                NKT_MAX = max(NKT_P, NKT_S)
                KLEN = max(T, P + TS)
                kt_r = Ring([sb(es, f"kt{i}", [128, KLEN], BF16) for i in range(2)])
                va_r = Ring([sb(es, f"va{i}", [128, NKT_MAX, 130], BF16) for i in range(2)])
                qt_r = Ring([sb(es, f"qt{i}", [128, T], BF16) for i in range(2)])
                z_r = Ring([sb(es, f"zz{i}", [128, NKT_P, 128], BF16) for i in range(2)])
                aT_r = Ring([sb(es, f"aT{i}", [128, T], BF16) for i in range(2)])
                pt_r = Ring([sb(es, f"pt{i}", [128, 1024], BF16) for i in range(3)])
                accs_r = Ring([sb(es, f"accs{i}", [128, 3, 512], F32) for i in range(2)])
                kc_r = Ring([sb(es, f"kc{i}", [128, 8, 128], BF16) for i in range(2)])
                sm_r = Ring([sb(es, f"sm{i}", [128, 8], F32) for i in range(4)])
                o1_r = Ring([sb(es, f"o1{i}", [128, 128], F32) for i in range(2)])
                o2_r = Ring([sb(es, f"o2{i}", [128, 128], F32) for i in range(2)])
                ab_r = Ring([sb(es, f"ab{i}", [128, 128], BF16) for i in range(2)])
                ps_r = Ring([psum(es, f"psc{i}", [128, 1024], F32) for i in range(2)])
                pacc = psum(es, "pacc", [128, 3, 512], F32)
                bpacc = [Buf() for _ in range(8)]
                ptr_r = Ring([psum(es, f"ptr{i}", [128, 1024], BF16) for i in range(1)])
                zl = sb(es, "zl", [128, 128], BF16)
                zr = sb(es, "zr", [128, 387], BF16)
                bz = Buf()
                fw.op(pool, lambda: nc.gpsimd.memset(zl[:], 0.0), writes=[bz])
                fw.op(pool, lambda: nc.gpsimd.memset(zr[:], 0.0), writes=[bz])

                def open_banks(banks):
                    for b in banks:
                        fw.op(pe, lambda: nc.tensor.matmul(pacc[:, b, 0:387], zl[:], zr[:], start=True, stop=False),
                              reads=[bz], writes=[bpacc[i] for i in range(8) if i // 3 == b])

                for (va, bva) in va_r.tiles:
                    fw.op(pool, lambda: nc.gpsimd.memset(va[:, :, 128:129], 1.0), writes=[bva])

                def acc_ap(i, nq):
                    return pacc[0:nq, i // 3, (i % 3) * 129:(i % 3) * 129 + 129]

                def finalize(sub_list, zt, bzt, aT, baT, zrow0):
                    accs, baccs = accs_r.next()
                    for b in range(3):
                        fw.op(act, lambda: nc.scalar.copy(out=accs[:, b, :], in_=pacc[:, b, :]),
                              reads=[bpacc[i] for i in range(8) if i // 3 == b], writes=[baccs])
                    for (i0, nq, q0, zi) in sub_list:
                        def ac(i, c0, c1):
                            return accs[0:nq, i // 3, (i % 3) * 129 + c0:(i % 3) * 129 + c1]
                        sm, bsm = sm_r.next()
                        fw.op(dve, lambda: nc.vector.reciprocal(sm[0:nq, 0:1], ac(i0, 128, 129)), reads=[baccs],
                              writes=[bsm])
                        fw.op(dve, lambda: nc.vector.reciprocal(sm[0:nq, 1:2], ac(i0 + 1, 128, 129)), reads=[baccs],
                              writes=[bsm])
                        fw.op(dve, lambda: nc.vector.tensor_scalar(out=sm[0:nq, 1:2], in0=sm[0:nq, 1:2],
                                                                   scalar1=nlam[0:nq, 0:1], scalar2=None,
                                                                   op0=ALU.mult), reads=[bsm, B_lm], writes=[bsm])
                        o1, bo1 = o1_r.next()
                        fw.op(dve, lambda: nc.vector.tensor_scalar(out=o1[0:nq, :], in0=ac(i0, 0, 128),
                                                                   scalar1=sm[0:nq, 0:1], scalar2=None, op0=ALU.mult),
                              reads=[baccs, bsm], writes=[bo1])
                        fw.op(dve, lambda: nc.vector.scalar_tensor_tensor(out=o1[0:nq, :], in0=ac(i0 + 1, 0, 128),
                                                                          scalar=sm[0:nq, 1:2], in1=o1[0:nq, :],
                                                                          op0=ALU.mult, op1=ALU.add),
                              reads=[baccs, bsm, bo1], writes=[bo1])
                        o2, bo2 = o2_r.next()
                        fw.op(dve, lambda: nc.vector.tensor_tensor(out=o2[0:nq, :], in0=o1[0:nq, :], in1=o1[0:nq, :],
                                                                   op=ALU.mult), reads=[bo1], writes=[bo2])
                        fw.op(dve, lambda: nc.vector.tensor_reduce(out=sm[0:nq, 2:3], in_=o2[0:nq, :], axis=AX.X,
                                                                   op=ALU.add), reads=[bo2], writes=[bsm])
                        fw.op(act, lambda: nc.scalar.activation(out=sm[0:nq, 2:3], in_=sm[0:nq, 2:3], func=AF.Ln,
                                                                scale=1.0 / 128, bias=epsb[0:nq, :]),
                              reads=[bsm, B_const], writes=[bsm])
                        fw.op(act, lambda: nc.scalar.activation(out=sm[0:nq, 2:3], in_=sm[0:nq, 2:3], func=AF.Exp,
                                                                scale=-0.5), reads=[bsm], writes=[bsm])
                        fw.op(dve, lambda: nc.vector.scalar_tensor_tensor(out=o2[0:nq, :], in0=o1[0:nq, :],
                                                                          scalar=sm[0:nq, 2:3], in1=sgB[0:nq, :],
                                                                          op0=ALU.mult, op1=ALU.mult),
                              reads=[bo1, bsm, B_lm, bo2], writes=[bo2])
                        ab, bab = ab_r.next()
                        fw.op(dve, lambda: nc.vector.tensor_tensor(out=ab[0:nq, :], in0=o2[0:nq, :],
                                                                   in1=zt[0:nq, zi, :], op=ALU.mult),
                              reads=[bo2, bzt], writes=[bab])
                        ptr, bptr = ptr_r.next()
                        fw.op(pe, lambda: nc.tensor.transpose(ptr[:, 0:nq], ab[0:nq, :], ident_bf[0:nq, 0:nq]),
                              reads=[bab, B_const], writes=[bptr])
                        fw.op(act, lambda: nc.scalar.copy(out=aT[:, q0:q0 + nq], in_=ptr[:, 0:nq]), reads=[bptr],
                              writes=[baT])

                for h in range(NH):
                    hs = slice(h * 128, (h + 1) * 128)
                    kt, bkt = kt_r.next()
                    fw.dma("sp", kt[:, 0:T], KTo[h, :, 0:T], reads=[B_KTo], writes=[bkt])
                    qt, bqt = qt_r.next()
                    fw.dma("sp", qt[:, 0:T], QTo[h, :, 0:T], reads=[B_QTo], writes=[bqt])
                    va, bva = va_r.next()
                    fw.dma("sp", va[:, 0:NKT_P, 0:128], Vo[0:T, hs].rearrange("(m p) d -> p m d", p=128),
                           reads=[B_Vo], writes=[bva])
                    zt, bzt = z_r.next()
                    fw.dma("sp", zt[:, 0:NKT_P, :], Zo[0:T, hs].rearrange("(m p) d -> p m d", p=128), reads=[B_Zo],
                           writes=[bzt])
                    aT, baT = aT_r.next()
                    for J in range(T // 512 if "P" not in os.environ.get("KSKIP2", "") else 0):
                        q0 = J * 512
                        nkt = 4 * J + 4
                        open_banks((0, 1, 2))
                        for m in range(nkt):
                            i_d = m - 4 * J
                            qlo = 0 if i_d < 0 else i_d * 128
                            psc, bpsc = ps_r.next()
                            for c in range(2):
                                fw.op(pe, lambda: nc.tensor.matmul(psc[:, c * 512 + qlo:(c + 1) * 512],
                                                                   kt[c * 64:(c + 1) * 64, m * 128:(m + 1) * 128],
                                                                   qt[c * 64:(c + 1) * 64, q0 + qlo:q0 + 512],
                                                                   start=True, stop=True),
                                      reads=[bkt, bqt], writes=[bpsc])
                            pt, bpt = pt_r.next()
                            psv = psc[:].rearrange("k (c q) -> k c q", c=2)
                            ptv = pt[:].rearrange("k (c q) -> k c q", c=2)
                            fw.op(act, lambda: nc.scalar.activation(out=ptv[:, :, qlo:512], in_=psv[:, :, qlo:512],
                                                                    func=AF.Exp), reads=[bpsc], writes=[bpt])
                            if i_d >= 0:
                                fw.op(pool, lambda: nc.gpsimd.memset(ptv[64:128, :, i_d * 128:i_d * 128 + 64], 0.0),
                                      reads=[bpt], writes=[bpt])
                            for s in range(4):
                                if i_d > s:
                                    continue
                                last_m = 4 * J + s
                                for c in range(2):
                                    ai = s * 2 + c
                                    fw.op(pe, lambda: nc.tensor.matmul(acc_ap(ai, 128),
                                                                       pt[:, c * 512 + s * 128:c * 512 + (s + 1) * 128],
                                                                       va[:, m, 0:129], start=False,
                                                                       stop=(m == last_m)),
                                          reads=[bpt, bva], writes=[bpacc[ai]])
                        finalize([(s * 2, 128, q0 + s * 128, J * 4 + s) for s in range(4)], zt, bzt, aT, baT, 0)
                    fw.dma("sp", AT[h, :, 0:T], aT[:, 0:T], reads=[baT], writes=[B_AT])

                    kt, bkt = kt_r.next()
                    va, bva = va_r.next()
                    fw.dma("pool", va[:, 0:NTK, 0:128], cache_v[o, :, hs].rearrange("(m p) d -> p m d", p=128),
                           reads=[B_in], writes=[bva])
                    fw.dma("sp", va[0:TS, NTK, 0:128], Vo[T:TT, hs], reads=[B_Vo], writes=[bva])
                    fw.dma("sp", kt[:, P:P + TS], KTo[h, :, T:TT], reads=[B_KTo], writes=[bkt])
                    for m0 in range(0, NTK, 8):
                        kc, bkc = kc_r.next()
                        nm = min(8, NTK - m0)
                        fw.dma("pool", kc[:, 0:nm, :],
                               cache_k[o, m0 * 128:(m0 + nm) * 128, hs].rearrange("(m p) d -> p m d", p=128),
                               reads=[B_in], writes=[bkc])
                        ptr, bptr = ptr_r.next()
                        for mm in range(nm):
                            fw.op(pe, lambda: nc.tensor.transpose(ptr[:, mm * 128:(mm + 1) * 128], kc[:, mm, :],
                                                                  ident_bf[:]),
                                  reads=[bkc, B_const], writes=[bptr])
                        fw.op(dve, lambda: nc.vector.tensor_copy(kt[:, m0 * 128:(m0 + nm) * 128], ptr[:, 0:nm * 128]),
                              reads=[bptr], writes=[bkt])
                    qt, bqt = qt_r.next()
                    fw.dma("sp", qt[:, 0:TS], QTo[h, :, T:TT], reads=[B_QTo], writes=[bqt])
                    zt, bzt = z_r.next()
                    fw.dma("sp", zt[0:TS, 0, :], Zo[T:TT, hs], reads=[B_Zo], writes=[bzt])
                    aT, baT = aT_r.next()
                    open_banks((0,))
                    GK = 32
                    for m0 in range(0, NKT_S if "S" not in os.environ.get("KSKIP2", "") else 0, GK):
                        nm = min(GK, NKT_S - m0)
                        psc, bpsc = ps_r.next()
                        pt, bpt = pt_r.next()
                        psv = psc[:].rearrange("k (c q) -> k c q", c=2)
                        ptv = pt[:].rearrange("k (c q) -> k c q", c=2)
                        for mm in range(nm):
                            m = m0 + mm
                            nk = 128 if m < NTK else TS
                            for c in range(2):
                                fw.op(pe, lambda: nc.tensor.matmul(psv[0:nk, c, mm * 16:(mm + 1) * 16],
                                                                   kt[c * 64:(c + 1) * 64, m * 128:m * 128 + nk],
                                                                   qt[c * 64:(c + 1) * 64, 0:TS], start=True,
                                                                   stop=True),
                                      reads=[bkt, bqt], writes=[bpsc])
                        nfull = nm if m0 + nm < NKT_S else nm - 1
                        if nfull > 0:
                            fw.op(act, lambda: nc.scalar.activation(out=ptv[:, :, 0:nfull * 16],
                                                                    in_=psv[:, :, 0:nfull * 16], func=AF.Exp),
                                  reads=[bpsc], writes=[bpt])
                        if nfull < nm:
                            fw.op(act, lambda: nc.scalar.activation(out=ptv[0:TS, :, nfull * 16:nm * 16],
                                                                    in_=psv[0:TS, :, nfull * 16:nm * 16],
                                                                    func=AF.Exp),
                                  reads=[bpsc], writes=[bpt])
                        for mm in range(nm):
                            m = m0 + mm
                            nk = 128 if m < NTK else TS
                            for c in range(2):
                                fw.op(pe, lambda: nc.tensor.matmul(acc_ap(c, TS), ptv[0:nk, c, mm * 16:(mm + 1) * 16],
                                                                   va[0:nk, m, 0:129], start=False,
                                                                   stop=(m == NKT_S - 1)),
                                      reads=[bpt, bva], writes=[bpacc[c]])
                    finalize([(0, TS, 0, 0)], zt, bzt, aT, baT, 0)
                    fw.dma("sp", AT[h, :, T:TT], aT[:, 0:TS], reads=[baT], writes=[B_AT])
            if os.environ.get("KDUMP", "") == "1" and l == 1:
                fw.dma("pool", dbg_AT, AT, reads=[B_AT], writes=[B_out])
            fw.barrier()
        phase_out(l, w_out_odd[o], xsrc, bxsrc, last=(l == DEPTH - 1))

    import os
    if os.environ.get("KTESTOUT", "") == "1":
        with contextlib.ExitStack() as es:
            tt_ = sb(es, "tt_", [128, 2048], F32)
            btt = Buf()
            fw.op(pool, lambda: nc.gpsimd.memset(tt_[:], 7.0), writes=[btt])
            for o in range(2):
                for t0 in range(0, T, 128):
                    fw.dma("sp", k_prompt[o, t0:t0 + 128, :], tt_[:], reads=[btt], writes=[B_out])
                    fw.dma("sp", v_prompt[o, t0:t0 + 128, :], tt_[:], reads=[btt], writes=[B_out])
                fw.dma("sp", k_sample[o, :, :], tt_[0:TS, :], reads=[btt], writes=[B_out])
                fw.dma("sp", v_sample[o, :, :], tt_[0:TS, :], reads=[btt], writes=[B_out])
                fw.dma("sp", conv_prompt[o, :, :], tt_[0:HALO, 0:D], reads=[btt], writes=[B_out])
                fw.dma("sp", conv_sample[o, :, :], tt_[0:HALO, 0:D], reads=[btt], writes=[B_out])
                fw.dma("sp", hgrn_prompt[o].rearrange("h k v -> k h v"), tt_[:, 0:1024].rearrange("k (h v) -> k h v", h=8), reads=[btt], writes=[B_out])
                fw.dma("sp", hgrn_sample[o].rearrange("h k v -> k h v"), tt_[:, 0:1024].rearrange("k (h v) -> k h v", h=8), reads=[btt], writes=[B_out])
            for t0 in range(0, T, 128):
                fw.dma("sp", y_prompt[t0:t0 + 128, :], tt_[:, 0:D], reads=[btt], writes=[B_out])
            fw.dma("sp", y_sample[:, :], tt_[0:TS, 0:D], reads=[btt], writes=[B_out])
        fw.stopped = True
        fw._barrier()
    try:
        phase_load_x()
        for l in range(DEPTH):
            if l % 2 == 0:
                even_layer(l)
            else:
                odd_layer(l)
    except StopBuild:
        print("build stopped at barrier", fw.n_bar)
    fw.finish()
    top.close()
    fw.close()
    return nc, fw


_CACHE = {}


def _get_program(T, P):
    key = (T, P)
    if key not in _CACHE:
        _CACHE[key] = build_program(T, P)
    return _CACHE[key]


def kernel(**inputs):
    x_prompt = np.asarray(inputs["x_prompt"], dtype=np.float32)
    BATCH, T, _ = x_prompt.shape
    x_sample = np.asarray(inputs["x_sample"], dtype=np.float32)
    NS = x_sample.shape[0]
    P = inputs["cache_k"].shape[2]
    n_cores = 8
    nc, fw = _get_program(T, P)
    shared = ["w_in_even", "w_out_even", "conv_w", "conv_b", "conv_ln_g", "conv_ln_b", "hgrn_lb_logits",
              "hgrn_norm_g", "w_in_odd", "w_out_odd", "lam_q1", "lam_k1", "lam_q2", "lam_k2", "subln_g",
              "post_ln_g", "post_ln_b"]
    sh = {k: np.ascontiguousarray(np.asarray(inputs[k], dtype=np.float32)) for k in shared}
    in_maps = []
    for c in range(n_cores):
        bp = c % BATCH
        bs = c % NS
        m = dict(sh)
        m["x_prompt"] = np.ascontiguousarray(x_prompt[bp])
        m["x_sample"] = np.ascontiguousarray(x_sample[bs])
        m["state_conv"] = np.ascontiguousarray(np.asarray(inputs["state_conv"])[:, bs])
        m["state_hgrn"] = np.ascontiguousarray(np.asarray(inputs["state_hgrn"])[:, bs])
        m["cache_k"] = np.ascontiguousarray(np.asarray(inputs["cache_k"])[:, bs]).reshape(2, P, 2048)
        m["cache_v"] = np.ascontiguousarray(np.asarray(inputs["cache_v"])[:, bs]).reshape(2, P, 2048)
        in_maps.append(m)
    res = run_bass_kernel_spmd(nc, in_maps, core_ids=list(range(n_cores)))
    R = res.results

    def stack_p(name, shp):
        return np.stack([np.asarray(R[b][name]).reshape(shp) for b in range(BATCH)], axis=0)

    def stack_s(name, shp):
        return np.stack([np.asarray(R[b][name]).reshape(shp) for b in range(NS)], axis=0)

    y_prompt = stack_p("y_prompt", (T, D))
    y_sample = stack_s("y_sample", (TS, D))
    conv_prompt = np.moveaxis(stack_p("conv_prompt", (2, HALO, D)), 0, 1)
    hgrn_prompt = np.moveaxis(stack_p("hgrn_prompt", (2, 8, 128, 128)), 0, 1)
    k_prompt = np.moveaxis(stack_p("k_prompt", (2, T, 16, 128)), 0, 1)
    v_prompt = np.moveaxis(stack_p("v_prompt", (2, T, 16, 128)), 0, 1)
    conv_sample = np.moveaxis(stack_s("conv_sample", (2, HALO, D)), 0, 1)
    hgrn_sample = np.moveaxis(stack_s("hgrn_sample", (2, 8, 128, 128)), 0, 1)
    k_sample = np.moveaxis(stack_s("k_sample", (2, TS, 16, 128)), 0, 1)
    v_sample = np.moveaxis(stack_s("v_sample", (2, TS, 16, 128)), 0, 1)
    outs = (y_prompt, y_sample, conv_prompt, hgrn_prompt, k_prompt, v_prompt, conv_sample, hgrn_sample, k_sample,
            v_sample)
    return tuple(np.ascontiguousarray(a, dtype=np.float32) for a in outs)
```

```python
import contextlib
import math
import numpy as np
import concourse.bass as bass
import concourse.mybir as mybir
from concourse.bass_utils import run_bass_kernel_spmd

F32 = mybir.dt.float32
BF16 = mybir.dt.bfloat16
AF = mybir.ActivationFunctionType
ALU = mybir.AluOpType
AX = mybir.AxisListType

D = 1024
DEPTH = 4
TS = 16
CONV_W = 31
HALO = CONV_W - 1
ALPHA = (2.0 * DEPTH) ** 0.25
EPS = 1e-5
NH = 16


class Buf:
    __slots__ = ("w", "r", "multi")

    def __init__(self, multi=False):
        self.multi = multi
        self.w = {}
        self.r = {}


class Eng:
    def __init__(self, eng, sem, raw_same=True):
        self.eng = eng
        self.sem = sem
        self.key = id(sem)
        self.count = 0
        self.waited = {}
        self.raw_same = raw_same


class FW:
    def __init__(self, nc, n_dma_sems=24):
        self.nc = nc
        self.es = contextlib.ExitStack()
        es = self.es
        self.pe = Eng(nc.tensor, es.enter_context(nc.semaphore("s_pe")), raw_same=False)
        self.act = Eng(nc.scalar, es.enter_context(nc.semaphore("s_act")))
        self.dve = Eng(nc.vector, es.enter_context(nc.semaphore("s_dve")))
        self.pool = Eng(nc.gpsimd, es.enter_context(nc.semaphore("s_pool")))
        self.sp = Eng(nc.sync, es.enter_context(nc.semaphore("s_sp")))
        self.engs = [self.pe, self.act, self.dve, self.pool, self.sp]
        self.dma_sems = {}
        for q in ("sp", "pool"):
            lst = [[es.enter_context(nc.semaphore(f"d_{q}{i}")), 0] for i in range(n_dma_sems)]
            self.dma_sems[q] = [lst, 0]
        self.n_inst = 0

    def _wait(self, E, sem, val):
        k = id(sem)
        if E.waited.get(k, 0) >= val:
            return
        E.eng.wait_ge(sem, val)
        E.waited[k] = val

    def _deps(self, E, reads, writes):
        for b in reads:
            for k, (sem, val) in b.w.items():
                if k == E.key and not E.raw_same:
                    continue
                self._wait(E, sem, val)
        for b in writes:
            if not b.multi:
                for k, (sem, val) in b.w.items():
                    if k == E.key:
                        continue
                    self._wait(E, sem, val)
            for k, (sem, val) in b.r.items():
                if k == E.key:
                    continue
                self._wait(E, sem, val)

    def _record(self, sem, val, reads, writes):
        k = id(sem)
        for b in reads:
            b.r[k] = (sem, val)
        for b in writes:
            if b.multi:
                b.w[k] = (sem, val)
            else:
                b.w = {k: (sem, val)}
                b.r = {}

    def _chk(self):
        import os
        lim = int(os.environ.get("KOPS", "0"))
        if lim and self.n_inst >= lim and not getattr(self, "stopped", False):
            self._barrier()
            self.stopped = True
            print("build stopped at inst", self.n_inst)

    def op(self, E, fn, reads=(), writes=()):
        self._chk()
        if getattr(self, "stopped", False):
            return None
        self._deps(E, reads, writes)
        inst = fn()
        E.count += 1
        inst.then_inc(E.sem, 1)
        self._record(E.sem, E.count, reads, writes)
        self.n_inst += 1
        return inst

    def dma(self, q, out, in_, reads=(), writes=(), **kw):
        self._chk()
        if getattr(self, "stopped", False):
            return None
        E = self.sp if q == "sp" else self.pool
        pool = self.dma_sems[q]
        lst, idx = pool
        ent = lst[idx % len(lst)]
        pool[1] = idx + 1
        sem, cnt = ent
        if cnt > 0:
            self._wait(E, sem, 16 * cnt)
        self._deps(E, reads, writes)
        inst = E.eng.dma_start(out=out, in_=in_, **kw)
        ent[1] = cnt + 1
        inst.then_inc(sem, 16)
        self._record(sem, 16 * (cnt + 1), reads, writes)
        self.n_inst += 1
        return inst

    def barrier(self):
        import os
        self.n_bar = getattr(self, "n_bar", 0) + 1
        if getattr(self, "stopped", False):
            return
        if int(os.environ.get("KSTOP", "0")) == self.n_bar:
            self._barrier()
            self.stopped = True
            print("build stopped at barrier", self.n_bar)
            return
        self._barrier()

    def _barrier(self):
        for E in self.engs:
            for q, (lst, idx) in self.dma_sems.items():
                for sem, cnt in lst:
                    if cnt > 0:
                        self._wait(E, sem, 16 * cnt)
            for X in self.engs:
                if X is E or X.count == 0 or X is self.sp:
                    continue
                self._wait(E, X.sem, X.count)

    def finish(self):
        if not getattr(self, "stopped", False):
            self._barrier()

    def close(self):
        self.es.close()


class StopBuild(Exception):
    pass


class Ring:
    def __init__(self, tiles):
        self.tiles = [(t, Buf()) for t in tiles]
        self.i = 0

    def next(self):
        t = self.tiles[self.i % len(self.tiles)]
        self.i += 1
        return t


def build_program(T, P):
    nc = bass.Bass("TRN2", target_bir_lowering=False)
    fw = FW(nc)
    pe, act, dve, pool = fw.pe, fw.act, fw.dve, fw.pool
    TT = T + TS
    NTK = P // 128

    def din(name, shape):
        return nc.dram_tensor(name, list(shape), F32, kind="ExternalInput").ap()

    def dout(name, shape):
        return nc.dram_tensor(name, list(shape), F32, kind="ExternalOutput").ap()

    def dscr(name, shape, dt):
        return nc.dram_tensor(name, list(shape), dt).ap()

    x_prompt = din("x_prompt", [T, D])
    x_sample = din("x_sample", [TS, D])
    state_conv = din("state_conv", [2, HALO, D])
    state_hgrn = din("state_hgrn", [2, 8, 128, 128])
    cache_k = din("cache_k", [2, P, 2048])
    cache_v = din("cache_v", [2, P, 2048])
    w_in_even = din("w_in_even", [2, D, 7168])
    w_out_even = din("w_out_even", [2, 2048, D])
    conv_w = din("conv_w", [2, CONV_W, D])
    conv_b = din("conv_b", [2, D])
    conv_ln_g = din("conv_ln_g", [2, D])
    conv_ln_b = din("conv_ln_b", [2, D])
    hgrn_lb_logits = din("hgrn_lb_logits", [2, D])
    hgrn_norm_g = din("hgrn_norm_g", [2, 128])
    w_in_odd = din("w_in_odd", [2, D, 8192])
    w_out_odd = din("w_out_odd", [2, 2048, D])
    lam_q1 = din("lam_q1", [2, 64])
    lam_k1 = din("lam_k1", [2, 64])
    lam_q2 = din("lam_q2", [2, 64])
    lam_k2 = din("lam_k2", [2, 64])
    subln_g = din("subln_g", [2, 128])
    post_ln_g = din("post_ln_g", [4, D])
    post_ln_b = din("post_ln_b", [4, D])
    y_prompt = dout("y_prompt", [T, D])
    y_sample = dout("y_sample", [TS, D])
    conv_prompt = dout("conv_prompt", [2, HALO, D])
    hgrn_prompt = dout("hgrn_prompt", [2, 8, 128, 128])
    k_prompt = dout("k_prompt", [2, T, 2048])
    v_prompt = dout("v_prompt", [2, T, 2048])
    conv_sample = dout("conv_sample", [2, HALO, D])
    hgrn_sample = dout("hgrn_sample", [2, 8, 128, 128])
    k_sample = dout("k_sample", [2, TS, 2048])
    v_sample = dout("v_sample", [2, TS, 2048])
    xres = [dscr("xres0", [TT, D], F32), dscr("xres1", [TT, D], F32)]
    UTp = dscr("UTp", [8, 128, HALO + T], BF16)
    UTs = dscr("UTs", [8, 128, HALO + TS], BF16)
    QTe = dscr("QTe", [8, 128, TT], F32)
    KKT = dscr("KKT", [8, 128, TT], F32)
    ZA = dscr("ZA", [TT, D], BF16)
    ZB = dscr("ZB", [TT, D], BF16)
    Gs = dscr("Gs", [TT, D], F32)
    KKs = dscr("KKs", [TT, D], F32)
    VH = dscr("VH", [TT, D], BF16)
    AT = dscr("AT", [16, 128, TT], BF16)
    QTo = dscr("QTo", [16, 128, TT], BF16)
    KTo = dscr("KTo", [16, 128, TT], BF16)
    Vo = dscr("Vo", [TT, 2048], BF16)
    Zo = dscr("Zo", [TT, 2048], BF16)
    import os
    if os.environ.get("KDUMP", "") == "1":
        dbg_AT = nc.dram_tensor("dbg_AT", [16, 128, TT], BF16, kind="ExternalOutput").ap()
    B_out = Buf(multi=True)
    Bx = [Buf(multi=True), Buf(multi=True)]
    B_UT, B_QTe, B_KKT, B_ZA, B_ZB, B_G, B_KK, B_VH, B_AT = [Buf(multi=True) for _ in range(9)]
    B_QTo, B_KTo, B_Vo, B_Zo = [Buf(multi=True) for _ in range(4)]
    B_in = Buf()
    _dma = fw.dma

    def dma_route(q, out, in_, reads=(), writes=(), **kw):
        if any(b is B_out for b in writes):
            q = "pool"
        return _dma(q, out, in_, reads=reads, writes=writes, **kw)

    fw.dma = dma_route

    token_blocks = [(t0, 512) for t0 in range(0, T, 512)] + [(T, TS)]
    token_tiles = [(t0, 128) for t0 in range(0, T, 128)] + [(T, TS)]

    top = contextlib.ExitStack()

    uid = [0]

    def sb(es, name, shape, dt):
        uid[0] += 1
        return es.enter_context(nc.sbuf_tensor(f"{name}_{uid[0]}", list(shape), dt))

    def psum(es, name, shape, dt):
        uid[0] += 1
        return es.enter_context(nc.psum_tensor(f"{name}_{uid[0]}", list(shape), dt))

    XT = sb(top, "XT", [128, 8, TT], BF16)
    B_XT = [Buf() for _ in token_tiles]
    ident_bf = sb(top, "ident_bf", [128, 128], BF16)
    ident_f = sb(top, "ident_f", [128, 128], F32)
    Mcum = sb(top, "Mcum", [128, 128], F32)
    Mext = sb(top, "Mext", [128, 2], F32)
    Mrev = sb(top, "Mrev", [128, 128], F32)
    Mmask = sb(top, "Mmask", [128, 128], F32)
    epsb = sb(top, "epsb", [128, 1], F32)
    plg = sb(top, "plg", [128, D], F32)
    plb = sb(top, "plb", [128, D], F32)
    B_const = Buf()
    B_pl = Buf()

    def xt_bufs(t0, n):
        return [B_XT[i] for i, (a, m) in enumerate(token_tiles) if a < t0 + n and a + m > t0]

    def build_consts():
        g = nc.gpsimd
        ops = [
            lambda: g.memset(ident_f[:], 1.0),
            lambda: g.affine_select(out=ident_f[:], in_=ident_f[:], pattern=[[-1, 128]], compare_op=ALU.is_equal,
                                    fill=0.0, base=0, channel_multiplier=1),
            lambda: g.tensor_copy(ident_bf[:], ident_f[:]),
            lambda: g.memset(Mmask[:], 1.0),
            lambda: g.affine_select(out=Mmask[:], in_=Mmask[:], pattern=[[1, 128]], compare_op=ALU.is_ge,
                                    fill=0.0, base=0, channel_multiplier=-1),
            lambda: g.memset(Mext[:], 1.0),
            lambda: g.affine_select(out=Mext[:, 0:1], in_=Mext[:, 0:1], pattern=[[0, 1]], compare_op=ALU.is_ge,
                                    fill=0.0, base=31, channel_multiplier=-1),
            lambda: g.tensor_scalar(out=Mcum[:], in0=Mmask[:], scalar1=Mext[:, 0:1], scalar2=None, op0=ALU.subtract),
            lambda: g.memset(Mrev[:], 1.0),
            lambda: g.affine_select(out=Mrev[:], in_=Mrev[:], pattern=[[-1, 128]], compare_op=ALU.is_gt,
                                    fill=0.0, base=0, channel_multiplier=1),
            lambda: g.memset(epsb[:], EPS),
        ]
        for f in ops:
            fw.op(pool, f, reads=[B_const], writes=[B_const])

    build_consts()

    def phase_load_x():
        with contextlib.ExitStack() as es:
            xin = Ring([sb(es, f"xin{i}", [128, D], F32) for i in range(2)])
            xbf = Ring([sb(es, f"xbf{i}", [128, D], BF16) for i in range(2)])
            pt = Ring([psum(es, f"ptx{i}", [128, 8, 128], BF16) for i in range(2)])
            for ti, (t0, n) in enumerate(token_tiles):
                src = x_prompt[t0:t0 + n, :] if t0 < T else x_sample[:, :]
                xt_, bx = xin.next()
                fw.dma("sp", xt_[0:n, :], src, writes=[bx])
                fw.dma("sp", xres[0][t0:t0 + n, :], xt_[0:n, :], reads=[bx], writes=[Bx[0]])
                xb, bxb = xbf.next()
                fw.op(act, lambda: nc.scalar.copy(out=xb[0:n, :], in_=xt_[0:n, :]), reads=[bx], writes=[bxb])
                p, bp = pt.next()
                for c in range(8):
                    fw.op(pe, lambda: nc.tensor.transpose(p[:, c, 0:n], xb[0:n, c * 128:(c + 1) * 128],
                                                          ident_bf[0:n, 0:n]),
                          reads=[bxb, B_const], writes=[bp])
                fw.op(dve, lambda: nc.vector.tensor_copy(XT[:, :, t0:t0 + n], p[:, :, 0:n]), reads=[bp],
                      writes=[B_XT[ti]])
        fw.barrier()

    def load_w(wt, bw, W, col0, ncols):
        fw.dma("pool", wt[:, :, 0:ncols], W[:, col0:col0 + ncols].rearrange("(c p) n -> p c n", p=128),
               reads=[B_in], writes=[bw])

    def mm_fm(p_ap, bp, wt, bw, wc0, t0, n):
        for c in range(8):
            fw.op(pe, lambda: nc.tensor.matmul(p_ap, wt[:, c, wc0:wc0 + 128], XT[:, c, t0:t0 + n],
                                               start=(c == 0), stop=(c == 7)),
                  reads=[bw] + xt_bufs(t0, n), writes=[bp])

    def mm_tm(p_ap, bp, wt, bw, wc0, ncols, t0, n):
        for c in range(8):
            fw.op(pe, lambda: nc.tensor.matmul(p_ap, XT[:, c, t0:t0 + n], wt[:, c, wc0:wc0 + ncols],
                                               start=(c == 0), stop=(c == 7)),
                  reads=[bw] + xt_bufs(t0, n), writes=[bp])

    def sigmoid_from(es_tile, src_ap, bsrc, dst, bdst, sign=-1.0):
        fw.op(act, lambda: nc.scalar.activation(out=dst, in_=src_ap, func=AF.Exp, scale=sign), reads=[bsrc],
              writes=[bdst])
        fw.op(dve, lambda: nc.vector.tensor_scalar_add(dst, dst, 1.0), reads=[bdst], writes=[bdst])
        fw.op(dve, lambda: nc.vector.reciprocal(dst, dst), reads=[bdst], writes=[bdst])

    def load_post_ln(l):
        fw.dma("sp", plg[:], post_ln_g[l:l + 1, :].partition_broadcast(128), writes=[B_pl])
        fw.dma("sp", plb[:], post_ln_b[l:l + 1, :].partition_broadcast(128), writes=[B_pl])

    def layer_norm_rows(es, src, bsrc, n, dst, bdst, gb, bb_, Bg, tag):
        st = es_tiles[tag + "st"]
        stats, bst = st.next()
        for hlf in range(2):
            fw.op(dve, lambda: nc.vector.bn_stats(stats[0:n, hlf, :], src[0:n, hlf * 512:(hlf + 1) * 512]),
                  reads=[bsrc], writes=[bst])
        mv, bmv = es_tiles[tag + "mv"].next()
        fw.op(dve, lambda: nc.vector.bn_aggr(mv[0:n, 0:2], stats[0:n, :, :]), reads=[bst], writes=[bmv])
        fw.op(act, lambda: nc.scalar.activation(out=mv[0:n, 2:3], in_=mv[0:n, 1:2], func=AF.Ln, bias=epsb[0:n, :]),
              reads=[bmv, B_const], writes=[bmv])
        fw.op(act, lambda: nc.scalar.activation(out=mv[0:n, 2:3], in_=mv[0:n, 2:3], func=AF.Exp, scale=-0.5),
              reads=[bmv], writes=[bmv])
        fw.op(dve, lambda: nc.vector.scalar_tensor_tensor(out=mv[0:n, 3:4], in0=mv[0:n, 0:1], scalar=-1.0,
                                                          in1=mv[0:n, 2:3], op0=ALU.mult, op1=ALU.mult),
              reads=[bmv], writes=[bmv])
        fw.op(dve, lambda: nc.vector.tensor_scalar(out=dst[0:n, :], in0=src[0:n, :], scalar1=mv[0:n, 2:3],
                                                   scalar2=mv[0:n, 3:4], op0=ALU.mult, op1=ALU.add),
              reads=[bsrc, bmv], writes=[bdst])
        fw.op(dve, lambda: nc.vector.tensor_tensor(out=dst[0:n, :], in0=dst[0:n, :], in1=gb[0:n, :], op=ALU.mult),
              reads=[bdst, Bg], writes=[bdst])
        fw.op(dve, lambda: nc.vector.tensor_tensor(out=dst[0:n, :], in0=dst[0:n, :], in1=bb_[0:n, :], op=ALU.add),
              reads=[bdst, Bg], writes=[bdst])

    es_tiles = {}

    def phase_out(l, W_out, xsrc, bxsrc, last):
        load_post_ln(l)
        xdst = xres[(l + 1) % 2]
        bxdst = Bx[(l + 1) % 2]
        with contextlib.ExitStack() as es:
            wo = sb(es, "wo", [128, 16, D], BF16)
            bwo = Buf()
            for c in range(0, 16, 4):
                fw.dma("pool", wo[:, c:c + 4, :],
                       W_out[c * 128:(c + 4) * 128, :].rearrange("(c p) n -> p c n", p=128), reads=[B_in],
                       writes=[bwo] if c == 0 else [bwo])
            at_r = Ring([sb(es, f"at{i}", [128, 16, 128], BF16) for i in range(2)])
            x_r = Ring([sb(es, f"xo{i}", [128, D], F32) for i in range(2)])
            r_r = Ring([sb(es, f"ro{i}", [128, D], F32) for i in range(2)])
            xn_r = Ring([sb(es, f"xn{i}", [128, D], F32) for i in range(2)])
            xb_r = Ring([sb(es, f"xnb{i}", [128, D], BF16) for i in range(2)])
            es_tiles["ost"] = Ring([sb(es, f"ost{i}", [128, 2, 6], F32) for i in range(2)])
            es_tiles["omv"] = Ring([sb(es, f"omv{i}", [128, 4], F32) for i in range(2)])
            py_r = Ring([psum(es, f"py{i}", [128, D], F32) for i in range(2)])
            pt_r = Ring([psum(es, f"pto{i}", [128, 8, 128], BF16) for i in range(2)])
            for ti, (t0, n) in enumerate(token_tiles):
                at_, bat = at_r.next()
                fw.dma("sp", at_[:, :, 0:n], AT[:, :, t0:t0 + n].rearrange("c p t -> p c t"), reads=[B_AT],
                       writes=[bat])
                xt_, bx = x_r.next()
                fw.dma("sp", xt_[0:n, :], xsrc[t0:t0 + n, :], reads=[bxsrc], writes=[bx])
                py, bpy = py_r.next()
                for hlf in range(2):
                    for c in range(16):
                        fw.op(pe, lambda: nc.tensor.matmul(py[0:n, hlf * 512:(hlf + 1) * 512], at_[:, c, 0:n],
                                                           wo[:, c, hlf * 512:(hlf + 1) * 512],
                                                           start=(c == 0), stop=(c == 15)),
                              reads=[bat, bwo], writes=[bpy])
                r_, br = r_r.next()
                fw.op(dve, lambda: nc.vector.scalar_tensor_tensor(out=r_[0:n, :], in0=xt_[0:n, :], scalar=ALPHA,
                                                                  in1=py[0:n, :], op0=ALU.mult, op1=ALU.add),
                      reads=[bx, bpy], writes=[br])
                xn, bxn = xn_r.next()
                layer_norm_rows(es, r_, br, n, xn, bxn, plg, plb, B_pl, "o")
                if last:
                    dst = y_prompt[t0:t0 + n, :] if t0 < T else y_sample[:, :]
                    fw.dma("sp", dst, xn[0:n, :], reads=[bxn], writes=[B_out])
                else:
                    fw.dma("sp", xdst[t0:t0 + n, :], xn[0:n, :], reads=[bxn], writes=[bxdst])
                    xb, bxb = xb_r.next()
                    fw.op(act, lambda: nc.scalar.copy(out=xb[0:n, :], in_=xn[0:n, :]), reads=[bxn], writes=[bxb])
                    p, bp = pt_r.next()
                    for c in range(8):
                        fw.op(pe, lambda: nc.tensor.transpose(p[:, c, 0:n], xb[0:n, c * 128:(c + 1) * 128],
                                                              ident_bf[0:n, 0:n]),
                              reads=[bxb, B_const], writes=[bp])
                    fw.op(act, lambda: nc.scalar.copy(out=XT[:, :, t0:t0 + n], in_=p[:, :, 0:n]), reads=[bp],
                          writes=[B_XT[ti]])
        fw.barrier()

    def even_layer(l):
        e = l // 2
        W = w_in_even[e]
        xsrc, bxsrc = xres[l % 2], Bx[l % 2]
        with contextlib.ExitStack() as esL:
            lbT = sb(esL, "lbT", [128, 8], F32)
            omlT = sb(esL, "omlT", [128, 8], F32)
            lbB = sb(esL, "lbB", [128, D], F32)
            omlB = sb(esL, "omlB", [128, D], F32)
            B_lb = Buf()
            if e == 0:
                fw.op(pool, lambda: nc.gpsimd.memset(lbT[:], 0.0), writes=[B_lb])
                fw.op(pool, lambda: nc.gpsimd.memset(omlT[:], 1.0), writes=[B_lb])
                fw.op(pool, lambda: nc.gpsimd.memset(lbB[:], 0.0), writes=[B_lb])
                fw.op(pool, lambda: nc.gpsimd.memset(omlB[:], 1.0), writes=[B_lb])
            else:
                with contextlib.ExitStack() as es:
                    l0T = sb(es, "l0T", [128, 8], F32)
                    l0B = sb(es, "l0B", [128, D], F32)
                    fw.dma("sp", lbT[:], hgrn_lb_logits[1, :].rearrange("(h k) -> k h", k=128), writes=[B_lb],
                           allow_slow_non_contiguous=True)
                    fw.dma("sp", l0T[:], hgrn_lb_logits[0, :].rearrange("(h k) -> k h", k=128), writes=[B_lb],
                           allow_slow_non_contiguous=True)
                    fw.dma("sp", lbB[:], hgrn_lb_logits[1:2, :].partition_broadcast(128), writes=[B_lb])
                    fw.dma("sp", l0B[:], hgrn_lb_logits[0:1, :].partition_broadcast(128), writes=[B_lb])
                    for (a_, b_, o_) in ((lbT, l0T, omlT), (lbB, l0B, omlB)):
                        fw.op(dve, lambda: nc.vector.tensor_tensor(out=a_[:], in0=a_[:], in1=b_[:], op=ALU.subtract),
                              reads=[B_lb], writes=[B_lb])
                        fw.op(act, lambda: nc.scalar.activation(out=a_[:], in_=a_[:], func=AF.Exp, scale=-1.0),
                              reads=[B_lb], writes=[B_lb])
                        fw.op(dve, lambda: nc.vector.tensor_scalar_add(a_[:], a_[:], 1.0), reads=[B_lb], writes=[B_lb])
                        fw.op(dve, lambda: nc.vector.reciprocal(a_[:], a_[:]), reads=[B_lb], writes=[B_lb])
                        fw.op(dve, lambda: nc.vector.tensor_scalar(out=o_[:], in0=a_[:], scalar1=-1.0, scalar2=1.0,
                                                                   op0=ALU.mult, op1=ALU.add),
                              reads=[B_lb], writes=[B_lb])
                    fw.barrier()

            with contextlib.ExitStack() as es:
                w_r = Ring([sb(es, f"w{i}", [128, 8, 512], BF16) for i in range(3)])
                ps_r = Ring([psum(es, f"pp{i}", [128, 512], F32) for i in range(6)])
                tmp_r = Ring([sb(es, f"tmp{i}", [128, 512], F32) for i in range(3)])
                tmp2_r = Ring([sb(es, f"tmpb{i}", [128, 512], F32) for i in range(3)])
                of_r = Ring([sb(es, f"of{i}", [128, 512], F32) for i in range(3)])
                of2_r = Ring([sb(es, f"ofb{i}", [128, 512], F32) for i in range(3)])
                ob_r = Ring([sb(es, f"ob{i}", [128, 512], BF16) for i in range(3)])
                halo = sb(es, "halo", [128, 8, HALO], BF16)
                bhalo = Buf()

                fw.op(pool, lambda: nc.gpsimd.memset(halo[:], 0.0), writes=[bhalo])
                fw.dma("sp", UTp[:, :, 0:HALO].rearrange("c p t -> p c t"), halo[:], reads=[bhalo], writes=[B_UT])
                with contextlib.ExitStack() as es2:
                    stt = sb(es2, "stt", [HALO, D], F32)
                    halo2 = sb(es2, "halo2", [128, 8, HALO], BF16)
                    bst, bh2 = Buf(), Buf()
                    pth = psum(es2, "pth", [128, 8, 32], F32)
                    bpth = Buf()
                    fw.dma("sp", stt[:], state_conv[e], writes=[bst])
                    for c in range(8):
                        fw.op(pe, lambda: nc.tensor.transpose(pth[:, c, 0:HALO], stt[0:HALO, c * 128:(c + 1) * 128],
                                                              ident_f[0:HALO, 0:HALO]),
                              reads=[bst, B_const], writes=[bpth])
                    fw.op(dve, lambda: nc.vector.tensor_copy(halo2[:], pth[:, :, 0:HALO]), reads=[bpth], writes=[bh2])
                    fw.dma("sp", UTs[:, :, 0:HALO].rearrange("c p t -> p c t"), halo2[:], reads=[bh2], writes=[B_UT])
                    fw.dma("sp", conv_sample[e, 0:HALO - TS, :], stt[TS:HALO, :], reads=[bst], writes=[B_out])
                    fw.barrier()

                def ut_dst(c, t0, n):
                    if t0 < T:
                        return UTp[c, :, HALO + t0:HALO + t0 + n]
                    return UTs[c, :, HALO:HALO + n]

                for j in range(2):
                    wa, bwa = w_r.next()
                    load_w(wa, bwa, W, j * 512, 512)
                    wg, bwg = w_r.next()
                    load_w(wg, bwg, W, 1024 + j * 512, 512)
                    for cc in range(4):
                        c = j * 4 + cc
                        for (t0, n) in token_blocks:
                            pa, bpa = ps_r.next()
                            pg, bpg = ps_r.next()
                            mm_fm(pa[:, 0:n], bpa, wa, bwa, cc * 128, t0, n)
                            mm_fm(pg[:, 0:n], bpg, wg, bwg, cc * 128, t0, n)
                            tm, btm = tmp_r.next()
                            sigmoid_from(None, pg[:, 0:n], bpg, tm[:, 0:n], btm)
                            ob, bob = ob_r.next()
                            fw.op(dve, lambda: nc.vector.tensor_tensor(out=ob[:, 0:n], in0=pa[:, 0:n], in1=tm[:, 0:n],
                                                                       op=ALU.mult),
                                  reads=[bpa, btm], writes=[bob])
                            fw.dma("sp", ut_dst(c, t0, n), ob[:, 0:n], reads=[bob], writes=[B_UT])
                for j in range(2):
                    wq, bwq = w_r.next()
                    load_w(wq, bwq, W, 3072 + j * 512, 512)
                    for cc in range(4):
                        c = j * 4 + cc
                        for (t0, n) in token_blocks:
                            pq, bpq = ps_r.next()
                            mm_fm(pq[:, 0:n], bpq, wq, bwq, cc * 128, t0, n)
                            tm, btm = tmp_r.next()
                            sigmoid_from(None, pq[:, 0:n], bpq, tm[:, 0:n], btm)
                            of, bof = of_r.next()
                            fw.op(dve, lambda: nc.vector.tensor_tensor(out=of[:, 0:n], in0=pq[:, 0:n], in1=tm[:, 0:n],
                                                                       op=ALU.mult),
                                  reads=[bpq, btm], writes=[bof])
                            fw.dma("sp", QTe[c, :, t0:t0 + n], of[:, 0:n], reads=[bof], writes=[B_QTe])
                for j in range(2):
                    wf, bwf = w_r.next()
                    load_w(wf, bwf, W, 4096 + j * 512, 512)
                    for cc in range(4):
                        c = j * 4 + cc
                        for (t0, n) in token_blocks:
                            pf, bpf = ps_r.next()
                            mm_fm(pf[:, 0:n], bpf, wf, bwf, cc * 128, t0, n)
                            tm, btm = tmp_r.next()
                            sigmoid_from(None, pf[:, 0:n], bpf, tm[:, 0:n], btm, sign=1.0)
                            of, bof = of_r.next()
                            fw.op(dve, lambda: nc.vector.tensor_scalar(out=of[:, 0:n], in0=tm[:, 0:n],
                                                                       scalar1=omlT[:, c:c + 1], scalar2=None,
                                                                       op0=ALU.mult),
                                  reads=[btm, B_lb], writes=[bof])
                            fw.dma("sp", KKT[c, :, t0:t0 + n], of[:, 0:n], reads=[bof], writes=[B_KKT])
                for (col0, dstT, bdst, kind) in ((2048, ZA, B_ZA, "silu"), (6144, ZB, B_ZB, "silu"),
                                                 (5120, VH, B_VH, "copy"), (4096, None, None, "f")):
                    for j in range(2):
                        w_, bw = w_r.next()
                        load_w(w_, bw, W, col0 + j * 512, 512)
                        for (t0, n) in token_tiles:
                            p_, bp = ps_r.next()
                            mm_tm(p_[0:n, :], bp, w_, bw, 0, 512, t0, n)
                            cs = slice(j * 512, (j + 1) * 512)
                            if kind == "copy":
                                ob, bob = ob_r.next()
                                fw.op(act, lambda: nc.scalar.copy(out=ob[0:n, :], in_=p_[0:n, :]), reads=[bp],
                                      writes=[bob])
                                fw.dma("sp", dstT[t0:t0 + n, cs], ob[0:n, :], reads=[bob], writes=[bdst])
                            elif kind == "silu":
                                tm, btm = tmp_r.next()
                                sigmoid_from(None, p_[0:n, :], bp, tm[0:n, :], btm)
                                ob, bob = ob_r.next()
                                fw.op(dve, lambda: nc.vector.tensor_tensor(out=ob[0:n, :], in0=p_[0:n, :],
                                                                           in1=tm[0:n, :], op=ALU.mult),
                                      reads=[bp, btm], writes=[bob])
                                fw.dma("sp", dstT[t0:t0 + n, cs], ob[0:n, :], reads=[bob], writes=[bdst])
                            else:
                                tm, btm = tmp_r.next()
                                sigmoid_from(None, p_[0:n, :], bp, tm[0:n, :], btm)
                                fw.op(dve, lambda: nc.vector.tensor_tensor(out=tm[0:n, :], in0=tm[0:n, :],
                                                                           in1=omlB[0:n, cs], op=ALU.mult),
                                      reads=[btm, B_lb], writes=[btm])
                                t2, bt2 = tmp2_r.next()
                                fw.op(dve, lambda: nc.vector.tensor_tensor(out=t2[0:n, :], in0=tm[0:n, :],
                                                                           in1=lbB[0:n, cs], op=ALU.add),
                                      reads=[btm, B_lb], writes=[bt2])
                                of, bof = of_r.next()
                                fw.op(act, lambda: nc.scalar.activation(out=of[0:n, :], in_=t2[0:n, :], func=AF.Ln),
                                      reads=[bt2], writes=[bof])
                                fw.dma("sp", Gs[t0:t0 + n, cs], of[0:n, :], reads=[bof], writes=[B_G])
                                o2, bo2 = of2_r.next()
                                fw.op(dve, lambda: nc.vector.tensor_tensor(out=o2[0:n, :], in0=omlB[0:n, cs],
                                                                           in1=tm[0:n, :], op=ALU.subtract),
                                      reads=[btm, B_lb], writes=[bo2])
                                fw.dma("sp", KKs[t0:t0 + n, cs], o2[0:n, :], reads=[bo2], writes=[B_KK])
                for j in range(2):
                    wa, bwa = w_r.next()
                    load_w(wa, bwa, W, j * 512, 512)
                    wg, bwg = w_r.next()
                    load_w(wg, bwg, W, 1024 + j * 512, 512)
                    cs = slice(j * 512, (j + 1) * 512)
                    for (t0, n) in ((T - 128, 128), (T, TS)):
                        pa, bpa = ps_r.next()
                        pg, bpg = ps_r.next()
                        mm_tm(pa[0:n, :], bpa, wa, bwa, 0, 512, t0, n)
                        mm_tm(pg[0:n, :], bpg, wg, bwg, 0, 512, t0, n)
                        tm, btm = tmp_r.next()
                        sigmoid_from(None, pg[0:n, :], bpg, tm[0:n, :], btm)
                        of, bof = of_r.next()
                        fw.op(dve, lambda: nc.vector.tensor_tensor(out=of[0:n, :], in0=pa[0:n, :], in1=tm[0:n, :],
                                                                   op=ALU.mult),
                              reads=[bpa, btm], writes=[bof])
                        if t0 < T:
                            fw.dma("sp", conv_prompt[e, :, cs], of[128 - HALO:128, :], reads=[bof], writes=[B_out])
                        else:
                            fw.dma("sp", conv_sample[e, HALO - TS:HALO, cs], of[0:TS, :], reads=[bof], writes=[B_out])
            fw.barrier()

            with contextlib.ExitStack() as es:
                diag = sb(es, "diag", [128, 8, CONV_W, 128], BF16)
                cwT = sb(es, "cwT", [128, 8, 32], F32)
                cbB = sb(es, "cbB", [128, D], F32)
                lgB = sb(es, "lgB", [128, D], F32)
                lbB2 = sb(es, "lbB2", [128, D], F32)
                B_cp = Buf()
                with contextlib.ExitStack() as es2:
                    cw = sb(es2, "cw", [CONV_W, D], F32)
                    bcw = Buf()
                    pcw = psum(es2, "pcw", [128, 8, 32], F32)
                    bpcw = Buf()
                    fw.dma("sp", cw[:], conv_w[e], writes=[bcw])
                    for c in range(8):
                        fw.op(pe, lambda: nc.tensor.transpose(pcw[:, c, 0:CONV_W], cw[0:CONV_W, c * 128:(c + 1) * 128],
                                                              ident_f[0:CONV_W, 0:CONV_W]),
                              reads=[bcw, B_const], writes=[bpcw])
                    fw.op(dve, lambda: nc.vector.tensor_copy(cwT[:, :, 0:CONV_W], pcw[:, :, 0:CONV_W]), reads=[bpcw],
                          writes=[B_cp])
                    for c in range(8):
                        for j in range(CONV_W):
                            eng = dve if (j % 2 == 0) else pool
                            ee = nc.vector if (j % 2 == 0) else nc.gpsimd
                            fw.op(eng, lambda: ee.tensor_scalar(out=diag[:, c, j, :], in0=ident_f[:],
                                                                scalar1=cwT[:, c, j:j + 1], scalar2=None,
                                                                op0=ALU.mult),
                                  reads=[B_cp, B_const], writes=[B_cp])
                    fw.dma("sp", cbB[:], conv_b[e:e + 1, :].partition_broadcast(128), writes=[B_cp])
                    fw.dma("sp", lgB[:], conv_ln_g[e:e + 1, :].partition_broadcast(128), writes=[B_cp])
                    fw.dma("sp", lbB2[:], conv_ln_b[e:e + 1, :].partition_broadcast(128), writes=[B_cp])
                    fw.barrier()
                wn_r = Ring([sb(es, f"wn{i}", [128, 8, 128 + HALO], BF16) for i in range(2)])
                za_r = Ring([sb(es, f"za{i}", [128, D], BF16) for i in range(2)])
                cb_r = Ring([sb(es, f"cb{i}", [128, D], F32) for i in range(2)])
                xn_r = Ring([sb(es, f"cxn{i}", [128, D], F32) for i in range(2)])
                ex_r = Ring([sb(es, f"cex{i}", [128, D], F32) for i in range(2)])
                oa_r = Ring([sb(es, f"coa{i}", [128, D], BF16) for i in range(2)])
                oT_r = Ring([sb(es, f"coT{i}", [128, 8, 128], BF16) for i in range(2)])
                es_tiles["cst"] = Ring([sb(es, f"cst{i}", [128, 2, 6], F32) for i in range(2)])
                es_tiles["cmv"] = Ring([sb(es, f"cmv{i}", [128, 4], F32) for i in range(2)])
                pc_r = Ring([psum(es, f"pc{i}", [128, D], F32) for i in range(2)])
                pt_r = Ring([psum(es, f"ptc{i}", [128, 8, 128], BF16) for i in range(2)])
                for (t0, n) in token_tiles:
                    wn, bwn = wn_r.next()
                    src = UTp[:, :, t0:t0 + n + HALO] if t0 < T else UTs[:, :, 0:n + HALO]
                    fw.dma("sp", wn[:, :, 0:n + HALO], src.rearrange("c p t -> p c t"), reads=[B_UT], writes=[bwn])
                    za, bza = za_r.next()
                    fw.dma("sp", za[0:n, :], ZA[t0:t0 + n, :], reads=[B_ZA], writes=[bza])
                    pc, bpc = pc_r.next()
                    for c in range(8):
                        for j in range(CONV_W):
                            fw.op(pe, lambda: nc.tensor.matmul(pc[0:n, c * 128:(c + 1) * 128], wn[:, c, j:j + n],
                                                               diag[:, c, j, :], start=(j == 0),
                                                               stop=(j == CONV_W - 1)),
                                  reads=[bwn, B_cp], writes=[bpc])
                    cb, bcb = cb_r.next()
                    fw.op(dve, lambda: nc.vector.tensor_tensor(out=cb[0:n, :], in0=pc[0:n, :], in1=cbB[0:n, :],
                                                               op=ALU.add),
                          reads=[bpc, B_cp], writes=[bcb])
                    xn, bxn = xn_r.next()
                    layer_norm_rows(es, cb, bcb, n, xn, bxn, lgB, lbB2, B_cp, "c")
                    ex, bex = ex_r.next()
                    sigmoid_from(None, xn[0:n, :], bxn, ex[0:n, :], bex)
                    fw.op(dve, lambda: nc.vector.tensor_tensor(out=ex[0:n, :], in0=ex[0:n, :], in1=xn[0:n, :],
                                                               op=ALU.mult),
                          reads=[bex, bxn], writes=[bex])
                    oa, boa = oa_r.next()
                    fw.op(dve, lambda: nc.vector.tensor_tensor(out=oa[0:n, :], in0=ex[0:n, :], in1=za[0:n, :],
                                                               op=ALU.mult),
                          reads=[bex, bza], writes=[boa])
                    p, bp = pt_r.next()
                    for c in range(8):
                        fw.op(pe, lambda: nc.tensor.transpose(p[:, c, 0:n], oa[0:n, c * 128:(c + 1) * 128],
                                                              ident_bf[0:n, 0:n]),
                              reads=[boa, B_const], writes=[bp])
                    oT, boT = oT_r.next()
                    fw.op(act, lambda: nc.scalar.copy(out=oT[:, :, 0:n], in_=p[:, :, 0:n]), reads=[bp], writes=[boT])
                    fw.dma("sp", AT[0:8, :, t0:t0 + n].rearrange("c p t -> p c t"), oT[:, :, 0:n], reads=[boT],
                           writes=[B_AT])
            fw.barrier()

            with contextlib.ExitStack() as es:
                hgB = sb(es, "hgB", [128, 128], F32)
                bhg = Buf()
                fw.dma("sp", hgB[:], hgrn_norm_g[e:e + 1, :].partition_broadcast(128), writes=[bhg])
                S = sb(es, "S", [128, 8, 128], F32)
                bS = Buf()
                Sa = sb(es, "Sa", [128, 8, 128], BF16)
                bSa = Buf()
                g_r = Ring([sb(es, f"hg{i}", [128, D], F32) for i in range(2)])
                kk_r = Ring([sb(es, f"hkk{i}", [128, D], F32) for i in range(2)])
                v_r = Ring([sb(es, f"hv{i}", [128, D], BF16) for i in range(2)])
                zb_r = Ring([sb(es, f"hzb{i}", [128, D], BF16) for i in range(2)])
                qT_r = Ring([sb(es, f"hqT{i}", [128, 8, 128], F32) for i in range(2)])
                kT_r = Ring([sb(es, f"hkT{i}", [128, 8, 128], F32) for i in range(2)])
                Eq = sb(es, "Eq", [128, 8, 128], F32)
                Ek = sb(es, "Ek", [128, 8, 128], F32)
                ER = sb(es, "ER", [128, D], F32)
                Ee = sb(es, "Ee", [128, 8, 2], F32)
                qtl = sb(es, "qtl", [128, 8, 128], BF16)
                ktl = sb(es, "ktl", [128, 8, 128], BF16)
                khat = sb(es, "khat", [128, D], BF16)
                attm = sb(es, "attm", [128, 8, 128], BF16)
                osq = sb(es, "osq", [128, D], F32)
                oss = sb(es, "oss", [128, 8], F32)
                on = sb(es, "on", [128, D], F32)
                ob = sb(es, "hob", [128, D], BF16)
                oT = sb(es, "hoT", [128, 8, 128], BF16)
                bEq, bEk, bER, bEe, bqtl, bktl, bkhat, battm, bosq, boss, bon, bob, boT = [Buf() for _ in range(13)]
                pBA = psum(es, "pBA", [128, 8, 128], F32)
                pRS = psum(es, "pRS", [128, D], F32)
                pO = psum(es, "pO", [128, D], F32)
                pE = psum(es, "pE", [128, 8, 2], F32)
                pT = psum(es, "pTh", [128, 8, 128], BF16)
                bpBA, bpRS, bpO, bpE, bpT = [Buf() for _ in range(5)]

                for seq in (0, 1):
                    tiles = [(t0, 64) for t0 in range(0, T, 64)] if seq == 0 else [(T, TS)]
                    if seq == 0:
                        fw.op(pool, lambda: nc.gpsimd.memset(S[:], 0.0), reads=[bS], writes=[bS])
                    else:
                        fw.dma("sp", S[:], state_hgrn[e].rearrange("h k v -> k h v"), reads=[bS], writes=[bS])
                    for (t0, n) in tiles:
                        g_, bg = g_r.next()
                        fw.dma("sp", g_[0:n, :], Gs[t0:t0 + n, :], reads=[B_G], writes=[bg])
                        kk_, bkk = kk_r.next()
                        fw.dma("sp", kk_[0:n, :], KKs[t0:t0 + n, :], reads=[B_KK], writes=[bkk])
                        v_, bv = v_r.next()
                        fw.dma("sp", v_[0:n, :], VH[t0:t0 + n, :], reads=[B_VH], writes=[bv])
                        zb_, bzb = zb_r.next()
                        fw.dma("sp", zb_[0:n, :], ZB[t0:t0 + n, :], reads=[B_ZB], writes=[bzb])
                        qT, bqT = qT_r.next()
                        fw.dma("sp", qT[:, :, 0:n], QTe[:, :, t0:t0 + n].rearrange("h k t -> k h t"), reads=[B_QTe],
                               writes=[bqT])
                        kT, bkT = kT_r.next()
                        fw.dma("sp", kT[:, :, 0:n], KKT[:, :, t0:t0 + n].rearrange("h k t -> k h t"), reads=[B_KKT],
                               writes=[bkT])
                        for h in range(8):
                            hs = slice(h * 128, (h + 1) * 128)
                            fw.op(pe, lambda: nc.tensor.matmul(pBA[:, h, 0:n], g_[0:n, hs], Mcum[0:n, 0:n],
                                                               start=True, stop=True),
                                  reads=[bg, B_const], writes=[bpBA])
                            fw.op(pe, lambda: nc.tensor.matmul(pE[:, h, :], g_[0:n, hs], Mext[0:n, :], start=True,
                                                               stop=True),
                                  reads=[bg, B_const], writes=[bpE])
                        for hlf in range(2):
                            cs = slice(hlf * 512, (hlf + 1) * 512)
                            fw.op(pe, lambda: nc.tensor.matmul(pRS[0:n, cs], Mrev[0:n, 0:n], g_[0:n, cs], start=True,
                                                               stop=True),
                                  reads=[bg, B_const], writes=[bpRS])
                        fw.op(act, lambda: nc.scalar.activation(out=Eq[:, :, 0:n], in_=pBA[:, :, 0:n], func=AF.Exp),
                              reads=[bpBA], writes=[bEq])
                        fw.op(act, lambda: nc.scalar.activation(out=Ek[:, :, 0:n], in_=pBA[:, :, 0:n], func=AF.Exp,
                                                                scale=-1.0),
                              reads=[bpBA], writes=[bEk])
                        fw.op(act, lambda: nc.scalar.activation(out=ER[0:n, :], in_=pRS[0:n, :], func=AF.Exp),
                              reads=[bpRS], writes=[bER])
                        fw.op(act, lambda: nc.scalar.activation(out=Ee[:], in_=pE[:], func=AF.Exp), reads=[bpE],
                              writes=[bEe])
                        fw.op(dve, lambda: nc.vector.tensor_tensor(out=qtl[:, :, 0:n], in0=qT[:, :, 0:n],
                                                                   in1=Eq[:, :, 0:n], op=ALU.mult),
                              reads=[bqT, bEq], writes=[bqtl])
                        fw.op(dve, lambda: nc.vector.tensor_tensor(out=ktl[:, :, 0:n], in0=kT[:, :, 0:n],
                                                                   in1=Ek[:, :, 0:n], op=ALU.mult),
                              reads=[bkT, bEk], writes=[bktl])
                        fw.op(dve, lambda: nc.vector.tensor_tensor(out=khat[0:n, :], in0=kk_[0:n, :], in1=ER[0:n, :],
                                                                   op=ALU.mult),
                              reads=[bkk, bER], writes=[bkhat])
                        fw.op(dve, lambda: nc.vector.tensor_tensor(out=Sa[:], in0=S[:],
                                                                   in1=Ee[:, :, 0:1].broadcast_to([128, 8, 128]),
                                                                   op=ALU.mult),
                              reads=[bS, bEe], writes=[bSa])
                        for h in range(8):
                            fw.op(pe, lambda: nc.tensor.matmul(pBA[0:n, h, 0:n], ktl[:, h, 0:n], qtl[:, h, 0:n],
                                                               start=True, stop=True),
                                  reads=[bktl, bqtl, bEq, bEk], writes=[bpBA])
                        fw.op(dve, lambda: nc.vector.tensor_tensor(
                            out=attm[0:n, :, 0:n], in0=pBA[0:n, :, 0:n],
                            in1=Mmask[0:n, 0:n].unsqueeze(1).broadcast_to([n, 8, n]), op=ALU.mult),
                            reads=[bpBA, B_const], writes=[battm])
                        for h in range(8):
                            hs = slice(h * 128, (h + 1) * 128)
                            fw.op(pe, lambda: nc.tensor.matmul(pO[0:n, hs], qtl[:, h, 0:n], Sa[:, h, :], start=True,
                                                               stop=False),
                                  reads=[bqtl, bSa], writes=[bpO])
                            fw.op(pe, lambda: nc.tensor.matmul(pO[0:n, hs], attm[0:n, h, 0:n], v_[0:n, hs],
                                                               start=False, stop=True),
                                  reads=[battm, bv], writes=[bpO])
                        for h in range(8):
                            hs = slice(h * 128, (h + 1) * 128)
                            fw.op(pe, lambda: nc.tensor.matmul(pRS[:, hs], khat[0:n, hs], v_[0:n, hs], start=True,
                                                               stop=True),
                                  reads=[bkhat, bv, bER], writes=[bpRS])
                        fw.op(dve, lambda: nc.vector.tensor_tensor(out=S[:], in0=S[:],
                                                                   in1=Ee[:, :, 1:2].broadcast_to([128, 8, 128]),
                                                                   op=ALU.mult),
                              reads=[bS, bEe, bSa], writes=[bS])
                        fw.op(dve, lambda: nc.vector.tensor_tensor(out=S[:], in0=S[:],
                                                                   in1=pRS[:].rearrange("k (h v) -> k h v", h=8),
                                                                   op=ALU.add),
                              reads=[bS, bpRS], writes=[bS])
                        fw.op(act, lambda: nc.scalar.activation(out=osq[0:n, :], in_=pO[0:n, :], func=AF.Square),
                              reads=[bpO], writes=[bosq])
                        fw.op(dve, lambda: nc.vector.tensor_reduce(out=oss[0:n, :],
                                                                   in_=osq[0:n, :].rearrange("t (h v) -> t h v", h=8),
                                                                   axis=AX.X, op=ALU.add),
                              reads=[bosq], writes=[boss])
                        fw.op(act, lambda: nc.scalar.activation(out=oss[0:n, :], in_=oss[0:n, :], func=AF.Ln,
                                                                scale=1.0 / 128, bias=epsb[0:n, :]),
                              reads=[boss, B_const], writes=[boss])
                        fw.op(act, lambda: nc.scalar.activation(out=oss[0:n, :], in_=oss[0:n, :], func=AF.Exp,
                                                                scale=-0.5),
                              reads=[boss], writes=[boss])
                        fw.op(dve, lambda: nc.vector.tensor_tensor(
                            out=on[0:n, :].rearrange("t (h v) -> t h v", h=8),
                            in0=pO[0:n, :].rearrange("t (h v) -> t h v", h=8),
                            in1=oss[0:n, :].unsqueeze(2).broadcast_to([n, 8, 128]), op=ALU.mult),
                            reads=[bpO, boss], writes=[bon])
                        fw.op(dve, lambda: nc.vector.tensor_tensor(
                            out=on[0:n, :].rearrange("t (h v) -> t h v", h=8),
                            in0=on[0:n, :].rearrange("t (h v) -> t h v", h=8),
                            in1=hgB[0:n, :].unsqueeze(1).broadcast_to([n, 8, 128]), op=ALU.mult),
                            reads=[bon, bhg], writes=[bon])
                        fw.op(dve, lambda: nc.vector.tensor_tensor(out=ob[0:n, :], in0=on[0:n, :], in1=zb_[0:n, :],
                                                                   op=ALU.mult),
                              reads=[bon, bzb], writes=[bob])
                        for h in range(8):
                            fw.op(pe, lambda: nc.tensor.transpose(pT[:, h, 0:n], ob[0:n, h * 128:(h + 1) * 128],
                                                                  ident_bf[0:n, 0:n]),
                                  reads=[bob, B_const], writes=[bpT])
                        fw.op(act, lambda: nc.scalar.copy(out=oT[:, :, 0:n], in_=pT[:, :, 0:n]), reads=[bpT],
                              writes=[boT])
                        fw.dma("sp", AT[8:16, :, t0:t0 + n].rearrange("c p t -> p c t"), oT[:, :, 0:n], reads=[boT],
                               writes=[B_AT])
                    dstS = hgrn_prompt[e] if seq == 0 else hgrn_sample[e]
                    fw.dma("sp", dstS.rearrange("h k v -> k h v"), S[:], reads=[bS], writes=[B_out])
            fw.barrier()
        phase_out(l, w_out_even[e], xsrc, bxsrc, last=False)

    def odd_layer(l):
        o = l // 2
        W = w_in_odd[o]
        lam_init = 0.8 - 0.6 * math.exp(-0.3 * l)
        xsrc, bxsrc = xres[l % 2], Bx[l % 2]
        with contextlib.ExitStack() as esL:
            nlam = sb(esL, "nlam", [128, 1], F32)
            sgB = sb(esL, "sgB", [128, 128], F32)
            B_lm = Buf()
            with contextlib.ExitStack() as es:
                lt = sb(es, "lt", [128, 4, 64], F32)
                ls = sb(es, "ls", [128, 2], F32)
                for i, src in enumerate((lam_q1, lam_k1, lam_q2, lam_k2)):
                    fw.dma("sp", lt[:, i, :], src[o:o + 1, :].partition_broadcast(128), writes=[B_lm])
                fw.op(dve, lambda: nc.vector.tensor_tensor(out=lt[:, 0, :], in0=lt[:, 0, :], in1=lt[:, 1, :],
                                                           op=ALU.mult), reads=[B_lm], writes=[B_lm])
                fw.op(dve, lambda: nc.vector.tensor_tensor(out=lt[:, 2, :], in0=lt[:, 2, :], in1=lt[:, 3, :],
                                                           op=ALU.mult), reads=[B_lm], writes=[B_lm])
                fw.op(dve, lambda: nc.vector.tensor_reduce(out=ls[:, 0:1], in_=lt[:, 0, :], axis=AX.X, op=ALU.add),
                      reads=[B_lm], writes=[B_lm])
                fw.op(dve, lambda: nc.vector.tensor_reduce(out=ls[:, 1:2], in_=lt[:, 2, :], axis=AX.X, op=ALU.add),
                      reads=[B_lm], writes=[B_lm])
                fw.op(act, lambda: nc.scalar.activation(out=ls[:], in_=ls[:], func=AF.Exp), reads=[B_lm],
                      writes=[B_lm])
                fw.op(dve, lambda: nc.vector.tensor_tensor(out=nlam[:], in0=ls[:, 1:2], in1=ls[:, 0:1],
                                                           op=ALU.subtract), reads=[B_lm], writes=[B_lm])
                fw.op(dve, lambda: nc.vector.tensor_scalar_add(nlam[:], nlam[:], -lam_init), reads=[B_lm],
                      writes=[B_lm])
                fw.dma("sp", sgB[:], subln_g[o:o + 1, :].partition_broadcast(128), writes=[B_lm])
                fw.op(dve, lambda: nc.vector.tensor_scalar(out=sgB[:], in0=sgB[:], scalar1=1.0 - lam_init,
                                                           scalar2=None, op0=ALU.mult), reads=[B_lm], writes=[B_lm])
                fw.barrier()

            with contextlib.ExitStack() as es:
                w_r = Ring([sb(es, f"w{i}", [128, 8, 512], BF16) for i in range(3)])
                ps_r = Ring([psum(es, f"pp{i}", [128, 512], F32) for i in range(6)])
                tmp_r = Ring([sb(es, f"tmp{i}", [128, 512], F32) for i in range(3)])
                of_r = Ring([sb(es, f"of{i}", [128, 512], F32) for i in range(3)])
                ob_r = Ring([sb(es, f"ob{i}", [128, 512], BF16) for i in range(3)])
                import os
                KSKIP = os.environ.get("KSKIP", "")
                for (col0, dstT, bdst, sc) in ((0, QTo, B_QTo, 0.125), (2048, KTo, B_KTo, 1.0)):
                    if "A" in KSKIP:
                        continue
                    for j in range(4):
                        w_, bw = w_r.next()
                        load_w(w_, bw, W, col0 + j * 512, 512)
                        for cc in range(4):
                            hh = j * 4 + cc
                            for (t0, n) in token_blocks:
                                p_, bp = ps_r.next()
                                mm_fm(p_[:, 0:n], bp, w_, bw, cc * 128, t0, n)
                                ob, bob = ob_r.next()
                                fw.op(act, lambda: nc.scalar.mul(out=ob[:, 0:n], in_=p_[:, 0:n], mul=sc), reads=[bp],
                                      writes=[bob])
                                fw.dma("sp", dstT[hh, :, t0:t0 + n], ob[:, 0:n], reads=[bob], writes=[bdst])
                for (col0, kind) in ((2048, "k"), (4096, "v"), (6144, "z")):
                    if kind in KSKIP:
                        continue
                    for j in range(4):
                        w_, bw = w_r.next()
                        load_w(w_, bw, W, col0 + j * 512, 512)
                        cs = slice(j * 512, (j + 1) * 512)
                        for (t0, n) in token_tiles:
                            p_, bp = ps_r.next()
                            mm_tm(p_[0:n, :], bp, w_, bw, 0, 512, t0, n)
                            if kind in ("k", "v"):
                                of, bof = of_r.next()
                                fw.op(act, lambda: nc.scalar.copy(out=of[0:n, :], in_=p_[0:n, :]), reads=[bp],
                                      writes=[bof])
                                if kind == "k":
                                    dst = k_prompt[o, t0:t0 + n, cs] if t0 < T else k_sample[o, :, cs]
                                else:
                                    dst = v_prompt[o, t0:t0 + n, cs] if t0 < T else v_sample[o, :, cs]
                                fw.dma("pool", dst, of[0:n, :], reads=[bof], writes=[B_out])
                                if kind == "v":
                                    ob, bob = ob_r.next()
                                    fw.op(dve, lambda: nc.vector.tensor_copy(ob[0:n, :], of[0:n, :]), reads=[bof],
                                          writes=[bob])
                                    fw.dma("sp", Vo[t0:t0 + n, cs], ob[0:n, :], reads=[bob], writes=[B_Vo])
                            else:
                                tm, btm = tmp_r.next()
                                sigmoid_from(None, p_[0:n, :], bp, tm[0:n, :], btm)
                                ob, bob = ob_r.next()
                                fw.op(dve, lambda: nc.vector.tensor_tensor(out=ob[0:n, :], in0=p_[0:n, :],
                                                                           in1=tm[0:n, :], op=ALU.mult),
                                      reads=[bp, btm], writes=[bob])
                                fw.dma("sp", Zo[t0:t0 + n, cs], ob[0:n, :], reads=[bob], writes=[B_Zo])
            fw.barrier()

            with contextlib.ExitStack() as es:
                NKT_P = T // 128
                NKT_S = NTK + 1
                NKT_MAX = max(NKT_P, NKT_S)
                KLEN = max(T, P + TS)
                kt_r = Ring([sb(es, f"kt{i}", [128, KLEN], BF16) for i in range(2)])
                va_r = Ring([sb(es, f"va{i}", [128, NKT_MAX, 130], BF16) for i in range(2)])
                qt_r = Ring([sb(es, f"qt{i}", [128, T], BF16) for i in range(2)])
                z_r = Ring([sb(es, f"zz{i}", [128, NKT_P, 128], BF16) for i in range(2)])
                aT_r = Ring([sb(es, f"aT{i}", [128, T], BF16) for i in range(2)])
                pt_r = Ring([sb(es, f"pt{i}", [128, 1024], BF16) for i in range(3)])
                accs_r = Ring([sb(es, f"accs{i}", [128, 3, 512], F32) for i in range(2)])
                kc_r = Ring([sb(es, f"kc{i}", [128, 8, 128], BF16) for i in range(2)])
                sm_r = Ring([sb(es, f"sm{i}", [128, 8], F32) for i in range(4)])
                o1_r = Ring([sb(es, f"o1{i}", [128, 128], F32) for i in range(2)])
                o2_r = Ring([sb(es, f"o2{i}", [128, 128], F32) for i in range(2)])
                ab_r = Ring([sb(es, f"ab{i}", [128, 128], BF16) for i in range(8)])
                ps_r = Ring([psum(es, f"psc{i}", [128, 1024], F32) for i in range(2)])
                pacc = psum(es, "pacc", [128, 3, 512], F32)
                bpacc = [Buf() for _ in range(8)]
                ptr_r = Ring([psum(es, "ptr_t", [128, 1024], BF16)])
                zl = sb(es, "zl", [128, 128], BF16)
                zr = sb(es, "zr", [128, 387], BF16)
                bz = Buf()
                fw.op(pool, lambda: nc.gpsimd.memset(zl[:], 0.0), writes=[bz])
                fw.op(pool, lambda: nc.gpsimd.memset(zr[:], 0.0), writes=[bz])

                def open_banks(banks):
                    for b in banks:
                        fw.op(pe, lambda: nc.tensor.matmul(pacc[:, b, 0:387], zl[:], zr[:], start=True, stop=False),
                              reads=[bz], writes=[bpacc[i] for i in range(8) if i // 3 == b])

                for (va, bva) in va_r.tiles:
                    fw.op(pool, lambda: nc.gpsimd.memset(va[:, :, 128:129], 1.0), writes=[bva])

                def acc_ap(i, nq):
                    return pacc[0:nq, i // 3, (i % 3) * 129:(i % 3) * 129 + 129]

                deferred = []

                def flush_deferred():
                    while deferred:
                        deferred.pop(0)()

                def finalize(sub_list, zt, bzt, aT, baT, zrow0):
                    accs, baccs = accs_r.next()
                    for b in range(3):
                        fw.op(dve, lambda: nc.vector.tensor_copy(accs[:, b, :], pacc[:, b, :]),
                              reads=[bpacc[i] for i in range(8) if i // 3 == b], writes=[baccs])
                    tails = []
                    for (i0, nq, q0, zi) in sub_list:
                        def ac(i, c0, c1):
                            return accs[0:nq, i // 3, (i % 3) * 129 + c0:(i % 3) * 129 + c1]
                        sm, bsm = sm_r.next()
                        fw.op(dve, lambda: nc.vector.reciprocal(sm[0:nq, 0:1], ac(i0, 128, 129)), reads=[baccs],
                              writes=[bsm])
                        fw.op(dve, lambda: nc.vector.reciprocal(sm[0:nq, 1:2], ac(i0 + 1, 128, 129)), reads=[baccs],
                              writes=[bsm])
                        fw.op(dve, lambda: nc.vector.tensor_scalar(out=sm[0:nq, 1:2], in0=sm[0:nq, 1:2],
                                                                   scalar1=nlam[0:nq, 0:1], scalar2=None,
                                                                   op0=ALU.mult), reads=[bsm, B_lm], writes=[bsm])
                        o1, bo1 = o1_r.next()
                        fw.op(dve, lambda: nc.vector.tensor_scalar(out=o1[0:nq, :], in0=ac(i0, 0, 128),
                                                                   scalar1=sm[0:nq, 0:1], scalar2=None, op0=ALU.mult),
                              reads=[baccs, bsm], writes=[bo1])
                        fw.op(dve, lambda: nc.vector.scalar_tensor_tensor(out=o1[0:nq, :], in0=ac(i0 + 1, 0, 128),
                                                                          scalar=sm[0:nq, 1:2], in1=o1[0:nq, :],
                                                                          op0=ALU.mult, op1=ALU.add),
                              reads=[baccs, bsm, bo1], writes=[bo1])
                        o2, bo2 = o2_r.next()
                        fw.op(dve, lambda: nc.vector.tensor_tensor(out=o2[0:nq, :], in0=o1[0:nq, :], in1=o1[0:nq, :],
                                                                   op=ALU.mult), reads=[bo1], writes=[bo2])
                        fw.op(dve, lambda: nc.vector.tensor_reduce(out=sm[0:nq, 2:3], in_=o2[0:nq, :], axis=AX.X,
                                                                   op=ALU.add), reads=[bo2], writes=[bsm])
                        fw.op(act, lambda: nc.scalar.activation(out=sm[0:nq, 2:3], in_=sm[0:nq, 2:3], func=AF.Ln,
                                                                scale=1.0 / 128, bias=epsb[0:nq, :]),
                              reads=[bsm, B_const], writes=[bsm])
                        fw.op(act, lambda: nc.scalar.activation(out=sm[0:nq, 2:3], in_=sm[0:nq, 2:3], func=AF.Exp,
                                                                scale=-0.5), reads=[bsm], writes=[bsm])
                        fw.op(dve, lambda: nc.vector.scalar_tensor_tensor(out=o2[0:nq, :], in0=o1[0:nq, :],
                                                                          scalar=sm[0:nq, 2:3], in1=sgB[0:nq, :],
                                                                          op0=ALU.mult, op1=ALU.mult),
                              reads=[bo1, bsm, B_lm, bo2], writes=[bo2])
                        ab, bab = ab_r.next()
                        fw.op(dve, lambda: nc.vector.tensor_tensor(out=ab[0:nq, :], in0=o2[0:nq, :],
                                                                   in1=zt[0:nq, zi, :], op=ALU.mult),
                              reads=[bo2, bzt], writes=[bab])
                        tails.append((ab, bab, nq, q0))

                    def tail(tails=tails, aT=aT, baT=baT):
                        ptr, bptr = ptr_r.next()
                        for i, (ab, bab, nq, q0) in enumerate(tails):
                            fw.op(pe, lambda: nc.tensor.transpose(ptr[:, i * 128:i * 128 + nq], ab[0:nq, :],
                                                                  ident_bf[0:nq, 0:nq]),
                                  reads=[bab, B_const], writes=[bptr])
                        q00 = tails[0][3]
                        wid = (len(tails) - 1) * 128 + tails[-1][2]
                        fw.op(dve, lambda: nc.vector.tensor_copy(aT[:, q00:q00 + wid], ptr[:, 0:wid]), reads=[bptr],
                              writes=[baT])
                    deferred.append(tail)

                for h in range(NH):
                    hs = slice(h * 128, (h + 1) * 128)
                    kt, bkt = kt_r.next()
                    fw.dma("sp", kt[:, 0:T], KTo[h, :, 0:T], reads=[B_KTo], writes=[bkt])
                    qt, bqt = qt_r.next()
                    fw.dma("sp", qt[:, 0:T], QTo[h, :, 0:T], reads=[B_QTo], writes=[bqt])
                    va, bva = va_r.next()
                    fw.dma("sp", va[:, 0:NKT_P, 0:128], Vo[0:T, hs].rearrange("(m p) d -> p m d", p=128),
                           reads=[B_Vo], writes=[bva])
                    zt, bzt = z_r.next()
                    fw.dma("sp", zt[:, 0:NKT_P, :], Zo[0:T, hs].rearrange("(m p) d -> p m d", p=128), reads=[B_Zo],
                           writes=[bzt])
                    aT, baT = aT_r.next()
                    its = [(J, m) for J in range(T // 512) for m in range(4 * J + 4)]
                    if "P" in os.environ.get("KSKIP2", ""):
                        its = []
                    st = {}

                    def emit_qk(k):
                        J, m = its[k]
                        q0 = J * 512
                        i_d = m - 4 * J
                        qlo = 0 if i_d < 0 else i_d * 128
                        psc, bpsc = ps_r.next()
                        for c in range(2):
                            fw.op(pe, lambda: nc.tensor.matmul(psc[:, c * 512 + qlo:(c + 1) * 512],
                                                               kt[c * 64:(c + 1) * 64, m * 128:(m + 1) * 128],
                                                               qt[c * 64:(c + 1) * 64, q0 + qlo:q0 + 512],
                                                               start=True, stop=True),
                                  reads=[bkt, bqt], writes=[bpsc])
                        st[k] = (psc, bpsc)

                    def emit_exp_pv(k):
                        J, m = its[k]
                        q0 = J * 512
                        i_d = m - 4 * J
                        qlo = 0 if i_d < 0 else i_d * 128
                        psc, bpsc = st.pop(k)
                        pt, bpt = pt_r.next()
                        psv = psc[:].rearrange("k (c q) -> k c q", c=2)
                        ptv = pt[:].rearrange("k (c q) -> k c q", c=2)
                        fw.op(act, lambda: nc.scalar.activation(out=ptv[:, :, qlo:512], in_=psv[:, :, qlo:512],
                                                                func=AF.Exp), reads=[bpsc], writes=[bpt])
                        if i_d >= 0:
                            fw.op(pool, lambda: nc.gpsimd.memset(ptv[64:128, :, i_d * 128:i_d * 128 + 64], 0.0),
                                  reads=[bpt], writes=[bpt])
                        if m == 0:
                            open_banks((0, 1, 2))
                        for s_ in range(4):
                            if i_d > s_:
                                continue
                            last_m = 4 * J + s_
                            for c in range(2):
                                ai = s_ * 2 + c
                                fw.op(pe, lambda: nc.tensor.matmul(acc_ap(ai, 128),
                                                                   pt[:, c * 512 + s_ * 128:c * 512 + (s_ + 1) * 128],
                                                                   va[:, m, 0:129], start=False, stop=(m == last_m)),
                                      reads=[bpt, bva], writes=[bpacc[ai]])

                    if its:
                        emit_qk(0)
                    for k in range(len(its)):
                        if k + 1 < len(its):
                            emit_qk(k + 1)
                        J, m = its[k]
                        if m == 2:
                            flush_deferred()
                        emit_exp_pv(k)
                        if m == 4 * J + 3:
                            flush_deferred()
                            finalize([(s_ * 2, 128, J * 512 + s_ * 128, J * 4 + s_) for s_ in range(4)], zt, bzt, aT,
                                     baT, 0)
                    flush_deferred()
                    fw.dma("sp", AT[h, :, 0:T], aT[:, 0:T], reads=[baT], writes=[B_AT])

                    kt, bkt = kt_r.next()
                    va, bva = va_r.next()
                    fw.dma("pool", va[:, 0:NTK, 0:128], cache_v[o, :, hs].rearrange("(m p) d -> p m d", p=128),
                           reads=[B_in], writes=[bva])
                    fw.dma("sp", va[0:TS, NTK, 0:128], Vo[T:TT, hs], reads=[B_Vo], writes=[bva])
                    fw.dma("sp", kt[:, P:P + TS], KTo[h, :, T:TT], reads=[B_KTo], writes=[bkt])
                    for m0 in range(0, NTK, 4):
                        kc, bkc = kc_r.next()
                        nm = min(4, NTK - m0)
                        fw.dma("pool", kc[:, 0:nm, :],
                               cache_k[o, m0 * 128:(m0 + nm) * 128, hs].rearrange("(m p) d -> p m d", p=128),
                               reads=[B_in], writes=[bkc])
                        ptr, bptr = ptr_r.next()
                        for mm in range(nm):
                            fw.op(pe, lambda: nc.tensor.transpose(ptr[:, mm * 128:(mm + 1) * 128], kc[:, mm, :],
                                                                  ident_bf[:]),
                                  reads=[bkc, B_const], writes=[bptr])
                        fw.op(dve, lambda: nc.vector.tensor_copy(kt[:, m0 * 128:(m0 + nm) * 128], ptr[:, 0:nm * 128]),
                              reads=[bptr], writes=[bkt])
                    qt, bqt = qt_r.next()
                    fw.dma("sp", qt[:, 0:TS], QTo[h, :, T:TT], reads=[B_QTo], writes=[bqt])
                    zt, bzt = z_r.next()
                    fw.dma("sp", zt[0:TS, 0, :], Zo[T:TT, hs], reads=[B_Zo], writes=[bzt])
                    aT, baT = aT_r.next()
                    open_banks((0,))
                    GK = 32
                    for m0 in range(0, NKT_S if "S" not in os.environ.get("KSKIP2", "") else 0, GK):
                        nm = min(GK, NKT_S - m0)
                        psc, bpsc = ps_r.next()
                        pt, bpt = pt_r.next()
                        psv = psc[:].rearrange("k (c q) -> k c q", c=2)
                        ptv = pt[:].rearrange("k (c q) -> k c q", c=2)
                        for mm in range(nm):
                            m = m0 + mm
                            nk = 128 if m < NTK else TS
                            for c in range(2):
                                fw.op(pe, lambda: nc.tensor.matmul(psv[0:nk, c, mm * 16:(mm + 1) * 16],
                                                                   kt[c * 64:(c + 1) * 64, m * 128:m * 128 + nk],
                                                                   qt[c * 64:(c + 1) * 64, 0:TS], start=True,
                                                                   stop=True),
                                      reads=[bkt, bqt], writes=[bpsc])
                        nfull = nm if m0 + nm < NKT_S else nm - 1
                        if nfull > 0:
                            fw.op(act, lambda: nc.scalar.activation(out=ptv[:, :, 0:nfull * 16],
                                                                    in_=psv[:, :, 0:nfull * 16], func=AF.Exp),
                                  reads=[bpsc], writes=[bpt])
                        if nfull < nm:
                            fw.op(act, lambda: nc.scalar.activation(out=ptv[0:TS, :, nfull * 16:nm * 16],
                                                                    in_=psv[0:TS, :, nfull * 16:nm * 16],
                                                                    func=AF.Exp),
                                  reads=[bpsc], writes=[bpt])
                        for mm in range(nm):
                            m = m0 + mm
                            nk = 128 if m < NTK else TS
                            for c in range(2):
                                fw.op(pe, lambda: nc.tensor.matmul(acc_ap(c, TS), ptv[0:nk, c, mm * 16:(mm + 1) * 16],
                                                                   va[0:nk, m, 0:129], start=False,
                                                                   stop=(m == NKT_S - 1)),
                                      reads=[bpt, bva], writes=[bpacc[c]])
                    finalize([(0, TS, 0, 0)], zt, bzt, aT, baT, 0)
                    flush_deferred()
                    fw.dma("sp", AT[h, :, T:TT], aT[:, 0:TS], reads=[baT], writes=[B_AT])
            if os.environ.get("KDUMP", "") == "1" and l == 1:
                fw.dma("pool", dbg_AT, AT, reads=[B_AT], writes=[B_out])
            fw.barrier()
        phase_out(l, w_out_odd[o], xsrc, bxsrc, last=(l == DEPTH - 1))

    import os
    if os.environ.get("KTESTOUT", "") == "1":
        with contextlib.ExitStack() as es:
            tt_ = sb(es, "tt_", [128, 2048], F32)
            btt = Buf()
            fw.op(pool, lambda: nc.gpsimd.memset(tt_[:], 7.0), writes=[btt])
            for o in range(2):
                for t0 in range(0, T, 128):
                    fw.dma("sp", k_prompt[o, t0:t0 + 128, :], tt_[:], reads=[btt], writes=[B_out])
                    fw.dma("sp", v_prompt[o, t0:t0 + 128, :], tt_[:], reads=[btt], writes=[B_out])
                fw.dma("sp", k_sample[o, :, :], tt_[0:TS, :], reads=[btt], writes=[B_out])
                fw.dma("sp", v_sample[o, :, :], tt_[0:TS, :], reads=[btt], writes=[B_out])
                fw.dma("sp", conv_prompt[o, :, :], tt_[0:HALO, 0:D], reads=[btt], writes=[B_out])
                fw.dma("sp", conv_sample[o, :, :], tt_[0:HALO, 0:D], reads=[btt], writes=[B_out])
                fw.dma("sp", hgrn_prompt[o].rearrange("h k v -> k h v"), tt_[:, 0:1024].rearrange("k (h v) -> k h v", h=8), reads=[btt], writes=[B_out])
                fw.dma("sp", hgrn_sample[o].rearrange("h k v -> k h v"), tt_[:, 0:1024].rearrange("k (h v) -> k h v", h=8), reads=[btt], writes=[B_out])
            for t0 in range(0, T, 128):
                fw.dma("sp", y_prompt[t0:t0 + 128, :], tt_[:, 0:D], reads=[btt], writes=[B_out])
            fw.dma("sp", y_sample[:, :], tt_[0:TS, 0:D], reads=[btt], writes=[B_out])
        fw.stopped = True
        fw._barrier()
    try:
        phase_load_x()
        for l in range(DEPTH):
            if l % 2 == 0:
                even_layer(l)
            else:
                odd_layer(l)
    except StopBuild:
        print("build stopped at barrier", fw.n_bar)
    fw.finish()
    top.close()
    fw.close()
    return nc, fw


_CACHE = {}


def _get_program(T, P):
    key = (T, P)
    if key not in _CACHE:
        _CACHE[key] = build_program(T, P)
    return _CACHE[key]


def kernel(**inputs):
    x_prompt = np.asarray(inputs["x_prompt"], dtype=np.float32)
    BATCH, T, _ = x_prompt.shape
    x_sample = np.asarray(inputs["x_sample"], dtype=np.float32)
    NS = x_sample.shape[0]
    P = inputs["cache_k"].shape[2]
    n_cores = 8
    nc, fw = _get_program(T, P)
    shared = ["w_in_even", "w_out_even", "conv_w", "conv_b", "conv_ln_g", "conv_ln_b", "hgrn_lb_logits",
              "hgrn_norm_g", "w_in_odd", "w_out_odd", "lam_q1", "lam_k1", "lam_q2", "lam_k2", "subln_g",
              "post_ln_g", "post_ln_b"]
    sh = {k: np.ascontiguousarray(np.asarray(inputs[k], dtype=np.float32)) for k in shared}
    in_maps = []
    for c in range(n_cores):
        bp = c % BATCH
        bs = c % NS
        m = dict(sh)
        m["x_prompt"] = np.ascontiguousarray(x_prompt[bp])
        m["x_sample"] = np.ascontiguousarray(x_sample[bs])
        m["state_conv"] = np.ascontiguousarray(np.asarray(inputs["state_conv"])[:, bs])
        m["state_hgrn"] = np.ascontiguousarray(np.asarray(inputs["state_hgrn"])[:, bs])
        m["cache_k"] = np.ascontiguousarray(np.asarray(inputs["cache_k"])[:, bs]).reshape(2, P, 2048)
        m["cache_v"] = np.ascontiguousarray(np.asarray(inputs["cache_v"])[:, bs]).reshape(2, P, 2048)
        in_maps.append(m)
    res = run_bass_kernel_spmd(nc, in_maps, core_ids=list(range(n_cores)))
    R = res.results

    def stack_p(name, shp):
        return np.stack([np.asarray(R[b][name]).reshape(shp) for b in range(BATCH)], axis=0)

    def stack_s(name, shp):
        return np.stack([np.asarray(R[b][name]).reshape(shp) for b in range(NS)], axis=0)

    y_prompt = stack_p("y_prompt", (T, D))
    y_sample = stack_s("y_sample", (TS, D))
    conv_prompt = np.moveaxis(stack_p("conv_prompt", (2, HALO, D)), 0, 1)
    hgrn_prompt = np.moveaxis(stack_p("hgrn_prompt", (2, 8, 128, 128)), 0, 1)
    k_prompt = np.moveaxis(stack_p("k_prompt", (2, T, 16, 128)), 0, 1)
    v_prompt = np.moveaxis(stack_p("v_prompt", (2, T, 16, 128)), 0, 1)
    conv_sample = np.moveaxis(stack_s("conv_sample", (2, HALO, D)), 0, 1)
    hgrn_sample = np.moveaxis(stack_s("hgrn_sample", (2, 8, 128, 128)), 0, 1)
    k_sample = np.moveaxis(stack_s("k_sample", (2, TS, 16, 128)), 0, 1)
    v_sample = np.moveaxis(stack_s("v_sample", (2, TS, 16, 128)), 0, 1)
    outs = (y_prompt, y_sample, conv_prompt, hgrn_prompt, k_prompt, v_prompt, conv_sample, hgrn_sample, k_sample,
            v_sample)
    return tuple(np.ascontiguousarray(a, dtype=np.float32) for a in outs)
```

```python
import contextlib
import math
import numpy as np
import concourse.bass as bass
import concourse.mybir as mybir
from concourse.bass_utils import run_bass_kernel_spmd

F32 = mybir.dt.float32
BF16 = mybir.dt.bfloat16
AF = mybir.ActivationFunctionType
ALU = mybir.AluOpType
AX = mybir.AxisListType

D = 1024
DEPTH = 4
TS = 16
CONV_W = 31
HALO = CONV_W - 1
ALPHA = (2.0 * DEPTH) ** 0.25
EPS = 1e-5
NH = 16


class Buf:
    __slots__ = ("w", "r", "multi")

    def __init__(self, multi=False):
        self.multi = multi
        self.w = {}
        self.r = {}


class Eng:
    def __init__(self, eng, sem, raw_same=True):
        self.eng = eng
        self.sem = sem
        self.key = id(sem)
        self.count = 0
        self.waited = {}
        self.raw_same = raw_same


class FW:
    def __init__(self, nc, n_dma_sems=24):
        self.nc = nc
        self.es = contextlib.ExitStack()
        es = self.es
        self.pe = Eng(nc.tensor, es.enter_context(nc.semaphore("s_pe")), raw_same=False)
        self.act = Eng(nc.scalar, es.enter_context(nc.semaphore("s_act")))
        self.dve = Eng(nc.vector, es.enter_context(nc.semaphore("s_dve")))
        self.pool = Eng(nc.gpsimd, es.enter_context(nc.semaphore("s_pool")))
        self.sp = Eng(nc.sync, es.enter_context(nc.semaphore("s_sp")))
        self.engs = [self.pe, self.act, self.dve, self.pool, self.sp]
        self.dma_sems = {}
        for q in ("sp", "pool"):
            lst = [[es.enter_context(nc.semaphore(f"d_{q}{i}")), 0] for i in range(n_dma_sems)]
            self.dma_sems[q] = [lst, 0]
        self.n_inst = 0

    def _wait(self, E, sem, val):
        k = id(sem)
        if E.waited.get(k, 0) >= val:
            return
        E.eng.wait_ge(sem, val)
        E.waited[k] = val

    def _deps(self, E, reads, writes):
        for b in reads:
            for k, (sem, val) in b.w.items():
                if k == E.key and not E.raw_same:
                    continue
                self._wait(E, sem, val)
        for b in writes:
            if not b.multi:
                for k, (sem, val) in b.w.items():
                    if k == E.key:
                        continue
                    self._wait(E, sem, val)
            for k, (sem, val) in b.r.items():
                if k == E.key:
                    continue
                self._wait(E, sem, val)

    def _record(self, sem, val, reads, writes):
        k = id(sem)
        for b in reads:
            b.r[k] = (sem, val)
        for b in writes:
            if b.multi:
                b.w[k] = (sem, val)
            else:
                b.w = {k: (sem, val)}
                b.r = {}

    def _chk(self):
        import os
        lim = int(os.environ.get("KOPS", "0"))
        if lim and self.n_inst >= lim and not getattr(self, "stopped", False):
            self._barrier()
            self.stopped = True
            print("build stopped at inst", self.n_inst)

    def op(self, E, fn, reads=(), writes=(), sig=True):
        self._chk()
        if getattr(self, "stopped", False):
            return None
        self._deps(E, reads, writes)
        inst = fn()
        if sig:
            E.count += 1
            inst.then_inc(E.sem, 1)
            val = E.count
        else:
            val = E.count + 1
        self._record(E.sem, val, reads, writes)
        self.n_inst += 1
        return inst

    def dma(self, q, out, in_, reads=(), writes=(), **kw):
        self._chk()
        if getattr(self, "stopped", False):
            return None
        E = self.sp if q == "sp" else self.pool
        pool = self.dma_sems[q]
        lst, idx = pool
        ent = lst[idx % len(lst)]
        pool[1] = idx + 1
        sem, cnt = ent
        if cnt > 0:
            self._wait(E, sem, 16 * cnt)
        self._deps(E, reads, writes)
        inst = E.eng.dma_start(out=out, in_=in_, **kw)
        ent[1] = cnt + 1
        inst.then_inc(sem, 16)
        self._record(sem, 16 * (cnt + 1), reads, writes)
        self.n_inst += 1
        return inst

    def barrier(self):
        import os
        self.n_bar = getattr(self, "n_bar", 0) + 1
        if getattr(self, "stopped", False):
            return
        if int(os.environ.get("KSTOP", "0")) == self.n_bar:
            self._barrier()
            self.stopped = True
            print("build stopped at barrier", self.n_bar)
            return
        self._barrier()

    def _barrier(self):
        for E in self.engs:
            for q, (lst, idx) in self.dma_sems.items():
                for sem, cnt in lst:
                    if cnt > 0:
                        self._wait(E, sem, 16 * cnt)
            for X in self.engs:
                if X is E or X.count == 0 or X is self.sp:
                    continue
                self._wait(E, X.sem, X.count)

    def finish(self):
        if not getattr(self, "stopped", False):
            self._barrier()

    def close(self):
        self.es.close()


class StopBuild(Exception):
    pass


class Ring:
    def __init__(self, tiles):
        self.tiles = [(t, Buf()) for t in tiles]
        self.i = 0

    def next(self):
        t = self.tiles[self.i % len(self.tiles)]
        self.i += 1
        return t


def build_program(T, P):
    nc = bass.Bass("TRN2", target_bir_lowering=False)
    fw = FW(nc)
    pe, act, dve, pool = fw.pe, fw.act, fw.dve, fw.pool
    TT = T + TS
    NTK = P // 128

    def din(name, shape):
        return nc.dram_tensor(name, list(shape), F32, kind="ExternalInput").ap()

    def dout(name, shape):
        return nc.dram_tensor(name, list(shape), F32, kind="ExternalOutput").ap()

    def dscr(name, shape, dt):
        return nc.dram_tensor(name, list(shape), dt).ap()

    x_prompt = din("x_prompt", [T, D])
    x_sample = din("x_sample", [TS, D])
    state_conv = din("state_conv", [2, HALO, D])
    state_hgrn = din("state_hgrn", [2, 8, 128, 128])
    cache_k = din("cache_k", [2, P, 2048])
    cache_v = din("cache_v", [2, P, 2048])
    w_in_even = din("w_in_even", [2, D, 7168])
    w_out_even = din("w_out_even", [2, 2048, D])
    conv_w = din("conv_w", [2, CONV_W, D])
    conv_b = din("conv_b", [2, D])
    conv_ln_g = din("conv_ln_g", [2, D])
    conv_ln_b = din("conv_ln_b", [2, D])
    hgrn_lb_logits = din("hgrn_lb_logits", [2, D])
    hgrn_norm_g = din("hgrn_norm_g", [2, 128])
    w_in_odd = din("w_in_odd", [2, D, 8192])
    w_out_odd = din("w_out_odd", [2, 2048, D])
    lam_q1 = din("lam_q1", [2, 64])
    lam_k1 = din("lam_k1", [2, 64])
    lam_q2 = din("lam_q2", [2, 64])
    lam_k2 = din("lam_k2", [2, 64])
    subln_g = din("subln_g", [2, 128])
    post_ln_g = din("post_ln_g", [4, D])
    post_ln_b = din("post_ln_b", [4, D])
    y_prompt = dout("y_prompt", [T, D])
    y_sample = dout("y_sample", [TS, D])
    conv_prompt = dout("conv_prompt", [2, HALO, D])
    hgrn_prompt = dout("hgrn_prompt", [2, 8, 128, 128])
    k_prompt = dout("k_prompt", [2, T, 2048])
    v_prompt = dout("v_prompt", [2, T, 2048])
    conv_sample = dout("conv_sample", [2, HALO, D])
    hgrn_sample = dout("hgrn_sample", [2, 8, 128, 128])
    k_sample = dout("k_sample", [2, TS, 2048])
    v_sample = dout("v_sample", [2, TS, 2048])
    xres = [dscr("xres0", [TT, D], F32), dscr("xres1", [TT, D], F32)]
    UTp = dscr("UTp", [8, 128, HALO + T], BF16)
    UTs = dscr("UTs", [8, 128, HALO + TS], BF16)
    QTe = dscr("QTe", [8, 128, TT], F32)
    KKT = dscr("KKT", [8, 128, TT], F32)
    ZA = dscr("ZA", [TT, D], BF16)
    ZB = dscr("ZB", [TT, D], BF16)
    Gs = dscr("Gs", [TT, D], F32)
    KKs = dscr("KKs", [TT, D], F32)
    VH = dscr("VH", [TT, D], BF16)
    AT = dscr("AT", [16, 128, TT], BF16)
    QTo = dscr("QTo", [16, 128, TT], BF16)
    KTo = dscr("KTo", [16, 128, TT], BF16)
    Vo = dscr("Vo", [TT, 2048], BF16)
    Zo = dscr("Zo", [TT, 2048], BF16)
    import os
    if os.environ.get("KDUMP", "") == "1":
        dbg_AT = nc.dram_tensor("dbg_AT", [16, 128, TT], BF16, kind="ExternalOutput").ap()
    B_out = Buf(multi=True)
    Bx = [Buf(multi=True), Buf(multi=True)]
    B_UT, B_QTe, B_KKT, B_ZA, B_ZB, B_G, B_KK, B_VH, B_AT = [Buf(multi=True) for _ in range(9)]
    B_QTo, B_KTo, B_Vo, B_Zo = [Buf(multi=True) for _ in range(4)]
    B_in = Buf()
    _dma = fw.dma

    def dma_route(q, out, in_, reads=(), writes=(), **kw):
        if any(b is B_out for b in writes):
            q = "pool"
        return _dma(q, out, in_, reads=reads, writes=writes, **kw)

    fw.dma = dma_route

    token_blocks = [(t0, 512) for t0 in range(0, T, 512)] + [(T, TS)]
    token_tiles = [(t0, 128) for t0 in range(0, T, 128)] + [(T, TS)]

    top = contextlib.ExitStack()

    uid = [0]

    def sb(es, name, shape, dt):
        uid[0] += 1
        return es.enter_context(nc.sbuf_tensor(f"{name}_{uid[0]}", list(shape), dt))

    def psum(es, name, shape, dt):
        uid[0] += 1
        return es.enter_context(nc.psum_tensor(f"{name}_{uid[0]}", list(shape), dt))

    XT = sb(top, "XT", [128, 8, TT], BF16)
    B_XT = [Buf() for _ in token_tiles]
    ident_bf = sb(top, "ident_bf", [128, 128], BF16)
    ident_f = sb(top, "ident_f", [128, 128], F32)
    Mcum = sb(top, "Mcum", [128, 128], F32)
    Mext = sb(top, "Mext", [128, 2], F32)
    Mrev = sb(top, "Mrev", [128, 128], F32)
    Mmask = sb(top, "Mmask", [128, 128], F32)
    epsb = sb(top, "epsb", [128, 1], F32)
    plg = sb(top, "plg", [128, D], F32)
    plb = sb(top, "plb", [128, D], F32)
    B_const = Buf()
    B_pl = Buf()

    def xt_bufs(t0, n):
        return [B_XT[i] for i, (a, m) in enumerate(token_tiles) if a < t0 + n and a + m > t0]

    def build_consts():
        g = nc.gpsimd
        ops = [
            lambda: g.memset(ident_f[:], 1.0),
            lambda: g.affine_select(out=ident_f[:], in_=ident_f[:], pattern=[[-1, 128]], compare_op=ALU.is_equal,
                                    fill=0.0, base=0, channel_multiplier=1),
            lambda: g.tensor_copy(ident_bf[:], ident_f[:]),
            lambda: g.memset(Mmask[:], 1.0),
            lambda: g.affine_select(out=Mmask[:], in_=Mmask[:], pattern=[[1, 128]], compare_op=ALU.is_ge,
                                    fill=0.0, base=0, channel_multiplier=-1),
            lambda: g.memset(Mext[:], 1.0),
            lambda: g.affine_select(out=Mext[:, 0:1], in_=Mext[:, 0:1], pattern=[[0, 1]], compare_op=ALU.is_ge,
                                    fill=0.0, base=31, channel_multiplier=-1),
            lambda: g.tensor_scalar(out=Mcum[:], in0=Mmask[:], scalar1=Mext[:, 0:1], scalar2=None, op0=ALU.subtract),
            lambda: g.memset(Mrev[:], 1.0),
            lambda: g.affine_select(out=Mrev[:], in_=Mrev[:], pattern=[[-1, 128]], compare_op=ALU.is_gt,
                                    fill=0.0, base=0, channel_multiplier=1),
            lambda: g.memset(epsb[:], EPS),
        ]
        for f in ops:
            fw.op(pool, f, reads=[B_const], writes=[B_const])

    build_consts()

    def phase_load_x():
        with contextlib.ExitStack() as es:
            xin = Ring([sb(es, f"xin{i}", [128, D], F32) for i in range(2)])
            xbf = Ring([sb(es, f"xbf{i}", [128, D], BF16) for i in range(2)])
            pt = Ring([psum(es, f"ptx{i}", [128, 8, 128], BF16) for i in range(2)])
            for ti, (t0, n) in enumerate(token_tiles):
                src = x_prompt[t0:t0 + n, :] if t0 < T else x_sample[:, :]
                xt_, bx = xin.next()
                fw.dma("sp", xt_[0:n, :], src, writes=[bx])
                fw.dma("sp", xres[0][t0:t0 + n, :], xt_[0:n, :], reads=[bx], writes=[Bx[0]])
                xb, bxb = xbf.next()
                fw.op(act, lambda: nc.scalar.copy(out=xb[0:n, :], in_=xt_[0:n, :]), reads=[bx], writes=[bxb])
                p, bp = pt.next()
                for c in range(8):
                    fw.op(pe, lambda: nc.tensor.transpose(p[:, c, 0:n], xb[0:n, c * 128:(c + 1) * 128],
                                                          ident_bf[0:n, 0:n]),
                          reads=[bxb, B_const], writes=[bp])
                fw.op(dve, lambda: nc.vector.tensor_copy(XT[:, :, t0:t0 + n], p[:, :, 0:n]), reads=[bp],
                      writes=[B_XT[ti]])
        fw.barrier()

    def load_w(wt, bw, W, col0, ncols):
        fw.dma("pool", wt[:, :, 0:ncols], W[:, col0:col0 + ncols].rearrange("(c p) n -> p c n", p=128),
               reads=[B_in], writes=[bw])

    def mm_fm(p_ap, bp, wt, bw, wc0, t0, n):
        for c in range(8):
            fw.op(pe, lambda: nc.tensor.matmul(p_ap, wt[:, c, wc0:wc0 + 128], XT[:, c, t0:t0 + n],
                                               start=(c == 0), stop=(c == 7)),
                  reads=[bw] + xt_bufs(t0, n), writes=[bp], sig=(c == 7))

    def mm_tm(p_ap, bp, wt, bw, wc0, ncols, t0, n):
        for c in range(8):
            fw.op(pe, lambda: nc.tensor.matmul(p_ap, XT[:, c, t0:t0 + n], wt[:, c, wc0:wc0 + ncols],
                                               start=(c == 0), stop=(c == 7)),
                  reads=[bw] + xt_bufs(t0, n), writes=[bp], sig=(c == 7))

    def act_fn(src_ap, bsrc, dst, bdst, func, scale=1.0):
        fw.op(act, lambda: nc.scalar.activation(out=dst, in_=src_ap, func=func, scale=scale), reads=[bsrc],
              writes=[bdst])

    def sigmoid_from(es_tile, src_ap, bsrc, dst, bdst, sign=-1.0):
        fw.op(act, lambda: nc.scalar.activation(out=dst, in_=src_ap, func=AF.Exp, scale=sign), reads=[bsrc],
              writes=[bdst])
        fw.op(dve, lambda: nc.vector.tensor_scalar_add(dst, dst, 1.0), reads=[bdst], writes=[bdst])
        fw.op(dve, lambda: nc.vector.reciprocal(dst, dst), reads=[bdst], writes=[bdst])

    def load_post_ln(l):
        fw.dma("sp", plg[:], post_ln_g[l:l + 1, :].partition_broadcast(128), writes=[B_pl])
        fw.dma("sp", plb[:], post_ln_b[l:l + 1, :].partition_broadcast(128), writes=[B_pl])

    def layer_norm_rows(es, src, bsrc, n, dst, bdst, gb, bb_, Bg, tag):
        st = es_tiles[tag + "st"]
        stats, bst = st.next()
        for hlf in range(2):
            fw.op(dve, lambda: nc.vector.bn_stats(stats[0:n, hlf, :], src[0:n, hlf * 512:(hlf + 1) * 512]),
                  reads=[bsrc], writes=[bst])
        mv, bmv = es_tiles[tag + "mv"].next()
        fw.op(dve, lambda: nc.vector.bn_aggr(mv[0:n, 0:2], stats[0:n, :, :]), reads=[bst], writes=[bmv])
        fw.op(act, lambda: nc.scalar.activation(out=mv[0:n, 2:3], in_=mv[0:n, 1:2], func=AF.Ln, bias=epsb[0:n, :]),
              reads=[bmv, B_const], writes=[bmv])
        fw.op(act, lambda: nc.scalar.activation(out=mv[0:n, 2:3], in_=mv[0:n, 2:3], func=AF.Exp, scale=-0.5),
              reads=[bmv], writes=[bmv])
        fw.op(dve, lambda: nc.vector.scalar_tensor_tensor(out=mv[0:n, 3:4], in0=mv[0:n, 0:1], scalar=-1.0,
                                                          in1=mv[0:n, 2:3], op0=ALU.mult, op1=ALU.mult),
              reads=[bmv], writes=[bmv])
        fw.op(dve, lambda: nc.vector.tensor_scalar(out=dst[0:n, :], in0=src[0:n, :], scalar1=mv[0:n, 2:3],
                                                   scalar2=mv[0:n, 3:4], op0=ALU.mult, op1=ALU.add),
              reads=[bsrc, bmv], writes=[bdst])
        fw.op(dve, lambda: nc.vector.tensor_tensor(out=dst[0:n, :], in0=dst[0:n, :], in1=gb[0:n, :], op=ALU.mult),
              reads=[bdst, Bg], writes=[bdst])
        fw.op(dve, lambda: nc.vector.tensor_tensor(out=dst[0:n, :], in0=dst[0:n, :], in1=bb_[0:n, :], op=ALU.add),
              reads=[bdst, Bg], writes=[bdst])

    es_tiles = {}

    def phase_out(l, W_out, xsrc, bxsrc, last):
        load_post_ln(l)
        xdst = xres[(l + 1) % 2]
        bxdst = Bx[(l + 1) % 2]
        with contextlib.ExitStack() as es:
            wo = sb(es, "wo", [128, 16, D], BF16)
            bwo = Buf()
            for c in range(0, 16, 4):
                fw.dma("pool", wo[:, c:c + 4, :],
                       W_out[c * 128:(c + 4) * 128, :].rearrange("(c p) n -> p c n", p=128), reads=[B_in],
                       writes=[bwo] if c == 0 else [bwo])
            at_r = Ring([sb(es, f"at{i}", [128, 16, 128], BF16) for i in range(2)])
            x_r = Ring([sb(es, f"xo{i}", [128, D], F32) for i in range(2)])
            r_r = Ring([sb(es, f"ro{i}", [128, D], F32) for i in range(2)])
            xn_r = Ring([sb(es, f"xn{i}", [128, D], F32) for i in range(2)])
            xb_r = Ring([sb(es, f"xnb{i}", [128, D], BF16) for i in range(2)])
            es_tiles["ost"] = Ring([sb(es, f"ost{i}", [128, 2, 6], F32) for i in range(2)])
            es_tiles["omv"] = Ring([sb(es, f"omv{i}", [128, 4], F32) for i in range(2)])
            py_r = Ring([psum(es, f"py{i}", [128, D], F32) for i in range(2)])
            pt_r = Ring([psum(es, f"pto{i}", [128, 8, 128], BF16) for i in range(2)])
            pend = []
            for ti, (t0, n) in enumerate(token_tiles):
                at_, bat = at_r.next()
                fw.dma("sp", at_[:, :, 0:n], AT[:, :, t0:t0 + n].rearrange("c p t -> p c t"), reads=[B_AT],
                       writes=[bat])
                xt_, bx = x_r.next()
                fw.dma("sp", xt_[0:n, :], xsrc[t0:t0 + n, :], reads=[bxsrc], writes=[bx])
                py, bpy = py_r.next()
                for hlf in range(2):
                    for c in range(16):
                        fw.op(pe, lambda: nc.tensor.matmul(py[0:n, hlf * 512:(hlf + 1) * 512], at_[:, c, 0:n],
                                                           wo[:, c, hlf * 512:(hlf + 1) * 512],
                                                           start=(c == 0), stop=(c == 15)),
                              reads=[bat, bwo], writes=[bpy], sig=(c == 15))
                while pend:
                    pend.pop(0)()
                r_, br = r_r.next()
                fw.op(dve, lambda: nc.vector.scalar_tensor_tensor(out=r_[0:n, :], in0=xt_[0:n, :], scalar=ALPHA,
                                                                  in1=py[0:n, :], op0=ALU.mult, op1=ALU.add),
                      reads=[bx, bpy], writes=[br])
                xn, bxn = xn_r.next()
                layer_norm_rows(es, r_, br, n, xn, bxn, plg, plb, B_pl, "o")
                if last:
                    dst = y_prompt[t0:t0 + n, :] if t0 < T else y_sample[:, :]
                    fw.dma("sp", dst, xn[0:n, :], reads=[bxn], writes=[B_out])
                else:
                    fw.dma("pool", xdst[t0:t0 + n, :], xn[0:n, :], reads=[bxn], writes=[bxdst])
                    xb, bxb = xb_r.next()
                    fw.op(act, lambda: nc.scalar.copy(out=xb[0:n, :], in_=xn[0:n, :]), reads=[bxn], writes=[bxb])
                    def tail(xb=xb, bxb=bxb, n=n, t0=t0, ti=ti):
                        p, bp = pt_r.next()
                        for c in range(8):
                            fw.op(pe, lambda: nc.tensor.transpose(p[:, c, 0:n], xb[0:n, c * 128:(c + 1) * 128],
                                                                  ident_bf[0:n, 0:n]),
                                  reads=[bxb, B_const], writes=[bp], sig=(c == 7))
                        fw.op(act, lambda: nc.scalar.copy(out=XT[:, :, t0:t0 + n], in_=p[:, :, 0:n]), reads=[bp],
                              writes=[B_XT[ti]])
                    pend.append(tail)
            while pend:
                pend.pop(0)()
        fw.barrier()

    def even_layer(l):
        e = l // 2
        W = w_in_even[e]
        xsrc, bxsrc = xres[l % 2], Bx[l % 2]
        with contextlib.ExitStack() as esL:
            lbT = sb(esL, "lbT", [128, 8], F32)
            omlT = sb(esL, "omlT", [128, 8], F32)
            lbB = sb(esL, "lbB", [128, D], F32)
            omlB = sb(esL, "omlB", [128, D], F32)
            B_lb = Buf()
            if e == 0:
                fw.op(pool, lambda: nc.gpsimd.memset(lbT[:], 0.0), writes=[B_lb])
                fw.op(pool, lambda: nc.gpsimd.memset(omlT[:], 1.0), writes=[B_lb])
                fw.op(pool, lambda: nc.gpsimd.memset(lbB[:], 0.0), writes=[B_lb])
                fw.op(pool, lambda: nc.gpsimd.memset(omlB[:], 1.0), writes=[B_lb])
            else:
                with contextlib.ExitStack() as es:
                    l0T = sb(es, "l0T", [128, 8], F32)
                    l0B = sb(es, "l0B", [128, D], F32)
                    fw.dma("sp", lbT[:], hgrn_lb_logits[1, :].rearrange("(h k) -> k h", k=128), writes=[B_lb],
                           allow_slow_non_contiguous=True)
                    fw.dma("sp", l0T[:], hgrn_lb_logits[0, :].rearrange("(h k) -> k h", k=128), writes=[B_lb],
                           allow_slow_non_contiguous=True)
                    fw.dma("sp", lbB[:], hgrn_lb_logits[1:2, :].partition_broadcast(128), writes=[B_lb])
                    fw.dma("sp", l0B[:], hgrn_lb_logits[0:1, :].partition_broadcast(128), writes=[B_lb])
                    for (a_, b_, o_) in ((lbT, l0T, omlT), (lbB, l0B, omlB)):
                        fw.op(dve, lambda: nc.vector.tensor_tensor(out=a_[:], in0=a_[:], in1=b_[:], op=ALU.subtract),
                              reads=[B_lb], writes=[B_lb])
                        fw.op(act, lambda: nc.scalar.activation(out=a_[:], in_=a_[:], func=AF.Exp, scale=-1.0),
                              reads=[B_lb], writes=[B_lb])
                        fw.op(dve, lambda: nc.vector.tensor_scalar_add(a_[:], a_[:], 1.0), reads=[B_lb], writes=[B_lb])
                        fw.op(dve, lambda: nc.vector.reciprocal(a_[:], a_[:]), reads=[B_lb], writes=[B_lb])
                        fw.op(dve, lambda: nc.vector.tensor_scalar(out=o_[:], in0=a_[:], scalar1=-1.0, scalar2=1.0,
                                                                   op0=ALU.mult, op1=ALU.add),
                              reads=[B_lb], writes=[B_lb])
                    fw.barrier()

            with contextlib.ExitStack() as es:
                w_r = Ring([sb(es, f"w{i}", [128, 8, 512], BF16) for i in range(3)])
                ps_r = Ring([psum(es, f"pp{i}", [128, 512], F32) for i in range(6)])
                tmp_r = Ring([sb(es, f"tmp{i}", [128, 512], F32) for i in range(3)])
                tmp2_r = Ring([sb(es, f"tmpb{i}", [128, 512], F32) for i in range(3)])
                of_r = Ring([sb(es, f"of{i}", [128, 512], F32) for i in range(3)])
                of2_r = Ring([sb(es, f"ofb{i}", [128, 512], F32) for i in range(3)])
                ob_r = Ring([sb(es, f"ob{i}", [128, 512], BF16) for i in range(3)])
                halo = sb(es, "halo", [128, 8, HALO], BF16)
                bhalo = Buf()

                fw.op(pool, lambda: nc.gpsimd.memset(halo[:], 0.0), writes=[bhalo])
                fw.dma("sp", UTp[:, :, 0:HALO].rearrange("c p t -> p c t"), halo[:], reads=[bhalo], writes=[B_UT])
                with contextlib.ExitStack() as es2:
                    stt = sb(es2, "stt", [HALO, D], F32)
                    halo2 = sb(es2, "halo2", [128, 8, HALO], BF16)
                    bst, bh2 = Buf(), Buf()
                    pth = psum(es2, "pth", [128, 8, 32], F32)
                    bpth = Buf()
                    fw.dma("sp", stt[:], state_conv[e], writes=[bst])
                    for c in range(8):
                        fw.op(pe, lambda: nc.tensor.transpose(pth[:, c, 0:HALO], stt[0:HALO, c * 128:(c + 1) * 128],
                                                              ident_f[0:HALO, 0:HALO]),
                              reads=[bst, B_const], writes=[bpth])
                    fw.op(dve, lambda: nc.vector.tensor_copy(halo2[:], pth[:, :, 0:HALO]), reads=[bpth], writes=[bh2])
                    fw.dma("sp", UTs[:, :, 0:HALO].rearrange("c p t -> p c t"), halo2[:], reads=[bh2], writes=[B_UT])
                    fw.dma("sp", conv_sample[e, 0:HALO - TS, :], stt[TS:HALO, :], reads=[bst], writes=[B_out])
                    fw.barrier()

                def ut_dst(c, t0, n):
                    if t0 < T:
                        return UTp[c, :, HALO + t0:HALO + t0 + n]
                    return UTs[c, :, HALO:HALO + n]

                for j in range(2):
                    wa, bwa = w_r.next()
                    load_w(wa, bwa, W, j * 512, 512)
                    wg, bwg = w_r.next()
                    load_w(wg, bwg, W, 1024 + j * 512, 512)
                    for cc in range(4):
                        c = j * 4 + cc
                        for (t0, n) in token_blocks:
                            pa, bpa = ps_r.next()
                            pg, bpg = ps_r.next()
                            mm_fm(pa[:, 0:n], bpa, wa, bwa, cc * 128, t0, n)
                            mm_fm(pg[:, 0:n], bpg, wg, bwg, cc * 128, t0, n)
                            tm, btm = tmp_r.next()
                            act_fn(pg[:, 0:n], bpg, tm[:, 0:n], btm, AF.Sigmoid)
                            ob, bob = ob_r.next()
                            fw.op(dve, lambda: nc.vector.tensor_tensor(out=ob[:, 0:n], in0=pa[:, 0:n], in1=tm[:, 0:n],
                                                                       op=ALU.mult),
                                  reads=[bpa, btm], writes=[bob])
                            fw.dma("sp", ut_dst(c, t0, n), ob[:, 0:n], reads=[bob], writes=[B_UT])
                for j in range(2):
                    wq, bwq = w_r.next()
                    load_w(wq, bwq, W, 3072 + j * 512, 512)
                    for cc in range(4):
                        c = j * 4 + cc
                        for (t0, n) in token_blocks:
                            pq, bpq = ps_r.next()
                            mm_fm(pq[:, 0:n], bpq, wq, bwq, cc * 128, t0, n)
                            of, bof = of_r.next()
                            act_fn(pq[:, 0:n], bpq, of[:, 0:n], bof, AF.Silu)
                            fw.dma("sp", QTe[c, :, t0:t0 + n], of[:, 0:n], reads=[bof], writes=[B_QTe])
                for j in range(2):
                    wf, bwf = w_r.next()
                    load_w(wf, bwf, W, 4096 + j * 512, 512)
                    for cc in range(4):
                        c = j * 4 + cc
                        for (t0, n) in token_blocks:
                            pf, bpf = ps_r.next()
                            mm_fm(pf[:, 0:n], bpf, wf, bwf, cc * 128, t0, n)
                            tm, btm = tmp_r.next()
                            act_fn(pf[:, 0:n], bpf, tm[:, 0:n], btm, AF.Sigmoid, scale=-1.0)
                            of, bof = of_r.next()
                            fw.op(dve, lambda: nc.vector.tensor_scalar(out=of[:, 0:n], in0=tm[:, 0:n],
                                                                       scalar1=omlT[:, c:c + 1], scalar2=None,
                                                                       op0=ALU.mult),
                                  reads=[btm, B_lb], writes=[bof])
                            fw.dma("sp", KKT[c, :, t0:t0 + n], of[:, 0:n], reads=[bof], writes=[B_KKT])
                for (col0, dstT, bdst, kind) in ((2048, ZA, B_ZA, "silu"), (6144, ZB, B_ZB, "silu"),
                                                 (5120, VH, B_VH, "copy"), (4096, None, None, "f")):
                    for j in range(2):
                        w_, bw = w_r.next()
                        load_w(w_, bw, W, col0 + j * 512, 512)
                        for (t0, n) in token_tiles:
                            p_, bp = ps_r.next()
                            mm_tm(p_[0:n, :], bp, w_, bw, 0, 512, t0, n)
                            cs = slice(j * 512, (j + 1) * 512)
                            if kind == "copy":
                                ob, bob = ob_r.next()
                                fw.op(act, lambda: nc.scalar.copy(out=ob[0:n, :], in_=p_[0:n, :]), reads=[bp],
                                      writes=[bob])
                                fw.dma("sp", dstT[t0:t0 + n, cs], ob[0:n, :], reads=[bob], writes=[bdst])
                            elif kind == "silu":
                                ob, bob = ob_r.next()
                                act_fn(p_[0:n, :], bp, ob[0:n, :], bob, AF.Silu)
                                fw.dma("sp", dstT[t0:t0 + n, cs], ob[0:n, :], reads=[bob], writes=[bdst])
                            else:
                                tm, btm = tmp_r.next()
                                sigmoid_from(None, p_[0:n, :], bp, tm[0:n, :], btm)
                                fw.op(dve, lambda: nc.vector.tensor_tensor(out=tm[0:n, :], in0=tm[0:n, :],
                                                                           in1=omlB[0:n, cs], op=ALU.mult),
                                      reads=[btm, B_lb], writes=[btm])
                                t2, bt2 = tmp2_r.next()
                                fw.op(dve, lambda: nc.vector.tensor_tensor(out=t2[0:n, :], in0=tm[0:n, :],
                                                                           in1=lbB[0:n, cs], op=ALU.add),
                                      reads=[btm, B_lb], writes=[bt2])
                                of, bof = of_r.next()
                                fw.op(act, lambda: nc.scalar.activation(out=of[0:n, :], in_=t2[0:n, :], func=AF.Ln),
                                      reads=[bt2], writes=[bof])
                                fw.dma("sp", Gs[t0:t0 + n, cs], of[0:n, :], reads=[bof], writes=[B_G])
                                o2, bo2 = of2_r.next()
                                fw.op(dve, lambda: nc.vector.tensor_tensor(out=o2[0:n, :], in0=omlB[0:n, cs],
                                                                           in1=tm[0:n, :], op=ALU.subtract),
                                      reads=[btm, B_lb], writes=[bo2])
                                fw.dma("sp", KKs[t0:t0 + n, cs], o2[0:n, :], reads=[bo2], writes=[B_KK])
                for j in range(2):
                    wa, bwa = w_r.next()
                    load_w(wa, bwa, W, j * 512, 512)
                    wg, bwg = w_r.next()
                    load_w(wg, bwg, W, 1024 + j * 512, 512)
                    cs = slice(j * 512, (j + 1) * 512)
                    for (t0, n) in ((T - 128, 128), (T, TS)):
                        pa, bpa = ps_r.next()
                        pg, bpg = ps_r.next()
                        mm_tm(pa[0:n, :], bpa, wa, bwa, 0, 512, t0, n)
                        mm_tm(pg[0:n, :], bpg, wg, bwg, 0, 512, t0, n)
                        tm, btm = tmp_r.next()
                        act_fn(pg[0:n, :], bpg, tm[0:n, :], btm, AF.Sigmoid)
                        of, bof = of_r.next()
                        fw.op(dve, lambda: nc.vector.tensor_tensor(out=of[0:n, :], in0=pa[0:n, :], in1=tm[0:n, :],
                                                                   op=ALU.mult),
                              reads=[bpa, btm], writes=[bof])
                        if t0 < T:
                            fw.dma("sp", conv_prompt[e, :, cs], of[128 - HALO:128, :], reads=[bof], writes=[B_out])
                        else:
                            fw.dma("sp", conv_sample[e, HALO - TS:HALO, cs], of[0:TS, :], reads=[bof], writes=[B_out])
            fw.barrier()

            with contextlib.ExitStack() as es:
                diag = sb(es, "diag", [128, 8, CONV_W, 128], BF16)
                cwT = sb(es, "cwT", [128, 8, 32], F32)
                cbB = sb(es, "cbB", [128, D], F32)
                lgB = sb(es, "lgB", [128, D], F32)
                lbB2 = sb(es, "lbB2", [128, D], F32)
                B_cp = Buf()
                with contextlib.ExitStack() as es2:
                    cw = sb(es2, "cw", [CONV_W, D], F32)
                    bcw = Buf()
                    pcw = psum(es2, "pcw", [128, 8, 32], F32)
                    bpcw = Buf()
                    fw.dma("sp", cw[:], conv_w[e], writes=[bcw])
                    for c in range(8):
                        fw.op(pe, lambda: nc.tensor.transpose(pcw[:, c, 0:CONV_W], cw[0:CONV_W, c * 128:(c + 1) * 128],
                                                              ident_f[0:CONV_W, 0:CONV_W]),
                              reads=[bcw, B_const], writes=[bpcw])
                    fw.op(dve, lambda: nc.vector.tensor_copy(cwT[:, :, 0:CONV_W], pcw[:, :, 0:CONV_W]), reads=[bpcw],
                          writes=[B_cp])
                    for c in range(8):
                        for j in range(CONV_W):
                            eng = dve if (j % 2 == 0) else pool
                            ee = nc.vector if (j % 2 == 0) else nc.gpsimd
                            fw.op(eng, lambda: ee.tensor_scalar(out=diag[:, c, j, :], in0=ident_f[:],
                                                                scalar1=cwT[:, c, j:j + 1], scalar2=None,
                                                                op0=ALU.mult),
                                  reads=[B_cp, B_const], writes=[B_cp])
                    fw.dma("sp", cbB[:], conv_b[e:e + 1, :].partition_broadcast(128), writes=[B_cp])
                    fw.dma("sp", lgB[:], conv_ln_g[e:e + 1, :].partition_broadcast(128), writes=[B_cp])
                    fw.dma("sp", lbB2[:], conv_ln_b[e:e + 1, :].partition_broadcast(128), writes=[B_cp])
                    fw.barrier()
                wn_r = Ring([sb(es, f"wn{i}", [128, 8, 128 + HALO], BF16) for i in range(2)])
                za_r = Ring([sb(es, f"za{i}", [128, D], BF16) for i in range(2)])
                cb_r = Ring([sb(es, f"cb{i}", [128, D], F32) for i in range(2)])
                xn_r = Ring([sb(es, f"cxn{i}", [128, D], F32) for i in range(2)])
                ex_r = Ring([sb(es, f"cex{i}", [128, D], F32) for i in range(2)])
                oa_r = Ring([sb(es, f"coa{i}", [128, D], BF16) for i in range(2)])
                oT_r = Ring([sb(es, f"coT{i}", [128, 8, 128], BF16) for i in range(2)])
                es_tiles["cst"] = Ring([sb(es, f"cst{i}", [128, 2, 6], F32) for i in range(2)])
                es_tiles["cmv"] = Ring([sb(es, f"cmv{i}", [128, 4], F32) for i in range(2)])
                pc_r = Ring([psum(es, f"pc{i}", [128, D], F32) for i in range(2)])
                pt_r = Ring([psum(es, f"ptc{i}", [128, 8, 128], BF16) for i in range(2)])
                pendc = []
                for (t0, n) in token_tiles:
                    wn, bwn = wn_r.next()
                    src = UTp[:, :, t0:t0 + n + HALO] if t0 < T else UTs[:, :, 0:n + HALO]
                    fw.dma("sp", wn[:, :, 0:n + HALO], src.rearrange("c p t -> p c t"), reads=[B_UT], writes=[bwn])
                    za, bza = za_r.next()
                    fw.dma("sp", za[0:n, :], ZA[t0:t0 + n, :], reads=[B_ZA], writes=[bza])
                    pc, bpc = pc_r.next()
                    for c in range(8):
                        for j in range(CONV_W):
                            fw.op(pe, lambda: nc.tensor.matmul(pc[0:n, c * 128:(c + 1) * 128], wn[:, c, j:j + n],
                                                               diag[:, c, j, :], start=(j == 0),
                                                               stop=(j == CONV_W - 1)),
                                  reads=[bwn, B_cp], writes=[bpc], sig=(j == CONV_W - 1))
                    while pendc:
                        pendc.pop(0)()
                    cb, bcb = cb_r.next()
                    fw.op(dve, lambda: nc.vector.tensor_tensor(out=cb[0:n, :], in0=pc[0:n, :], in1=cbB[0:n, :],
                                                               op=ALU.add),
                          reads=[bpc, B_cp], writes=[bcb])
                    xn, bxn = xn_r.next()
                    layer_norm_rows(es, cb, bcb, n, xn, bxn, lgB, lbB2, B_cp, "c")
                    ex, bex = ex_r.next()
                    sigmoid_from(None, xn[0:n, :], bxn, ex[0:n, :], bex)
                    fw.op(dve, lambda: nc.vector.tensor_tensor(out=ex[0:n, :], in0=ex[0:n, :], in1=xn[0:n, :],
                                                               op=ALU.mult),
                          reads=[bex, bxn], writes=[bex])
                    oa, boa = oa_r.next()
                    fw.op(dve, lambda: nc.vector.tensor_tensor(out=oa[0:n, :], in0=ex[0:n, :], in1=za[0:n, :],
                                                               op=ALU.mult),
                          reads=[bex, bza], writes=[boa])
                    def tailc(oa=oa, boa=boa, n=n, t0=t0):
                        p, bp = pt_r.next()
                        for c in range(8):
                            fw.op(pe, lambda: nc.tensor.transpose(p[:, c, 0:n], oa[0:n, c * 128:(c + 1) * 128],
                                                                  ident_bf[0:n, 0:n]),
                                  reads=[boa, B_const], writes=[bp], sig=(c == 7))
                        oT, boT = oT_r.next()
                        fw.op(act, lambda: nc.scalar.copy(out=oT[:, :, 0:n], in_=p[:, :, 0:n]), reads=[bp],
                              writes=[boT])
                        fw.dma("pool", AT[0:8, :, t0:t0 + n].rearrange("c p t -> p c t"), oT[:, :, 0:n], reads=[boT],
                               writes=[B_AT])
                    pendc.append(tailc)
                while pendc:
                    pendc.pop(0)()
            fw.barrier()

            with contextlib.ExitStack() as es:
                hgB = sb(es, "hgB", [128, 128], F32)
                bhg = Buf()
                fw.dma("sp", hgB[:], hgrn_norm_g[e:e + 1, :].partition_broadcast(128), writes=[bhg])
                S = sb(es, "S", [128, 8, 128], F32)
                bS = Buf()
                Sa = sb(es, "Sa", [128, 8, 128], BF16)
                bSa = Buf()
                g_r = Ring([sb(es, f"hg{i}", [128, D], F32) for i in range(2)])
                kk_r = Ring([sb(es, f"hkk{i}", [128, D], F32) for i in range(2)])
                v_r = Ring([sb(es, f"hv{i}", [128, D], BF16) for i in range(2)])
                zb_r = Ring([sb(es, f"hzb{i}", [128, D], BF16) for i in range(2)])
                qT_r = Ring([sb(es, f"hqT{i}", [128, 8, 128], F32) for i in range(2)])
                kT_r = Ring([sb(es, f"hkT{i}", [128, 8, 128], F32) for i in range(2)])
                Eq = sb(es, "Eq", [128, 8, 128], F32)
                Ek = sb(es, "Ek", [128, 8, 128], F32)
                ER = sb(es, "ER", [128, D], F32)
                Ee = sb(es, "Ee", [128, 8, 2], F32)
                qtl = sb(es, "qtl", [128, 8, 128], BF16)
                ktl = sb(es, "ktl", [128, 8, 128], BF16)
                khat = sb(es, "khat", [128, D], BF16)
                attm = sb(es, "attm", [128, 8, 128], BF16)
                osq = sb(es, "osq", [128, D], F32)
                oss = sb(es, "oss", [128, 8], F32)
                on = sb(es, "on", [128, D], F32)
                ob = sb(es, "hob", [128, D], BF16)
                oT = sb(es, "hoT", [128, 8, 128], BF16)
                bEq, bEk, bER, bEe, bqtl, bktl, bkhat, battm, bosq, boss, bon, bob, boT = [Buf() for _ in range(13)]
                pBA = psum(es, "pBA", [128, 8, 128], F32)
                pRS = psum(es, "pRS", [128, D], F32)
                pO = psum(es, "pO", [128, D], F32)
                pE = psum(es, "pE", [128, 8, 2], F32)
                pT = psum(es, "pTh", [128, 8, 128], BF16)
                bpBA, bpRS, bpO, bpE, bpT = [Buf() for _ in range(5)]

                for seq in (0, 1):
                    tiles = [(t0, 64) for t0 in range(0, T, 64)] if seq == 0 else [(T, TS)]
                    if seq == 0:
                        fw.op(pool, lambda: nc.gpsimd.memset(S[:], 0.0), reads=[bS], writes=[bS])
                    else:
                        fw.dma("sp", S[:], state_hgrn[e].rearrange("h k v -> k h v"), reads=[bS], writes=[bS])
                    for (t0, n) in tiles:
                        g_, bg = g_r.next()
                        fw.dma("sp", g_[0:n, :], Gs[t0:t0 + n, :], reads=[B_G], writes=[bg])
                        kk_, bkk = kk_r.next()
                        fw.dma("sp", kk_[0:n, :], KKs[t0:t0 + n, :], reads=[B_KK], writes=[bkk])
                        v_, bv = v_r.next()
                        fw.dma("sp", v_[0:n, :], VH[t0:t0 + n, :], reads=[B_VH], writes=[bv])
                        zb_, bzb = zb_r.next()
                        fw.dma("sp", zb_[0:n, :], ZB[t0:t0 + n, :], reads=[B_ZB], writes=[bzb])
                        qT, bqT = qT_r.next()
                        fw.dma("sp", qT[:, :, 0:n], QTe[:, :, t0:t0 + n].rearrange("h k t -> k h t"), reads=[B_QTe],
                               writes=[bqT])
                        kT, bkT = kT_r.next()
                        fw.dma("sp", kT[:, :, 0:n], KKT[:, :, t0:t0 + n].rearrange("h k t -> k h t"), reads=[B_KKT],
                               writes=[bkT])
                        for h in range(8):
                            hs = slice(h * 128, (h + 1) * 128)
                            fw.op(pe, lambda: nc.tensor.matmul(pBA[:, h, 0:n], g_[0:n, hs], Mcum[0:n, 0:n],
                                                               start=True, stop=True),
                                  reads=[bg, B_const], writes=[bpBA])
                            fw.op(pe, lambda: nc.tensor.matmul(pE[:, h, :], g_[0:n, hs], Mext[0:n, :], start=True,
                                                               stop=True),
                                  reads=[bg, B_const], writes=[bpE])
                        for hlf in range(2):
                            cs = slice(hlf * 512, (hlf + 1) * 512)
                            fw.op(pe, lambda: nc.tensor.matmul(pRS[0:n, cs], Mrev[0:n, 0:n], g_[0:n, cs], start=True,
                                                               stop=True),
                                  reads=[bg, B_const], writes=[bpRS])
                        fw.op(act, lambda: nc.scalar.activation(out=Eq[:, :, 0:n], in_=pBA[:, :, 0:n], func=AF.Exp),
                              reads=[bpBA], writes=[bEq])
                        fw.op(act, lambda: nc.scalar.activation(out=Ek[:, :, 0:n], in_=pBA[:, :, 0:n], func=AF.Exp,
                                                                scale=-1.0),
                              reads=[bpBA], writes=[bEk])
                        fw.op(act, lambda: nc.scalar.activation(out=ER[0:n, :], in_=pRS[0:n, :], func=AF.Exp),
                              reads=[bpRS], writes=[bER])
                        fw.op(act, lambda: nc.scalar.activation(out=Ee[:], in_=pE[:], func=AF.Exp), reads=[bpE],
                              writes=[bEe])
                        fw.op(dve, lambda: nc.vector.tensor_tensor(out=qtl[:, :, 0:n], in0=qT[:, :, 0:n],
                                                                   in1=Eq[:, :, 0:n], op=ALU.mult),
                              reads=[bqT, bEq], writes=[bqtl])
                        fw.op(dve, lambda: nc.vector.tensor_tensor(out=ktl[:, :, 0:n], in0=kT[:, :, 0:n],
                                                                   in1=Ek[:, :, 0:n], op=ALU.mult),
                              reads=[bkT, bEk], writes=[bktl])
                        fw.op(dve, lambda: nc.vector.tensor_tensor(out=khat[0:n, :], in0=kk_[0:n, :], in1=ER[0:n, :],
                                                                   op=ALU.mult),
                              reads=[bkk, bER], writes=[bkhat])
                        fw.op(dve, lambda: nc.vector.tensor_tensor(out=Sa[:], in0=S[:],
                                                                   in1=Ee[:, :, 0:1].broadcast_to([128, 8, 128]),
                                                                   op=ALU.mult),
                              reads=[bS, bEe], writes=[bSa])
                        for h in range(8):
                            fw.op(pe, lambda: nc.tensor.matmul(pBA[0:n, h, 0:n], ktl[:, h, 0:n], qtl[:, h, 0:n],
                                                               start=True, stop=True),
                                  reads=[bktl, bqtl, bEq, bEk], writes=[bpBA])
                        fw.op(dve, lambda: nc.vector.tensor_tensor(
                            out=attm[0:n, :, 0:n], in0=pBA[0:n, :, 0:n],
                            in1=Mmask[0:n, 0:n].unsqueeze(1).broadcast_to([n, 8, n]), op=ALU.mult),
                            reads=[bpBA, B_const], writes=[battm])
                        for h in range(8):
                            hs = slice(h * 128, (h + 1) * 128)
                            fw.op(pe, lambda: nc.tensor.matmul(pO[0:n, hs], qtl[:, h, 0:n], Sa[:, h, :], start=True,
                                                               stop=False),
                                  reads=[bqtl, bSa], writes=[bpO])
                            fw.op(pe, lambda: nc.tensor.matmul(pO[0:n, hs], attm[0:n, h, 0:n], v_[0:n, hs],
                                                               start=False, stop=True),
                                  reads=[battm, bv], writes=[bpO])
                        for h in range(8):
                            hs = slice(h * 128, (h + 1) * 128)
                            fw.op(pe, lambda: nc.tensor.matmul(pRS[:, hs], khat[0:n, hs], v_[0:n, hs], start=True,
                                                               stop=True),
                                  reads=[bkhat, bv, bER], writes=[bpRS])
                        fw.op(dve, lambda: nc.vector.tensor_tensor(out=S[:], in0=S[:],
                                                                   in1=Ee[:, :, 1:2].broadcast_to([128, 8, 128]),
                                                                   op=ALU.mult),
                              reads=[bS, bEe, bSa], writes=[bS])
                        fw.op(dve, lambda: nc.vector.tensor_tensor(out=S[:], in0=S[:],
                                                                   in1=pRS[:].rearrange("k (h v) -> k h v", h=8),
                                                                   op=ALU.add),
                              reads=[bS, bpRS], writes=[bS])
                        fw.op(act, lambda: nc.scalar.activation(out=osq[0:n, :], in_=pO[0:n, :], func=AF.Square),
                              reads=[bpO], writes=[bosq])
                        fw.op(dve, lambda: nc.vector.tensor_reduce(out=oss[0:n, :],
                                                                   in_=osq[0:n, :].rearrange("t (h v) -> t h v", h=8),
                                                                   axis=AX.X, op=ALU.add),
                              reads=[bosq], writes=[boss])
                        fw.op(act, lambda: nc.scalar.activation(out=oss[0:n, :], in_=oss[0:n, :], func=AF.Ln,
                                                                scale=1.0 / 128, bias=epsb[0:n, :]),
                              reads=[boss, B_const], writes=[boss])
                        fw.op(act, lambda: nc.scalar.activation(out=oss[0:n, :], in_=oss[0:n, :], func=AF.Exp,
                                                                scale=-0.5),
                              reads=[boss], writes=[boss])
                        fw.op(dve, lambda: nc.vector.tensor_tensor(
                            out=on[0:n, :].rearrange("t (h v) -> t h v", h=8),
                            in0=pO[0:n, :].rearrange("t (h v) -> t h v", h=8),
                            in1=oss[0:n, :].unsqueeze(2).broadcast_to([n, 8, 128]), op=ALU.mult),
                            reads=[bpO, boss], writes=[bon])
                        fw.op(dve, lambda: nc.vector.tensor_tensor(
                            out=on[0:n, :].rearrange("t (h v) -> t h v", h=8),
                            in0=on[0:n, :].rearrange("t (h v) -> t h v", h=8),
                            in1=hgB[0:n, :].unsqueeze(1).broadcast_to([n, 8, 128]), op=ALU.mult),
                            reads=[bon, bhg], writes=[bon])
                        fw.op(dve, lambda: nc.vector.tensor_tensor(out=ob[0:n, :], in0=on[0:n, :], in1=zb_[0:n, :],
                                                                   op=ALU.mult),
                              reads=[bon, bzb], writes=[bob])
                        for h in range(8):
                            fw.op(pe, lambda: nc.tensor.transpose(pT[:, h, 0:n], ob[0:n, h * 128:(h + 1) * 128],
                                                                  ident_bf[0:n, 0:n]),
                                  reads=[bob, B_const], writes=[bpT])
                        fw.op(act, lambda: nc.scalar.copy(out=oT[:, :, 0:n], in_=pT[:, :, 0:n]), reads=[bpT],
                              writes=[boT])
                        fw.dma("pool", AT[8:16, :, t0:t0 + n].rearrange("c p t -> p c t"), oT[:, :, 0:n], reads=[boT],
                               writes=[B_AT])
                    dstS = hgrn_prompt[e] if seq == 0 else hgrn_sample[e]
                    fw.dma("sp", dstS.rearrange("h k v -> k h v"), S[:], reads=[bS], writes=[B_out])
            fw.barrier()
        phase_out(l, w_out_even[e], xsrc, bxsrc, last=False)

    def odd_layer(l):
        o = l // 2
        W = w_in_odd[o]
        lam_init = 0.8 - 0.6 * math.exp(-0.3 * l)
        xsrc, bxsrc = xres[l % 2], Bx[l % 2]
        with contextlib.ExitStack() as esL:
            nlam = sb(esL, "nlam", [128, 1], F32)
            sgB = sb(esL, "sgB", [128, 128], F32)
            B_lm = Buf()
            with contextlib.ExitStack() as es:
                lt = sb(es, "lt", [128, 4, 64], F32)
                ls = sb(es, "ls", [128, 2], F32)
                for i, src in enumerate((lam_q1, lam_k1, lam_q2, lam_k2)):
                    fw.dma("sp", lt[:, i, :], src[o:o + 1, :].partition_broadcast(128), writes=[B_lm])
                fw.op(dve, lambda: nc.vector.tensor_tensor(out=lt[:, 0, :], in0=lt[:, 0, :], in1=lt[:, 1, :],
                                                           op=ALU.mult), reads=[B_lm], writes=[B_lm])
                fw.op(dve, lambda: nc.vector.tensor_tensor(out=lt[:, 2, :], in0=lt[:, 2, :], in1=lt[:, 3, :],
                                                           op=ALU.mult), reads=[B_lm], writes=[B_lm])
                fw.op(dve, lambda: nc.vector.tensor_reduce(out=ls[:, 0:1], in_=lt[:, 0, :], axis=AX.X, op=ALU.add),
                      reads=[B_lm], writes=[B_lm])
                fw.op(dve, lambda: nc.vector.tensor_reduce(out=ls[:, 1:2], in_=lt[:, 2, :], axis=AX.X, op=ALU.add),
                      reads=[B_lm], writes=[B_lm])
                fw.op(act, lambda: nc.scalar.activation(out=ls[:], in_=ls[:], func=AF.Exp), reads=[B_lm],
                      writes=[B_lm])
                fw.op(dve, lambda: nc.vector.tensor_tensor(out=nlam[:], in0=ls[:, 1:2], in1=ls[:, 0:1],
                                                           op=ALU.subtract), reads=[B_lm], writes=[B_lm])
                fw.op(dve, lambda: nc.vector.tensor_scalar_add(nlam[:], nlam[:], -lam_init), reads=[B_lm],
                      writes=[B_lm])
                fw.dma("sp", sgB[:], subln_g[o:o + 1, :].partition_broadcast(128), writes=[B_lm])
                fw.op(dve, lambda: nc.vector.tensor_scalar(out=sgB[:], in0=sgB[:], scalar1=1.0 - lam_init,
                                                           scalar2=None, op0=ALU.mult), reads=[B_lm], writes=[B_lm])
                fw.barrier()

            with contextlib.ExitStack() as es:
                w_r = Ring([sb(es, f"w{i}", [128, 8, 512], BF16) for i in range(3)])
                ps_r = Ring([psum(es, f"pp{i}", [128, 512], F32) for i in range(6)])
                tmp_r = Ring([sb(es, f"tmp{i}", [128, 512], F32) for i in range(3)])
                of_r = Ring([sb(es, f"of{i}", [128, 512], F32) for i in range(3)])
                ob_r = Ring([sb(es, f"ob{i}", [128, 512], BF16) for i in range(3)])
                import os
                KSKIP = os.environ.get("KSKIP", "")
                for (col0, dstT, bdst, sc) in ((0, QTo, B_QTo, 0.125), (2048, KTo, B_KTo, 1.0)):
                    if "A" in KSKIP:
                        continue
                    for j in range(4):
                        w_, bw = w_r.next()
                        load_w(w_, bw, W, col0 + j * 512, 512)
                        for cc in range(4):
                            hh = j * 4 + cc
                            for (t0, n) in token_blocks:
                                p_, bp = ps_r.next()
                                mm_fm(p_[:, 0:n], bp, w_, bw, cc * 128, t0, n)
                                ob, bob = ob_r.next()
                                fw.op(act, lambda: nc.scalar.mul(out=ob[:, 0:n], in_=p_[:, 0:n], mul=sc), reads=[bp],
                                      writes=[bob])
                                fw.dma("sp", dstT[hh, :, t0:t0 + n], ob[:, 0:n], reads=[bob], writes=[bdst])
                for (col0, kind) in ((2048, "k"), (4096, "v"), (6144, "z")):
                    if kind in KSKIP:
                        continue
                    for j in range(4):
                        w_, bw = w_r.next()
                        load_w(w_, bw, W, col0 + j * 512, 512)
                        cs = slice(j * 512, (j + 1) * 512)
                        for (t0, n) in token_tiles:
                            p_, bp = ps_r.next()
                            mm_tm(p_[0:n, :], bp, w_, bw, 0, 512, t0, n)
                            if kind in ("k", "v"):
                                of, bof = of_r.next()
                                fw.op(act, lambda: nc.scalar.copy(out=of[0:n, :], in_=p_[0:n, :]), reads=[bp],
                                      writes=[bof])
                                if kind == "k":
                                    dst = k_prompt[o, t0:t0 + n, cs] if t0 < T else k_sample[o, :, cs]
                                else:
                                    dst = v_prompt[o, t0:t0 + n, cs] if t0 < T else v_sample[o, :, cs]
                                fw.dma("pool", dst, of[0:n, :], reads=[bof], writes=[B_out])
                                if kind == "v":
                                    ob, bob = ob_r.next()
                                    fw.op(dve, lambda: nc.vector.tensor_copy(ob[0:n, :], of[0:n, :]), reads=[bof],
                                          writes=[bob])
                                    fw.dma("sp", Vo[t0:t0 + n, cs], ob[0:n, :], reads=[bob], writes=[B_Vo])
                            else:
                                ob, bob = ob_r.next()
                                act_fn(p_[0:n, :], bp, ob[0:n, :], bob, AF.Silu)
                                fw.dma("sp", Zo[t0:t0 + n, cs], ob[0:n, :], reads=[bob], writes=[B_Zo])
            fw.barrier()

            with contextlib.ExitStack() as es:
                NKT_P = T // 128
                NKT_S = NTK + 1
                NKT_MAX = max(NKT_P, NKT_S)
                KLEN = max(T, P + TS)
                kt_r = Ring([sb(es, f"kt{i}", [128, KLEN], BF16) for i in range(2)])
                va_r = Ring([sb(es, f"va{i}", [128, NKT_MAX, 130], BF16) for i in range(2)])
                qt_r = Ring([sb(es, f"qt{i}", [128, T], BF16) for i in range(2)])
                z_r = Ring([sb(es, f"zz{i}", [128, NKT_P, 128], BF16) for i in range(2)])
                aT_r = Ring([sb(es, f"aT{i}", [128, T], BF16) for i in range(2)])
                pt_r = Ring([sb(es, f"pt{i}", [128, 1024], BF16) for i in range(3)])
                accs_r = Ring([sb(es, f"accs{i}", [128, 3, 512], F32) for i in range(3)])
                kc_r = Ring([sb(es, f"kc{i}", [128, 4, 128], BF16) for i in range(max(2, (NTK + 3) // 4))])
                sm_r = Ring([sb(es, f"sm{i}", [128, 8], F32) for i in range(12)])
                o1_r = Ring([sb(es, f"o1{i}", [128, 128], F32) for i in range(12)])
                o2_r = Ring([sb(es, f"o2{i}", [128, 128], F32) for i in range(12)])
                ab_r = Ring([sb(es, f"ab{i}", [128, 128], BF16) for i in range(12)])
                ps_r = Ring([psum(es, f"psc{i}", [128, 1024], F32) for i in range(2)])
                pacc = psum(es, "pacc", [128, 3, 512], F32)
                bpacc = [Buf() for _ in range(8)]
                ptr_r = Ring([psum(es, "ptr_t", [128, 1024], BF16)])
                zl = sb(es, "zl", [128, 128], BF16)
                zr = sb(es, "zr", [128, 387], BF16)
                bz = Buf()
                fw.op(pool, lambda: nc.gpsimd.memset(zl[:], 0.0), writes=[bz])
                fw.op(pool, lambda: nc.gpsimd.memset(zr[:], 0.0), writes=[bz])

                def open_banks(banks):
                    for b in banks:
                        fw.op(pe, lambda: nc.tensor.matmul(pacc[:, b, 0:387], zl[:], zr[:], start=True, stop=False),
                              reads=[bz], writes=[bpacc[i] for i in range(8) if i // 3 == b])

                for (va, bva) in va_r.tiles:
                    fw.op(pool, lambda: nc.gpsimd.memset(va[:, :, 128:129], 1.0), writes=[bva])

                def acc_ap(i, nq):
                    return pacc[0:nq, i // 3, (i % 3) * 129:(i % 3) * 129 + 129]

                deferred = []

                def flush_deferred():
                    while deferred:
                        deferred.pop(0)()

                def finalize(sub_list, zt, bzt, aT, baT, zrow0):
                    accs, baccs = accs_r.next()
                    ctx = []

                    def stage_a():
                        for b in range(3):
                            fw.op(dve, lambda: nc.vector.tensor_copy(accs[:, b, :], pacc[:, b, :]),
                                  reads=[bpacc[i] for i in range(8) if i // 3 == b], writes=[baccs])
                        for (i0, nq, q0, zi) in sub_list:
                            def ac(i, c0, c1):
                                return accs[0:nq, i // 3, (i % 3) * 129 + c0:(i % 3) * 129 + c1]
                            sm, bsm = sm_r.next()
                            fw.op(dve, lambda: nc.vector.reciprocal(sm[0:nq, 0:1], ac(i0, 128, 129)), reads=[baccs],
                                  writes=[bsm])
                            fw.op(dve, lambda: nc.vector.reciprocal(sm[0:nq, 1:2], ac(i0 + 1, 128, 129)),
                                  reads=[baccs], writes=[bsm])
                            fw.op(dve, lambda: nc.vector.tensor_scalar(out=sm[0:nq, 1:2], in0=sm[0:nq, 1:2],
                                                                       scalar1=nlam[0:nq, 0:1], scalar2=None,
                                                                       op0=ALU.mult), reads=[bsm, B_lm], writes=[bsm])
                            o1, bo1 = o1_r.next()
                            fw.op(dve, lambda: nc.vector.tensor_scalar(out=o1[0:nq, :], in0=ac(i0, 0, 128),
                                                                       scalar1=sm[0:nq, 0:1], scalar2=None,
                                                                       op0=ALU.mult),
                                  reads=[baccs, bsm], writes=[bo1])
                            fw.op(dve, lambda: nc.vector.scalar_tensor_tensor(out=o1[0:nq, :], in0=ac(i0 + 1, 0, 128),
                                                                              scalar=sm[0:nq, 1:2], in1=o1[0:nq, :],
                                                                              op0=ALU.mult, op1=ALU.add),
                                  reads=[baccs, bsm, bo1], writes=[bo1])
                            o2, bo2 = o2_r.next()
                            fw.op(dve, lambda: nc.vector.tensor_tensor(out=o2[0:nq, :], in0=o1[0:nq, :],
                                                                       in1=o1[0:nq, :], op=ALU.mult), reads=[bo1],
                                  writes=[bo2])
                            fw.op(dve, lambda: nc.vector.tensor_reduce(out=sm[0:nq, 2:3], in_=o2[0:nq, :], axis=AX.X,
                                                                       op=ALU.add), reads=[bo2], writes=[bsm])
                            ctx.append((nq, q0, zi, sm, bsm, o1, bo1, o2, bo2))

                    def stage_b():
                        for (nq, q0, zi, sm, bsm, o1, bo1, o2, bo2) in ctx:
                            fw.op(act, lambda: nc.scalar.activation(out=sm[0:nq, 2:3], in_=sm[0:nq, 2:3], func=AF.Ln,
                                                                    scale=1.0 / 128, bias=epsb[0:nq, :]),
                                  reads=[bsm, B_const], writes=[bsm])
                            fw.op(act, lambda: nc.scalar.activation(out=sm[0:nq, 2:3], in_=sm[0:nq, 2:3], func=AF.Exp,
                                                                    scale=-0.5), reads=[bsm], writes=[bsm])

                    def stage_c():
                        tails = []
                        for (nq, q0, zi, sm, bsm, o1, bo1, o2, bo2) in ctx:
                            fw.op(dve, lambda: nc.vector.scalar_tensor_tensor(out=o2[0:nq, :], in0=o1[0:nq, :],
                                                                              scalar=sm[0:nq, 2:3], in1=sgB[0:nq, :],
                                                                              op0=ALU.mult, op1=ALU.mult),
                                  reads=[bo1, bsm, B_lm, bo2], writes=[bo2])
                            ab, bab = ab_r.next()
                            fw.op(dve, lambda: nc.vector.tensor_tensor(out=ab[0:nq, :], in0=o2[0:nq, :],
                                                                       in1=zt[0:nq, zi, :], op=ALU.mult),
                                  reads=[bo2, bzt], writes=[bab])
                            tails.append((ab, bab, nq, q0))
                        ptr, bptr = ptr_r.next()
                        for i, (ab, bab, nq, q0) in enumerate(tails):
                            fw.op(pe, lambda: nc.tensor.transpose(ptr[:, i * 128:i * 128 + nq], ab[0:nq, :],
                                                                  ident_bf[0:nq, 0:nq]),
                                  reads=[bab, B_const], writes=[bptr])
                        q00 = tails[0][3]
                        wid = (len(tails) - 1) * 128 + tails[-1][2]
                        fw.op(dve, lambda: nc.vector.tensor_copy(aT[:, q00:q00 + wid], ptr[:, 0:wid]), reads=[bptr],
                              writes=[baT])

                    return stage_a, stage_b, stage_c

                for h in range(NH):
                    hs = slice(h * 128, (h + 1) * 128)
                    kt, bkt = kt_r.next()
                    fw.dma("sp", kt[:, 0:T], KTo[h, :, 0:T], reads=[B_KTo], writes=[bkt])
                    qt, bqt = qt_r.next()
                    fw.dma("sp", qt[:, 0:T], QTo[h, :, 0:T], reads=[B_QTo], writes=[bqt])
                    va, bva = va_r.next()
                    fw.dma("sp", va[:, 0:NKT_P, 0:128], Vo[0:T, hs].rearrange("(m p) d -> p m d", p=128),
                           reads=[B_Vo], writes=[bva])
                    zt, bzt = z_r.next()
                    fw.dma("sp", zt[:, 0:NKT_P, :], Zo[0:T, hs].rearrange("(m p) d -> p m d", p=128), reads=[B_Zo],
                           writes=[bzt])
                    aT, baT = aT_r.next()
                    kt_s, bkt_s = kt_r.next()
                    va_s, bva_s = va_r.next()
                    fw.dma("pool", va_s[:, 0:NTK, 0:128], cache_v[o, :, hs].rearrange("(m p) d -> p m d", p=128),
                           reads=[B_in], writes=[bva_s])
                    fw.dma("sp", va_s[0:TS, NTK, 0:128], Vo[T:TT, hs], reads=[B_Vo], writes=[bva_s])
                    fw.dma("sp", kt_s[:, P:P + TS], KTo[h, :, T:TT], reads=[B_KTo], writes=[bkt_s])
                    kcs = []
                    for m0 in range(0, NTK, 4):
                        kc, bkc = kc_r.next()
                        nm = min(4, NTK - m0)
                        fw.dma("pool", kc[:, 0:nm, :],
                               cache_k[o, m0 * 128:(m0 + nm) * 128, hs].rearrange("(m p) d -> p m d", p=128),
                               reads=[B_in], writes=[bkc])
                        kcs.append((m0, nm, kc, bkc))
                    qt_s, bqt_s = qt_r.next()
                    fw.dma("sp", qt_s[:, 0:TS], QTo[h, :, T:TT], reads=[B_QTo], writes=[bqt_s])
                    zt_s, bzt_s = z_r.next()
                    fw.dma("sp", zt_s[0:TS, 0, :], Zo[T:TT, hs], reads=[B_Zo], writes=[bzt_s])
                    its = [(J, m) for J in range(T // 512) for m in range(4 * J + 4)]
                    if "P" in os.environ.get("KSKIP2", ""):
                        its = []
                    st = {}

                    def emit_qk(k):
                        J, m = its[k]
                        q0 = J * 512
                        i_d = m - 4 * J
                        qlo = 0 if i_d < 0 else i_d * 128
                        psc, bpsc = ps_r.next()
                        for c in range(2):
                            fw.op(pe, lambda: nc.tensor.matmul(psc[:, c * 512 + qlo:(c + 1) * 512],
                                                               kt[c * 64:(c + 1) * 64, m * 128:(m + 1) * 128],
                                                               qt[c * 64:(c + 1) * 64, q0 + qlo:q0 + 512],
                                                               start=True, stop=True),
                                  reads=[bkt, bqt], writes=[bpsc], sig=(c == 1))
                        st[k] = (psc, bpsc)

                    def emit_exp_pv(k):
                        J, m = its[k]
                        q0 = J * 512
                        i_d = m - 4 * J
                        qlo = 0 if i_d < 0 else i_d * 128
                        psc, bpsc = st.pop(k)
                        pt, bpt = pt_r.next()
                        psv = psc[:].rearrange("k (c q) -> k c q", c=2)
                        ptv = pt[:].rearrange("k (c q) -> k c q", c=2)
                        fw.op(act, lambda: nc.scalar.activation(out=ptv[:, :, qlo:512], in_=psv[:, :, qlo:512],
                                                                func=AF.Exp), reads=[bpsc], writes=[bpt])
                        if i_d >= 0:
                            zv = zr[64:128, 0:128].rearrange("p (c q) -> p c q", c=2)
                            fw.op(act, lambda: nc.scalar.copy(out=ptv[64:128, :, i_d * 128:i_d * 128 + 64], in_=zv),
                                  reads=[bpt, bz], writes=[bpt])
                        if m == 0:
                            open_banks((0, 1, 2))
                        for s_ in range(4):
                            if i_d > s_:
                                continue
                            last_m = 4 * J + s_
                            for c in range(2):
                                ai = s_ * 2 + c
                                fw.op(pe, lambda: nc.tensor.matmul(acc_ap(ai, 128),
                                                                   pt[:, c * 512 + s_ * 128:c * 512 + (s_ + 1) * 128],
                                                                   va[:, m, 0:129], start=False, stop=(m == last_m)),
                                      reads=[bpt, bva], writes=[bpacc[ai]], sig=(s_ == 3 and c == 1))

                    sched = []
                    if its:
                        emit_qk(0)
                    for k in range(len(its)):
                        if k + 1 < len(its):
                            emit_qk(k + 1)
                        J, m = its[k]
                        emit_exp_pv(k)
                        while sched and sched[0][0] <= k:
                            sched.pop(0)[1]()
                        if m == 4 * J + 3:
                            sa, sb_, sc = finalize([(s_ * 2, 128, J * 512 + s_ * 128, J * 4 + s_) for s_ in range(4)],
                                                   zt, bzt, aT, baT, 0)
                            sa()
                            sched.append((k + 5, sb_))
                            sched.append((k + 8, sc))
                    while sched:
                        sched.pop(0)[1]()
                    fw.dma("sp", AT[h, :, 0:T], aT[:, 0:T], reads=[baT], writes=[B_AT])

                    kt, bkt, va, bva, qt, bqt, zt, bzt = kt_s, bkt_s, va_s, bva_s, qt_s, bqt_s, zt_s, bzt_s
                    for (m0, nm, kc, bkc) in kcs:
                        ptr, bptr = ptr_r.next()
                        for mm in range(nm):
                            fw.op(pe, lambda: nc.tensor.transpose(ptr[:, mm * 128:(mm + 1) * 128], kc[:, mm, :],
                                                                  ident_bf[:]),
                                  reads=[bkc, B_const], writes=[bptr])
                        fw.op(dve, lambda: nc.vector.tensor_copy(kt[:, m0 * 128:(m0 + nm) * 128], ptr[:, 0:nm * 128]),
                              reads=[bptr], writes=[bkt])
                    aT, baT = aT_r.next()
                    open_banks((0,))
                    GK = 32
                    for m0 in range(0, NKT_S if "S" not in os.environ.get("KSKIP2", "") else 0, GK):
                        nm = min(GK, NKT_S - m0)
                        psc, bpsc = ps_r.next()
                        pt, bpt = pt_r.next()
                        psv = psc[:].rearrange("k (c q) -> k c q", c=2)
                        ptv = pt[:].rearrange("k (c q) -> k c q", c=2)
                        for mm in range(nm):
                            m = m0 + mm
                            nk = 128 if m < NTK else TS
                            for c in range(2):
                                fw.op(pe, lambda: nc.tensor.matmul(psv[0:nk, c, mm * 16:(mm + 1) * 16],
                                                                   kt[c * 64:(c + 1) * 64, m * 128:m * 128 + nk],
                                                                   qt[c * 64:(c + 1) * 64, 0:TS], start=True,
                                                                   stop=True),
                                      reads=[bkt, bqt], writes=[bpsc])
                        nfull = nm if m0 + nm < NKT_S else nm - 1
                        if nfull > 0:
                            fw.op(act, lambda: nc.scalar.activation(out=ptv[:, :, 0:nfull * 16],
                                                                    in_=psv[:, :, 0:nfull * 16], func=AF.Exp),
                                  reads=[bpsc], writes=[bpt])
                        if nfull < nm:
                            fw.op(act, lambda: nc.scalar.activation(out=ptv[0:TS, :, nfull * 16:nm * 16],
                                                                    in_=psv[0:TS, :, nfull * 16:nm * 16],
                                                                    func=AF.Exp),
                                  reads=[bpsc], writes=[bpt])
                        for mm in range(nm):
                            m = m0 + mm
                            nk = 128 if m < NTK else TS
                            for c in range(2):
                                fw.op(pe, lambda: nc.tensor.matmul(acc_ap(c, TS), ptv[0:nk, c, mm * 16:(mm + 1) * 16],
                                                                   va[0:nk, m, 0:129], start=False,
                                                                   stop=(m == NKT_S - 1)),
                                      reads=[bpt, bva], writes=[bpacc[c]])
                    for stg in finalize([(0, TS, 0, 0)], zt, bzt, aT, baT, 0):
                        stg()
                    fw.dma("sp", AT[h, :, T:TT], aT[:, 0:TS], reads=[baT], writes=[B_AT])
            if os.environ.get("KDUMP", "") == "1" and l == 1:
                fw.dma("pool", dbg_AT, AT, reads=[B_AT], writes=[B_out])
            fw.barrier()
        phase_out(l, w_out_odd[o], xsrc, bxsrc, last=(l == DEPTH - 1))

    import os
    if os.environ.get("KTESTOUT", "") == "1":
        with contextlib.ExitStack() as es:
            tt_ = sb(es, "tt_", [128, 2048], F32)
            btt = Buf()
            fw.op(pool, lambda: nc.gpsimd.memset(tt_[:], 7.0), writes=[btt])
            for o in range(2):
                for t0 in range(0, T, 128):
                    fw.dma("sp", k_prompt[o, t0:t0 + 128, :], tt_[:], reads=[btt], writes=[B_out])
                    fw.dma("sp", v_prompt[o, t0:t0 + 128, :], tt_[:], reads=[btt], writes=[B_out])
                fw.dma("sp", k_sample[o, :, :], tt_[0:TS, :], reads=[btt], writes=[B_out])
                fw.dma("sp", v_sample[o, :, :], tt_[0:TS, :], reads=[btt], writes=[B_out])
                fw.dma("sp", conv_prompt[o, :, :], tt_[0:HALO, 0:D], reads=[btt], writes=[B_out])
                fw.dma("sp", conv_sample[o, :, :], tt_[0:HALO, 0:D], reads=[btt], writes=[B_out])
                fw.dma("sp", hgrn_prompt[o].rearrange("h k v -> k h v"), tt_[:, 0:1024].rearrange("k (h v) -> k h v", h=8), reads=[btt], writes=[B_out])
                fw.dma("sp", hgrn_sample[o].rearrange("h k v -> k h v"), tt_[:, 0:1024].rearrange("k (h v) -> k h v", h=8), reads=[btt], writes=[B_out])
            for t0 in range(0, T, 128):
                fw.dma("sp", y_prompt[t0:t0 + 128, :], tt_[:, 0:D], reads=[btt], writes=[B_out])
            fw.dma("sp", y_sample[:, :], tt_[0:TS, 0:D], reads=[btt], writes=[B_out])
        fw.stopped = True
        fw._barrier()
    try:
        phase_load_x()
        for l in range(DEPTH):
            if l % 2 == 0:
                even_layer(l)
            else:
                odd_layer(l)
    except StopBuild:
        print("build stopped at barrier", fw.n_bar)
    fw.finish()
    top.close()
    fw.close()
    return nc, fw


_CACHE = {}


def _get_program(T, P):
    key = (T, P)
    if key not in _CACHE:
        _CACHE[key] = build_program(T, P)
    return _CACHE[key]


def kernel(**inputs):
    x_prompt = np.asarray(inputs["x_prompt"], dtype=np.float32)
    BATCH, T, _ = x_prompt.shape
    x_sample = np.asarray(inputs["x_sample"], dtype=np.float32)
    NS = x_sample.shape[0]
    P = inputs["cache_k"].shape[2]
    n_cores = 8
    nc, fw = _get_program(T, P)
    shared = ["w_in_even", "w_out_even", "conv_w", "conv_b", "conv_ln_g", "conv_ln_b", "hgrn_lb_logits",
              "hgrn_norm_g", "w_in_odd", "w_out_odd", "lam_q1", "lam_k1", "lam_q2", "lam_k2", "subln_g",
              "post_ln_g", "post_ln_b"]
    sh = {k: np.ascontiguousarray(np.asarray(inputs[k], dtype=np.float32)) for k in shared}
    in_maps = []
    for c in range(n_cores):
        bp = c % BATCH
        bs = c % NS
        m = dict(sh)
        m["x_prompt"] = np.ascontiguousarray(x_prompt[bp])
        m["x_sample"] = np.ascontiguousarray(x_sample[bs])
        m["state_conv"] = np.ascontiguousarray(np.asarray(inputs["state_conv"])[:, bs])
        m["state_hgrn"] = np.ascontiguousarray(np.asarray(inputs["state_hgrn"])[:, bs])
        m["cache_k"] = np.ascontiguousarray(np.asarray(inputs["cache_k"])[:, bs]).reshape(2, P, 2048)
        m["cache_v"] = np.ascontiguousarray(np.asarray(inputs["cache_v"])[:, bs]).reshape(2, P, 2048)
        in_maps.append(m)
    res = run_bass_kernel_spmd(nc, in_maps, core_ids=list(range(n_cores)))
    R = res.results

    def stack_p(name, shp):
        return np.stack([np.asarray(R[b][name]).reshape(shp) for b in range(BATCH)], axis=0)

    def stack_s(name, shp):
        return np.stack([np.asarray(R[b][name]).reshape(shp) for b in range(NS)], axis=0)

    y_prompt = stack_p("y_prompt", (T, D))
    y_sample = stack_s("y_sample", (TS, D))
    conv_prompt = np.moveaxis(stack_p("conv_prompt", (2, HALO, D)), 0, 1)
    hgrn_prompt = np.moveaxis(stack_p("hgrn_prompt", (2, 8, 128, 128)), 0, 1)
    k_prompt = np.moveaxis(stack_p("k_prompt", (2, T, 16, 128)), 0, 1)
    v_prompt = np.moveaxis(stack_p("v_prompt", (2, T, 16, 128)), 0, 1)
    conv_sample = np.moveaxis(stack_s("conv_sample", (2, HALO, D)), 0, 1)
    hgrn_sample = np.moveaxis(stack_s("hgrn_sample", (2, 8, 128, 128)), 0, 1)
    k_sample = np.moveaxis(stack_s("k_sample", (2, TS, 16, 128)), 0, 1)
    v_sample = np.moveaxis(stack_s("v_sample", (2, TS, 16, 128)), 0, 1)
    outs = (y_prompt, y_sample, conv_prompt, hgrn_prompt, k_prompt, v_prompt, conv_sample, hgrn_sample, k_sample,
            v_sample)
    return tuple(np.ascontiguousarray(a, dtype=np.float32) for a in outs)
```
